# Optimizing a Trainium2 kernel written in Bass

```python
import math, functools
import jax, jax.numpy as jnp
from jax import lax
import numpy as np

D_MODEL = 1024
BATCH = 2
SEQ = 16384
DEPTH = 1
DEC_BATCH = 16
DEC_SEQ = 16
PAST_LEN = 2048

CHUNK = 64
Q_BLOCK = 128
A_HEADS = 8
A_HEAD_DIM = 64
A_WIDTH = A_HEADS * A_HEAD_DIM
M_HEADS = 4
M_HEAD_DIM = 128
M_WIDTH = M_HEADS * M_HEAD_DIM
CONV_W = 4
D_FF = 2816
LN_EPS = 1e-5
ALPHA = (2.0 * DEPTH) ** 0.25
BETA = (8.0 * DEPTH) ** -0.25
IN_SIZES = (A_WIDTH, A_WIDTH, A_WIDTH, A_HEADS,
            M_WIDTH, M_WIDTH, M_WIDTH, M_HEADS, M_HEADS, M_WIDTH,
            D_MODEL, D_MODEL)
D_IN = sum(IN_SIZES)

kernel_name = "fox_mlstm_gated_macaron_deepnorm_adaln_step"


def _layer_norm(x, g, b):
    xf = x.astype(jnp.float32)
    mu = jnp.mean(xf, axis=-1, keepdims=True)
    var = jnp.mean(jnp.square(xf - mu), axis=-1, keepdims=True)
    return ((xf - mu) * lax.rsqrt(var + LN_EPS) * g + b).astype(x.dtype)


def _ffn(h, w_gu, w_down):
    g, u = jnp.split(h @ w_gu, 2, axis=-1)
    return (jax.nn.silu(g) * u) @ w_down


def _split_in(z):
    idx = [int(i) for i in np.cumsum(IN_SIZES)[:-1]]
    return jnp.split(z, idx, axis=-1)


def _causal_conv(u, buf, w, b):
    L = u.shape[1]
    full = jnp.concatenate([buf.astype(u.dtype), u], axis=1)
    out = b
    for j in range(CONV_W):
        out = out + full[:, j:j + L] * w[j]
    return out, full[:, -(CONV_W - 1):]


def _branch_inputs(h, w_in, b_in, conv_w, conv_b, conv_buf):
    B, L = h.shape[:2]
    z = h @ w_in + b_in
    aq, ak, av, af, mq, mk, mv, mi, mf, mo, ga, gb = _split_in(z)
    qk, new_buf = _causal_conv(jnp.concatenate([mq, mk], axis=-1), conv_buf, conv_w, conv_b)
    mq, mk = jnp.split(jax.nn.silu(qk), 2, axis=-1)
    fox = (aq.reshape(B, L, A_HEADS, A_HEAD_DIM),
           ak.reshape(B, L, A_HEADS, A_HEAD_DIM),
           av.reshape(B, L, A_HEADS, A_HEAD_DIM),
           jax.nn.log_sigmoid(af.astype(jnp.float32)))
    f32 = jnp.float32
    mls = (mq.reshape(B, L, M_HEADS, M_HEAD_DIM).astype(f32),
           mk.reshape(B, L, M_HEADS, M_HEAD_DIM).astype(f32) * (M_HEAD_DIM ** -0.5),
           mv.reshape(B, L, M_HEADS, M_HEAD_DIM).astype(f32),
           mi.astype(f32),
           jax.nn.log_sigmoid(mf.astype(f32)),
           mo)
    return fox, mls, ga, gb, new_buf


def _fox_prompt(q, k, v, logf):
    B, S = q.shape[:2]
    nb = S // Q_BLOCK
    Ft = jnp.cumsum(logf, axis=1).transpose(0, 2, 1)
    qb = q.reshape(B, nb, Q_BLOCK, A_HEADS, A_HEAD_DIM).transpose(1, 0, 2, 3, 4)
    Fb = Ft.reshape(B, A_HEADS, nb, Q_BLOCK).transpose(2, 0, 1, 3)
    pos_k = jnp.arange(S)
    scale = A_HEAD_DIM ** -0.5

    def block(args):
        i, qi, Fi = args
        s = jnp.einsum('bqhd,bkhd->bhqk', qi, k, preferred_element_type=jnp.float32) * scale
        s = s + Fi[..., :, None] - Ft[..., None, :]
        pos_q = i * Q_BLOCK + jnp.arange(Q_BLOCK)
        s = jnp.where(pos_k[None, :] <= pos_q[:, None], s, -jnp.inf)
        p = jax.nn.softmax(s, axis=-1)
        return jnp.einsum('bhqk,bkhd->bqhd', p.astype(v.dtype), v)

    out = lax.map(block, (jnp.arange(nb), qb, Fb))
    return out.transpose(1, 0, 2, 3, 4).reshape(B, S, A_WIDTH)


def _fox_sample(q, k_new, v_new, logf_new, k_cache, v_cache, logf_cache):
    B, L = q.shape[:2]
    P = k_cache.shape[1]
    k = jnp.concatenate([k_cache.astype(k_new.dtype), k_new], axis=1)
    v = jnp.concatenate([v_cache.astype(v_new.dtype), v_new], axis=1)
    logf = jnp.concatenate([logf_cache.astype(jnp.float32), logf_new], axis=1)
    Ft = jnp.cumsum(logf, axis=1).transpose(0, 2, 1)
    s = jnp.einsum('bqhd,bkhd->bhqk', q, k, preferred_element_type=jnp.float32) * (A_HEAD_DIM ** -0.5)
    s = s + Ft[..., P:, None] - Ft[..., None, :]
    pos_q = P + jnp.arange(L)
    pos_k = jnp.arange(P + L)
    s = jnp.where(pos_k[None, :] <= pos_q[:, None], s, -jnp.inf)
    p = jax.nn.softmax(s, axis=-1)
    return jnp.einsum('bhqk,bkhd->bqhd', p.astype(v.dtype), v).reshape(B, L, A_WIDTH)


def _mlstm_chunk(carry, inp):
    C, n, m = carry
    q, k, v, ig, lf = inp
    L = q.shape[1]
    b = jnp.cumsum(lf, axis=1).transpose(0, 2, 1)
    it = ig.transpose(0, 2, 1)
    causal = jnp.tril(jnp.ones((L, L), dtype=bool))
    d = jnp.where(causal, b[..., :, None] - b[..., None, :] + it[..., None, :], -jnp.inf)
    inter = b + m[..., None]
    m_t = jnp.maximum(inter, jnp.max(d, axis=-1))
    w_inter = jnp.exp(inter - m_t)
    a = jnp.exp(d - m_t[..., None]) * jnp.einsum('blhd,bshd->bhls', q, k)
    num = jnp.einsum('bhls,bshv->bhlv', a, v) + w_inter[..., None] * jnp.einsum('bhvd,blhd->bhlv', C, q)
    den = jnp.sum(a, axis=-1) + w_inter * jnp.einsum('bhd,blhd->bhl', n, q)
    h = num / jnp.maximum(jnp.abs(den), jnp.exp(-m_t))[..., None]
    m_new = m_t[..., -1]
    w_state = jnp.exp(b[..., -1] + m - m_new)
    w_s = jnp.exp(b[..., -1:] - b + it - m_new[..., None])
    C_new = w_state[..., None, None] * C + jnp.einsum('bhs,bshv,bshd->bhvd', w_s, v, k)
    n_new = w_state[..., None] * n + jnp.einsum('bhs,bshd->bhd', w_s, k)
    return (C_new, n_new, m_new), h.transpose(0, 2, 1, 3)


def _mlstm_prompt(q, k, v, ig, lf):
    B, S = q.shape[:2]
    nc = S // CHUNK

    def to_chunks(a):
        return a.reshape((B, nc, CHUNK) + a.shape[2:]).swapaxes(0, 1)

    init = (jnp.zeros((B, M_HEADS, M_HEAD_DIM, M_HEAD_DIM), jnp.float32),
            jnp.zeros((B, M_HEADS, M_HEAD_DIM), jnp.float32),
            jnp.zeros((B, M_HEADS), jnp.float32))
    carry, hs = lax.scan(_mlstm_chunk, init, tuple(to_chunks(a) for a in (q, k, v, ig, lf)))
    return hs.swapaxes(0, 1).reshape(B, S, M_HEADS, M_HEAD_DIM), carry


def _mlstm_out(h, o, g):
    B, L = h.shape[:2]
    mu = jnp.mean(h, axis=-1, keepdims=True)
    var = jnp.mean(jnp.square(h - mu), axis=-1, keepdims=True)
    hn = ((h - mu) * lax.rsqrt(var + LN_EPS)).reshape(B, L, M_WIDTH)
    return (hn * g * jax.nn.sigmoid(o.astype(jnp.float32))).astype(o.dtype)


def _merge(o_a, o_b, ga, gb, w_branch_a, w_branch_b, w_out):
    m = jax.nn.sigmoid(ga) * (o_a @ w_branch_a) + jax.nn.sigmoid(gb) * (o_b @ w_branch_b)
    return m @ w_out


def _mixer_prompt(h, w_in, b_in, conv_w, conv_b, norm_g, w_branch_a, w_branch_b, w_out):
    B = h.shape[0]
    buf0 = jnp.zeros((B, CONV_W - 1, 2 * M_WIDTH), h.dtype)
    fox, mls, ga, gb, new_buf = _branch_inputs(h, w_in, b_in, conv_w, conv_b, buf0)
    aq, ak, av, alf = fox
    o_a = _fox_prompt(aq, ak, av, alf)
    mq, mk, mv, mi, mf, mo = mls
    hm, (C, n, m) = _mlstm_prompt(mq, mk, mv, mi, mf)
    o_b = _mlstm_out(hm, mo, norm_g)
    y = _merge(o_a, o_b, ga, gb, w_branch_a, w_branch_b, w_out)
    return y, (ak, av, alf, C, n, m, new_buf)


def _mixer_sample(h, k_cache, v_cache, logf_cache, C0, n0, m0, conv_buf,
                  w_in, b_in, conv_w, conv_b, norm_g, w_branch_a, w_branch_b, w_out):
    fox, mls, ga, gb, new_buf = _branch_inputs(h, w_in, b_in, conv_w, conv_b, conv_buf)
    aq, ak, av, alf = fox
    o_a = _fox_sample(aq, ak, av, alf, k_cache, v_cache, logf_cache)
    mq, mk, mv, mi, mf, mo = mls
    carry0 = (C0.astype(jnp.float32), n0.astype(jnp.float32), m0.astype(jnp.float32))
    (C, n, m), hm = _mlstm_chunk(carry0, (mq, mk, mv, mi, mf))
    o_b = _mlstm_out(hm, mo, norm_g)
    y = _merge(o_a, o_b, ga, gb, w_branch_a, w_branch_b, w_out)
    return y, (ak, av, alf, C, n, m, new_buf)


def _trunk_layer(x, c, mixer_fn, w_ada, b_ada, ffn1_w_gu, ffn1_w_down, ffn2_w_gu, ffn2_w_down, ln_g, ln_b):
    mod = jax.nn.silu(c) @ w_ada + b_ada
    sh1, sc1, g1, sh2, sc2, g2, sh3, sc3, g3 = jnp.split(mod[:, None, :], 9, axis=-1)
    h = x * (1 + sc1) + sh1
    x = _layer_norm(ALPHA * x + 0.5 * g1 * _ffn(h, ffn1_w_gu, ffn1_w_down), ln_g[0], ln_b[0])
    h = x * (1 + sc2) + sh2
    mix, new_state = mixer_fn(h)
    x = _layer_norm(ALPHA * x + g2 * mix, ln_g[1], ln_b[1])
    h = x * (1 + sc3) + sh3
    x = _layer_norm(ALPHA * x + 0.5 * g3 * _ffn(h, ffn2_w_gu, ffn2_w_down), ln_g[2], ln_b[2])
    return x, new_state


def setup_inputs(seed: int = 0) -> dict:
    key = jax.random.key(seed)
    ks = jax.random.split(key, 32)
    nrm = jax.random.normal
    f32 = jnp.float32
    D = D_MODEL
    off = [int(o) for o in np.cumsum((0,) + IN_SIZES)]
    b_in = 0.02 * nrm(ks[0], (DEPTH, D_IN), f32)
    b_in = b_in.at[:, off[3]:off[4]].add(jnp.linspace(1.0, 4.0, A_HEADS))
    b_in = b_in.at[:, off[8]:off[9]].add(jnp.linspace(3.0, 6.0, M_HEADS))
    return {
        "x_prompt": nrm(ks[1], (BATCH, SEQ, D), f32),
        "x_sample": nrm(ks[2], (DEC_BATCH, DEC_SEQ, D), f32),
        "cache_fox_k": nrm(ks[3], (DEPTH, DEC_BATCH, PAST_LEN, A_HEADS, A_HEAD_DIM), f32),
        "cache_fox_v": nrm(ks[4], (DEPTH, DEC_BATCH, PAST_LEN, A_HEADS, A_HEAD_DIM), f32),
        "cache_fox_logf": jax.nn.log_sigmoid(3.0 + nrm(ks[5], (DEPTH, DEC_BATCH, PAST_LEN, A_HEADS), f32)),
        "state_mlstm_C": 0.1 * nrm(ks[6], (DEPTH, DEC_BATCH, M_HEADS, M_HEAD_DIM, M_HEAD_DIM), f32),
        "state_mlstm_n": 0.1 * nrm(ks[7], (DEPTH, DEC_BATCH, M_HEADS, M_HEAD_DIM), f32),
        "state_mlstm_m": nrm(ks[8], (DEPTH, DEC_BATCH, M_HEADS), f32),
        "state_conv": nrm(ks[9], (DEPTH, DEC_BATCH, CONV_W - 1, 2 * M_WIDTH), f32),
        "c_prompt": nrm(ks[10], (BATCH, D), f32),
        "c_sample": nrm(ks[11], (DEC_BATCH, D), f32),
        "w_ada": 0.5 * D ** -0.5 * nrm(ks[12], (DEPTH, D, 9 * D), f32),
        "b_ada": 0.02 * nrm(ks[13], (DEPTH, 9 * D), f32),
        "ffn1_w_gu": D ** -0.5 * nrm(ks[14], (DEPTH, D, 2 * D_FF), f32),
        "ffn1_w_down": BETA * D_FF ** -0.5 * nrm(ks[15], (DEPTH, D_FF, D), f32),
        "w_in": D ** -0.5 * nrm(ks[16], (DEPTH, D, D_IN), f32),
        "b_in": b_in,
        "conv_w": CONV_W ** -0.5 * nrm(ks[17], (DEPTH, CONV_W, 2 * M_WIDTH), f32),
        "conv_b": 0.02 * nrm(ks[18], (DEPTH, 2 * M_WIDTH), f32),
        "mlstm_norm_g": 1.0 + 0.02 * nrm(ks[19], (DEPTH, M_WIDTH), f32),
        "w_branch_a": A_WIDTH ** -0.5 * nrm(ks[20], (DEPTH, A_WIDTH, D), f32),
        "w_branch_b": M_WIDTH ** -0.5 * nrm(ks[21], (DEPTH, M_WIDTH, D), f32),
        "w_out": BETA * D ** -0.5 * nrm(ks[22], (DEPTH, D, D), f32),
        "ffn2_w_gu": D ** -0.5 * nrm(ks[23], (DEPTH, D, 2 * D_FF), f32),
        "ffn2_w_down": BETA * D_FF ** -0.5 * nrm(ks[24], (DEPTH, D_FF, D), f32),
        "ln_g": 1.0 + 0.02 * nrm(ks[25], (DEPTH, 3, D), f32),
        "ln_b": 0.02 * nrm(ks[26], (DEPTH, 3, D), f32),
    }


def reference(x_prompt, x_sample, cache_fox_k, cache_fox_v, cache_fox_logf, state_mlstm_C,
              state_mlstm_n, state_mlstm_m, state_conv, c_prompt, c_sample, w_ada, b_ada,
              ffn1_w_gu, ffn1_w_down, w_in, b_in, conv_w, conv_b, mlstm_norm_g, w_branch_a,
              w_branch_b, w_out, ffn2_w_gu, ffn2_w_down, ln_g, ln_b):
    xp, xs = x_prompt, x_sample
    sp = [[] for _ in range(7)]
    ss = [[] for _ in range(7)]
    for l in range(DEPTH):
        mix_w = (w_in[l], b_in[l], conv_w[l], conv_b[l], mlstm_norm_g[l], w_branch_a[l], w_branch_b[l], w_out[l])
        layer_w = (w_ada[l], b_ada[l], ffn1_w_gu[l], ffn1_w_down[l], ffn2_w_gu[l], ffn2_w_down[l], ln_g[l], ln_b[l])
        xp, st_p = _trunk_layer(xp, c_prompt, lambda h: _mixer_prompt(h, *mix_w), *layer_w)
        xs, st_s = _trunk_layer(
            xs, c_sample,
            lambda h: _mixer_sample(h, cache_fox_k[l], cache_fox_v[l], cache_fox_logf[l], state_mlstm_C[l],
                                    state_mlstm_n[l], state_mlstm_m[l], state_conv[l], *mix_w),
            *layer_w)
        for j in range(7):
            sp[j].append(st_p[j])
            ss[j].append(st_s[j])
    fox_k_p, fox_v_p, fox_logf_p, mlstm_C_p, mlstm_n_p, mlstm_m_p, conv_p = [jnp.stack(a, 0) for a in sp]
    fox_k_s, fox_v_s, fox_logf_s, mlstm_C_s, mlstm_n_s, mlstm_m_s, conv_s = [jnp.stack(a, 0) for a in ss]
    return (xp, xs, fox_k_p, fox_v_p, fox_logf_p, mlstm_C_p, mlstm_n_p, mlstm_m_p, conv_p,
            fox_k_s, fox_v_s, fox_logf_s, mlstm_C_s, mlstm_n_s, mlstm_m_s, conv_s)
```

```python
import numpy as np
import concourse.bass as bass
import concourse.mybir as mybir
from concourse.bass_utils import run_bass_kernel_spmd

F32 = mybir.dt.float32
BF16 = mybir.dt.bfloat16
ALU = mybir.AluOpType
AF = mybir.ActivationFunctionType
AX = mybir.AxisListType

D = 1024
S = 16384
NT = 64
TN = 256
OWN0 = 48
DFF = 2816
NFF = 22
DIN = 5648
ALPHA = 2.0 ** 0.25
EPS = 1e-5
BIG = 30000.0


class Res:
    __slots__ = ("name", "w", "r")

    def __init__(self, name):
        self.name = name
        self.w = None
        self.r = []


class Op:
    __slots__ = ("id", "eng", "fn", "deps", "dma", "sig", "need")

    def __init__(self, i, eng, fn, dma):
        self.id = i
        self.eng = eng
        self.fn = fn
        self.deps = set()
        self.dma = dma
        self.sig = None
        self.need = False


class Sched:
    ENGS = ("pe", "act", "dve", "pool", "sp")

    def __init__(self, nc):
        self.nc = nc
        self.ops = []
        self.last_barrier = None

    def op(self, eng, fn, reads=(), writes=(), dma=False):
        o = Op(len(self.ops), eng, fn, dma)
        for r in reads:
            if r.w is not None:
                o.deps.add(r.w)
        for w in writes:
            if w.w is not None:
                o.deps.add(w.w)
            for x in w.r:
                o.deps.add(x)
        for r in reads:
            r.r.append(o.id)
        for w in writes:
            w.w = o.id
            w.r = []
        o.deps.discard(o.id)
        self.ops.append(o)
        return o

    def emit(self, out_dma_ops):
        nc = self.nc
        ops = self.ops
        for o in ops:
            best = {}
            keep = set()
            for d in o.deps:
                p = ops[d]
                if p.dma:
                    keep.add(d)
                    continue
                if p.eng == "pe" and o.eng == "pe" and not o.dma:
                    continue
                if d > best.get(p.eng, -1):
                    best[p.eng] = d
            keep.update(best.values())
            o.deps = keep
            for d in keep:
                ops[d].need = True
        for d in out_dma_ops:
            ops[d].need = True
        import contextlib
        es = contextlib.ExitStack()
        NDS = 12
        csem = {e: es.enter_context(nc.semaphore("c_" + e)) for e in ("pe", "act", "dve", "pool")}
        dsem = {e: [es.enter_context(nc.semaphore("d_%s%d" % (e, i))) for i in range(NDS)] for e in ("sp", "pool", "act")}
        ccount = {e: 0 for e in csem}
        dcount = {e: [0] * NDS for e in dsem}
        drr = {e: 0 for e in dsem}
        streams = {e: [] for e in self.ENGS}
        known = {e: {} for e in self.ENGS}
        for o in ops:
            want = {}
            for d in o.deps:
                p = ops[d]
                if p.sig is None:
                    continue
                sem, val = p.sig
                if val > want.get(id(sem), (None, 0))[1]:
                    want[id(sem)] = (sem, val)
            waits = []
            for k_, (sem, val) in want.items():
                if known[o.eng].get(k_, 0) >= val:
                    continue
                known[o.eng][k_] = val
                waits.append((sem, val))
            if o.dma:
                i = drr[o.eng]
                drr[o.eng] = (i + 1) % NDS
                sem = dsem[o.eng][i]
                prev = dcount[o.eng][i]
                if prev > 0 and known[o.eng].get(id(sem), 0) < prev:
                    known[o.eng][id(sem)] = prev
                    waits.append((sem, prev))
                dcount[o.eng][i] = prev + 16
                o.sig = (sem, prev + 16)
                inc = 16
            elif o.need:
                ccount[o.eng] += 1
                o.sig = (csem[o.eng], ccount[o.eng])
                inc = 1
            else:
                inc = 0
            streams[o.eng].append((o, waits, inc))
        finals = [ops[d].sig for d in out_dma_ops]
        with es, nc.Block() as block:
            def run(engname):
                def body(eng):
                    for (o, waits, inc) in streams[engname]:
                        for (sem, val) in waits:
                            eng.wait_ge(sem, val)
                        ins = o.fn(eng)
                        if inc:
                            ins.then_inc(o.sig[0], inc)
                    if engname == "sp":
                        for (sem, val) in finals:
                            eng.wait_ge(sem, val)
                return body
            block.tensor(run("pe"))
            block.scalar(run("act"))
            block.vector(run("dve"))
            block.gpsimd(run("pool"))
            block.sync(run("sp"))


class Builder:
    def __init__(self):
        self.nc = bass.Bass("TRN2", target_bir_lowering=False)
        self.s = Sched(self.nc)
        self.out_dmas = []
        self.rescache = {}
        self.dq = 0
        self.bar = None
        self.bar_tile = None

    def R(self, name):
        if name not in self.rescache:
            r = Res(name)
            r.w = self.bar
            self.rescache[name] = r
        return self.rescache[name]

    def barrier(self):
        allr = list(self.rescache.values())
        scr = self.bar_tile
        o = self.s.op("pool", lambda e: e.memset(scr, 0.0), (), allr)
        self.bar = o.id

    def dma(self, out, in_, reads=(), writes=(), final=False, q=None):
        if q is None:
            q = "sp"
        o = self.s.op(q, lambda e, out=out, in_=in_: e.dma_start(out=out, in_=in_), reads, writes, dma=True)
        if final:
            self.out_dmas.append(o.id)
        return o

    def mm(self, out, lhsT, rhs, start, stop, reads=(), writes=()):
        return self.s.op("pe", lambda e: e.matmul(out, lhsT, rhs, start=start, stop=stop), reads, writes)

    def act(self, out, in_, func, bias=None, scale=None, reads=(), writes=(), accum_out=None):
        def f(e):
            kw = {}
            if bias is not None:
                kw["bias"] = bias
            if scale is not None:
                kw["scale"] = scale
            if accum_out is not None:
                kw["accum_out"] = accum_out
            return e.activation(out, in_, func, **kw)
        return self.s.op("act", f, reads, writes)

    def ts(self, eng, out, in0, s1, s2, op0, op1=None, reads=(), writes=()):
        def f(e):
            if op1 is None:
                return e.tensor_scalar(out, in0, s1, None, op0)
            return e.tensor_scalar(out, in0, s1, s2, op0, op1)
        return self.s.op(eng, f, reads, writes)

    def tt(self, eng, out, in0, in1, op, reads=(), writes=()):
        return self.s.op(eng, lambda e: e.tensor_tensor(out, in0, in1, op), reads, writes)

    def stt(self, eng, out, in0, scalar, in1, op0, op1, reads=(), writes=()):
        return self.s.op(eng, lambda e: e.scalar_tensor_tensor(out, in0, scalar, in1, op0, op1), reads, writes)

    def cp(self, eng, out, in_, reads=(), writes=()):
        if eng == "act":
            return self.s.op("act", lambda e: e.copy(out, in_), reads, writes)
        return self.s.op(eng, lambda e: e.tensor_copy(out, in_), reads, writes)

    def memset(self, eng, ap, val, writes=()):
        return self.s.op(eng, lambda e: e.memset(ap, val), (), writes)


def build():
    B = Builder()
    nc = B.nc
    s = B.s

    def din(name, shape, dt=F32):
        return nc.dram_tensor(name, list(shape), dt, kind="ExternalInput").ap()

    def dout(name, shape, dt=F32):
        return nc.dram_tensor(name, list(shape), dt, kind="ExternalOutput").ap()

    def dscr(name, shape, dt=F32):
        return nc.dram_tensor(name, list(shape), dt).ap()

    xT = din("xT", [128, 8, S])
    xsT = din("xsT", [128, 8, 32])
    keep = din("keep", [128, NT])
    cT = din("cT", [128, 8, 3])
    w_ada = din("w_ada", [128, 8, 9 * D])
    b_ada = din("b_ada", [128, 72])
    w_gu = [din("w_gu%d" % i, [128, 8, 2 * DFF]) for i in (1, 2)]
    w_dn = [din("w_dn%d" % i, [128, NFF, D]) for i in (1, 2)]
    w_in = din("w_in", [128, 8, DIN])
    b_in_fm = din("b_in_fm", [128, 40])
    b_in_bc = din("b_in_bc", [128, 1032])
    ident_d = din("ident", [128, 128])
    tri_d = din("tri", [128, 128])
    cmask_d = din("cmask", [128, 128])
    segm_d = din("segm", [128, 128])
    ng_d = din("ng", [128, 4])
    convs_d = din("convs", [128, 8, 2, 3])
    kcT_d = din("kcT", [2, 8, 64, 2048])
    vc_d = din("vc", [2, 8, 128, 16, 64])
    lfc_d = din("lfc", [2, 128, 16 * 8])
    Cs0_d = din("Cs0", [2, 128, 4 * 129])
    m0_d = din("m0c", [2, 4, 1])
    w_ba = din("w_ba", [128, 4, D])
    w_bb = din("w_bb", [128, 4, D])
    w_o = din("w_o", [128, 8, D])
    ln_g = din("ln_g", [128, 3, 8])
    ln_b = din("ln_b", [128, 3, 8])
    conv_w = din("conv_w", [128, 8, 4])
    conv_b = din("conv_b", [128, 8])

    yT = dout("yT", [128, 8, 4096])
    ysT = dout("ysT", [128, 8, 32])
    okT = dout("okT", [128, 4, 4096])
    ov = dout("ov", [4096, 512])
    olf = dout("olf", [4096, 8])
    oconv = dout("oconv", [128, 8, 3])
    oC = dout("oC", [128, 4 * 129])
    oksT = dout("oksT", [128, 4, 32])
    ovs = dout("ovs", [2, 16, 512])
    olfs = dout("olfs", [2, 16, 8])
    ocs = dout("ocs", [128, 8, 2, 3])
    oCs = dout("oCs", [2, 128, 4 * 129])
    oms = dout("oms", [2, 4, 1])
    om = dout("om", [128, 1])
    KT = dscr("KT", [8, 64, S], BF16)
    VA = dscr("VA", [8, 128, 128, 65], BF16)
    MK = dscr("MK", [128, 128, 512], BF16)
    MV = dscr("MV", [128, 128, 4 * 130], BF16)
    GI = dscr("GI", [4, S])
    GF = dscr("GF", [4, S])
    QT = dscr("QT", [8, 64, 4096], BF16)
    MQT = dscr("MQT", [4, 128, 4096], BF16)
    MKT = dscr("MKT", [4, 128, 4096], BF16)
    SG = dscr("SG", [16, 128, 20 * TN], BF16)
    QF = dscr("QF", [8, 4096], BF16)
    OA = dscr("OA", [8, 64, 4096], BF16)
    OB = dscr("OB", [4, 128, 4096], BF16)
    X2 = dscr("X2", [17, 128, 8, TN])
    KTs = dscr("KTs", [8, 64, 32], BF16)
    QTs = dscr("QTs", [8, 64, 32], BF16)
    VAs = dscr("VAs", [2, 16, 8 * 65], BF16)
    MQTs = dscr("MQTs", [4, 128, 32], BF16)
    MKTs = dscr("MKTs", [4, 128, 32], BF16)
    MKs = dscr("MKs", [2, 16, 512], BF16)
    MVs = dscr("MVs", [2, 16, 520], BF16)
    SGs = dscr("SGs", [128, 20 * 32], BF16)
    OAs = dscr("OAs", [8, 64, 32], BF16)
    OBs = dscr("OBs", [4, 128, 32], BF16)
    QFs = dscr("QFs", [2, 8, 16], BF16)

    X1 = dscr("X1", [NT + 1, 128, 8, TN])

    import contextlib
    es = contextlib.ExitStack()
    ARENA_F = 50432
    arena = es.enter_context(nc.sbuf_tensor("arena", [128, 2 * ARENA_F], BF16))
    psum = [es.enter_context(nc.psum_tensor("ps%d" % i, [128, 512], F32)) for i in range(8)]
    PS = [B.R("ps%d" % i) for i in range(8)]

    class Arena:
        def __init__(self):
            self.off = 0

        def f32(self, n):
            a = arena[:, 2 * self.off:2 * (self.off + n)].bitcast(F32)
            self.off += n
            assert self.off <= ARENA_F, self.off
            return a

        def bf16(self, n):
            m = (n + 1) // 2
            a = arena[:, 2 * self.off:2 * (self.off + m)]
            self.off += m
            assert self.off <= ARENA_F, self.off
            return a[:, 0:n]

    A = Arena()
    modv = A.f32(72 * 3)
    modv3 = modv.rearrange("p (j c) -> p j c", c=3)
    opsc = A.f32(72 * 3)
    opsc3 = opsc.rearrange("p (j c) -> p j c", c=3)
    keep_sb = A.f32(NT)
    lng = A.f32(24)
    lnb = A.f32(24)
    lng3 = lng.rearrange("p (a c) -> p a c", c=8)
    lnb3 = lnb.rearrange("p (a c) -> p a c", c=8)
    ones_f = A.f32(128)
    ones_b = A.bf16(128)
    R_const = B.R("const")
    B.memset("pool", ones_f, 1.0, writes=[R_const])
    B.memset("pool", ones_b, 1.0, writes=[R_const])
    B.dma(keep_sb, keep[:, :], writes=[R_const])
    B.dma(lng, ln_g.rearrange("p a c -> p (a c)"), writes=[R_const])
    B.dma(lnb, ln_b.rearrange("p a c -> p (a c)"), writes=[R_const])
    hg = A.f32(72 * 3)
    hg3 = hg.rearrange("p (j c) -> p j c", c=3)
    B.bar_tile = A.f32(2)
    LF = A.f32(128 * 8)
    LF3 = LF.rearrange("p (b h) -> p b h", h=8)
    R_LF = B.R("LF")
    identf = A.f32(128)
    identb = A.bf16(128)
    negkeep = A.f32(NT)
    keepbig = A.f32(NT)
    B.dma(identf, ident_d[:, :], writes=[R_const])
    trif = A.f32(128)
    cmask = A.f32(128)
    B.dma(trif, tri_d[:, :], writes=[R_const])
    B.dma(cmask, cmask_d[:, :], writes=[R_const])
    B.cp("dve", identb, identf, reads=[R_const], writes=[R_const])
    B.ts("dve", negkeep, keep_sb, -1.0, None, ALU.mult, reads=[R_const], writes=[R_const])
    B.ts("dve", keepbig, keep_sb, -1.0, BIG, ALU.add, ALU.mult, reads=[R_const], writes=[R_const])
    gis = A.f32(32)
    gfs = A.f32(32)
    lfn = A.f32(16)
    R_smp = B.R("smp")
    stage_off = A.off

    ct = A.f32(24)
    ct3 = ct.rearrange("p (k c) -> p k c", c=3)
    sct = A.f32(24)
    sct3 = sct.rearrange("p (k c) -> p k c", c=3)
    bada = A.f32(72)
    R_ct = B.R("ct")
    B.dma(ct, cT.rearrange("p k c -> p (k c)"), writes=[R_ct])
    B.dma(bada, b_ada[:, :], writes=[R_ct])
    B.act(sct, ct, AF.Silu, reads=[R_ct], writes=[R_ct])
    WA = 1152
    wbuf = [A.f32(8 * WA) for _ in range(2)]
    Rw = [B.R("wada0"), B.R("wada1")]
    R_mod = B.R("mod")
    for g in range(8):
        wb = wbuf[g % 2]
        wb3 = wb.rearrange("p (k n) -> p k n", n=WA)
        B.dma(wb3, w_ada[:, :, g * WA:(g + 1) * WA], writes=[Rw[g % 2]], q=("sp" if g % 2 == 0 else "pool"))
        pt = psum[g % 2]
        for j in range(9):
            for k in range(8):
                B.mm(pt[:, j * 3:j * 3 + 3], wb3[:, k, j * 128:(j + 1) * 128], sct3[:, k, :], k == 0, k == 7,
                     reads=[Rw[g % 2], R_ct], writes=[PS[g % 2]])
        for j in range(9):
            jj = g * 9 + j
            B.ts("dve", modv3[:, jj, :], pt[:, j * 3:j * 3 + 3], bada[:, jj:jj + 1], None, ALU.add,
                 reads=[PS[g % 2], R_ct], writes=[R_mod])
    B.ts("dve", opsc, modv, 1.0, None, ALU.add, reads=[R_mod], writes=[R_mod])
    B.ts("dve", hg, modv, 0.5, None, ALU.mult, reads=[R_mod], writes=[R_mod])

    def barrier_all(tag):
        r = B.R("bar_" + tag)
        allres = list(B.rescache.values())
        for e in ("pe", "act", "dve", "pool"):
            pass
        return r

    def ffn_stage(idx, wgu_d, wdn_d, tiles, sh_j, sc_j, g_j, ln_i, load_fn, store_fn, tagp):
        B.barrier()
        A.off = stage_off
        wgu = A.bf16(8 * 2 * DFF)
        wgu3 = wgu.rearrange("p (k n) -> p k n", n=2 * DFF)
        wdn = A.bf16(NFF * D)
        wdn3 = wdn.rearrange("p (k n) -> p k n", n=D)
        R_wgu = B.R(tagp + "wgu")
        R_wdn = B.R(tagp + "wdn")
        for k in range(8):
            B.dma(wgu3[:, k, :], wgu_d[:, k, :], writes=[R_wgu], q="pool")
        for k0 in range(0, NFF, 2):
            B.dma(wdn3[:, k0:k0 + 2, :], wdn_d[:, k0:k0 + 2, :], writes=[R_wdn], q="pool")
        xin = [A.f32(8 * TN) for _ in range(2)]
        Rx = [B.R(tagp + "x0"), B.R(tagp + "x1")]
        hb = A.bf16(8 * TN)
        R_h = B.R(tagp + "h")
        actb = A.bf16(NFF * TN)
        R_act = B.R(tagp + "actb")
        sg = [A.f32(TN) for _ in range(2)]
        Rsg = [B.R(tagp + "sg0"), B.R(tagp + "sg1")]
        sq = A.bf16(8 * TN)
        R_sq = B.R(tagp + "sq")
        rb = A.bf16(8 * TN)
        R_rb = B.R(tagp + "rb")
        st1 = A.f32(TN)
        st2 = A.f32(TN)
        st3 = A.f32(TN)
        R_st = B.R(tagp + "st")
        pi = 0
        for ti, (tid, N, cond) in enumerate(tiles):
            b = ti % 2
            x3 = xin[b].rearrange("p (k n) -> p k n", n=TN)
            if ti == 0:
                load_fn(tid, x3[:, :, 0:N], Rx[b])
            if ti + 1 < len(tiles):
                nt_, nN, _ = tiles[ti + 1]
                load_fn(nt_, xin[1 - b].rearrange("p (k n) -> p k n", n=TN)[:, :, 0:nN], Rx[1 - b])
            h3 = hb.rearrange("p (k n) -> p k n", n=TN)
            a3 = actb.rearrange("p (k n) -> p k n", n=TN)
            r3 = x3
            R_r = Rx[b]
            q3 = sq.rearrange("p (k n) -> p k n", n=TN)
            rb3 = rb.rearrange("p (k n) -> p k n", n=TN)
            def grp(cond_, N_):
                return [(0, N_, cond_)] if isinstance(cond_, int) else cond_
            groups = grp(cond, N)

            def emit_h(tj):
                _, Nj, condj = tiles[tj]
                bj = tj % 2
                xj = xin[bj].rearrange("p (k n) -> p k n", n=TN)
                for k in range(8):
                    for (c0, c1, ci) in grp(condj, Nj):
                        B.act(h3[:, k, c0:c1], xj[:, k, c0:c1], AF.Identity,
                              bias=modv3[:, sh_j + k, ci:ci + 1], scale=opsc3[:, sc_j + k, ci:ci + 1],
                              reads=[Rx[bj], R_mod], writes=[R_h])
                B.act(xj[:, :, 0:Nj], xj[:, :, 0:Nj], AF.Copy, scale=ALPHA, reads=[Rx[bj]], writes=[Rx[bj]])
            if ti == 0:
                emit_h(0)
            for m in range(NFF):
                pg = pi % 8
                pu = (pi + 1) % 8
                pi += 2
                for k in range(8):
                    B.mm(psum[pg][:, 0:N], wgu3[:, k, m * 128:(m + 1) * 128], h3[:, k, 0:N], k == 0, k == 7,
                         reads=[R_wgu, R_h], writes=[PS[pg]])
                for k in range(8):
                    B.mm(psum[pu][:, 0:N], wgu3[:, k, DFF + m * 128:DFF + (m + 1) * 128], h3[:, k, 0:N], k == 0, k == 7,
                         reads=[R_wgu, R_h], writes=[PS[pu]])
                si = m % 2
                B.act(sg[si][:, 0:N], psum[pg][:, 0:N], AF.Silu, reads=[PS[pg]], writes=[Rsg[si]])
                B.tt("dve", a3[:, m, 0:N], sg[si][:, 0:N], psum[pu][:, 0:N], ALU.mult,
                     reads=[Rsg[si], PS[pu]], writes=[R_act])
            if ti + 1 < len(tiles):
                emit_h(ti + 1)
            for d in range(8):
                pd = pi % 8
                pi += 1
                for m in range(NFF):
                    B.mm(psum[pd][:, 0:N], wdn3[:, m, d * 128:(d + 1) * 128], a3[:, m, 0:N], m == 0, m == NFF - 1,
                         reads=[R_wdn, R_act], writes=[PS[pd]])
                for (c0, c1, ci) in groups:
                    B.stt("dve", r3[:, d, c0:c1], psum[pd][:, c0:c1], hg3[:, g_j + d, ci:ci + 1], x3[:, d, c0:c1],
                          ALU.mult, ALU.add, reads=[PS[pd], R_mod, Rx[b]], writes=[R_r])
                B.act(q3[:, d, 0:N], r3[:, d, 0:N], AF.Square, reads=[R_r], writes=[R_sq])
                B.act(rb3[:, d, 0:N], r3[:, d, 0:N], AF.Copy, reads=[R_r], writes=[R_rb])
            p1 = pi % 8
            p2 = (pi + 1) % 8
            pi += 2
            for d in range(8):
                B.mm(psum[p1][:, 0:N], ones_b.rearrange("p (a n) -> p a n", a=1)[:, 0, :], rb3[:, d, 0:N], d == 0, d == 7,
                     reads=[R_rb, R_const], writes=[PS[p1]])
            for d in range(8):
                B.mm(psum[p2][:, 0:N], ones_b.rearrange("p (a n) -> p a n", a=1)[:, 0, :], q3[:, d, 0:N], d == 0, d == 7,
                     reads=[R_sq, R_const], writes=[PS[p2]])
            B.ts("dve", st1[:, 0:N], psum[p1][:, 0:N], -1.0 / D, None, ALU.mult, reads=[PS[p1]], writes=[R_st])
            B.tt("dve", st2[:, 0:N], st1[:, 0:N], st1[:, 0:N], ALU.mult, reads=[R_st], writes=[R_st])
            B.stt("dve", st3[:, 0:N], psum[p2][:, 0:N], 1.0 / D, st2[:, 0:N], ALU.mult, ALU.subtract,
                  reads=[PS[p2], R_st], writes=[R_st])
            B.ts("dve", st3[:, 0:N], st3[:, 0:N], EPS, None, ALU.add, reads=[R_st], writes=[R_st])
            B.act(st3[:, 0:N], st3[:, 0:N], AF.Sqrt, reads=[R_st], writes=[R_st])
            B.s.op("dve", lambda e, a=st3[:, 0:N]: e.reciprocal(a, a), [R_st], [R_st])
            for d in range(8):
                B.tt("dve", r3[:, d, 0:N], r3[:, d, 0:N], st1[:, 0:N], ALU.add, reads=[R_r, R_st], writes=[R_r])
                B.tt("dve", r3[:, d, 0:N], r3[:, d, 0:N], st3[:, 0:N], ALU.mult, reads=[R_r, R_st], writes=[R_r])
                B.act(r3[:, d, 0:N], r3[:, d, 0:N], AF.Identity, bias=lnb3[:, ln_i, d:d + 1], scale=lng3[:, ln_i, d:d + 1],
                      reads=[R_r, R_const], writes=[R_r])
            store_fn(tid, r3[:, :, 0:N], R_r)

    R_X1 = B.R("X1")

    def load1(tid, dst, res):
        if tid < NT:
            B.dma(dst, xT[:, :, tid * TN:(tid + 1) * TN], writes=[res])
        else:
            B.dma(dst, xsT[:, :, :], writes=[res])

    def store1(tid, src, res):
        Nn = TN if tid < NT else 32
        B.dma(X1[tid][:, :, 0:Nn], src, reads=[res], writes=[R_X1], q="pool")

    tiles1 = [(t, TN, 0) for t in range(NT)] + [(NT, 32, [(0, 16, 1), (16, 32, 2)])]
    ffn_stage(0, w_gu[0], w_dn[0], tiles1, 0, 8, 16, 0, load1, store1, "f1")


    FMC = ([("ak", 512 + 128 * c) for c in range(4)] + [("mk", 2056 + 128 * c) for c in range(4)] +
           [("aq", 128 * c) for c in range(4)] + [("mq", 1544 + 128 * c) for c in range(4)] +
           [("mo", 3088 + 128 * c) for c in range(4)] + [("ga", 3600 + 128 * c) for c in range(8)] +
           [("gb", 4624 + 128 * c) for c in range(8)])
    B.barrier()
    A.off = stage_off
    win = A.bf16(8 * DIN)
    win3 = win.rearrange("p (k n) -> p k n", n=DIN)
    R_win = B.R("win")
    for k in range(8):
        B.dma(win3[:, k, :], w_in[:, k, :], writes=[R_win], q="pool")
    bbc = A.f32(1032)
    bfm = A.f32(40)
    nbfm = A.f32(40)
    cw = A.f32(32)
    cw3 = cw.rearrange("p (c j) -> p c j", j=4)
    cb = A.f32(8)
    R_c2 = B.R("c2")
    B.dma(bbc, b_in_bc[:, :], writes=[R_c2])
    B.dma(bfm, b_in_fm[:, :], writes=[R_c2])
    B.dma(cw, conv_w.rearrange("p c j -> p (c j)"), writes=[R_c2])
    B.dma(cb, conv_b[:, :], writes=[R_c2])
    B.ts("dve", nbfm, bfm, -1.0, None, ALU.mult, reads=[R_c2], writes=[R_c2])
    x1b = [A.f32(8 * TN) for _ in range(3)]
    Rx1 = [B.R("s2x0"), B.R("s2x1"), B.R("s2x2")]
    h2s = [A.bf16(8 * TN) for _ in range(2)]
    h2s3 = [h.rearrange("p (k n) -> p k n", n=TN) for h in h2s]
    R_h2s = [B.R("h2_0"), B.R("h2_1")]
    UW = TN + 3
    ub = A.f32(8 * UW)
    u3 = ub.rearrange("p (c n) -> p c n", n=UW)
    R_u = B.R("u")
    B.memset("pool", ub, 0.0, writes=[R_u])
    co = A.f32(8 * TN)
    co3 = co.rearrange("p (c n) -> p c n", n=TN)
    R_co = B.R("co")
    qkbs = [A.bf16(8 * TN) for _ in range(2)]
    qkbs3 = [q_.rearrange("p (c n) -> p c n", n=TN) for q_ in qkbs]
    R_qkbs = [B.R("qkb0"), B.R("qkb1")]
    kTf = A.f32(4 * TN)
    kTf3 = kTf.rearrange("p (c n) -> p c n", n=TN)
    R_kTf = B.R("kTf")
    kTb = A.bf16(4 * TN)
    kTb3 = kTb.rearrange("p (c n) -> p c n", n=TN)
    R_kTb = B.R("kTb")
    qTb = A.bf16(4 * TN)
    qTb3 = qTb.rearrange("p (c n) -> p c n", n=TN)
    R_qTb = B.R("qTb")
    sgb = A.bf16(20 * TN)
    sgb3 = sgb.rearrange("p (c n) -> p c n", n=TN)
    R_sgb = B.R("sgb")
    vf = [A.f32(520) for _ in range(2)]
    R_vf = [B.R("vf0"), B.R("vf1")]
    va = [A.bf16(8 * 65) for _ in range(2)]
    R_va = [B.R("va0"), B.R("va1")]
    mvb = [A.bf16(4 * 130) for _ in range(2)]
    R_mvb = [B.R("mvb0"), B.R("mvb1")]
    ktok = [A.bf16(512) for _ in range(2)]
    R_ktok = [B.R("ktok0"), B.R("ktok1")]
    lft = A.f32(16)
    R_lft = B.R("lft")
    gt = A.f32(2 * TN)
    gt3 = gt.rearrange("p (a n) -> p a n", n=TN)
    R_gt = B.R("gt")
    for i in range(2):
        B.memset("pool", va[i], 1.0, writes=[R_va[i]])
        B.memset("pool", mvb[i], 1.0, writes=[R_mvb[i]])
    R_KT = B.R("KT"); R_VA = B.R("VA"); R_MK = B.R("MK"); R_MV = B.R("MV"); R_G = B.R("GIF")
    R_QT = B.R("QT"); R_MQT = B.R("MQT"); R_MKT = B.R("MKT"); R_SG = B.R("SG")
    pi = 0
    sq_ = ["sp", "sp"]
    def emit_h2(tj):
        bj = tj % 2
        xj = x1b[tj % 3].rearrange("p (k n) -> p k n", n=TN)
        for k in range(8):
            B.act(h2s3[bj][:, k, :], xj[:, k, :], AF.Identity, bias=modv3[:, 24 + k, 0:1], scale=opsc3[:, 32 + k, 0:1],
                  reads=[Rx1[tj % 3], R_mod], writes=[R_h2s[bj]])

    def emit_ktr(tj):
        bj = tj % 2
        nonlocal_pi = [0]
        for sb in range(2):
            blk = tj * 2 + sb
            p = 6 + sb
            pb = psum[p][:, :].bitcast(BF16)
            for hh in range(4):
                s.op("pe", lambda e, o_=pb[:, hh * 128:(hh + 1) * 128], i_=qkbs3[bj][:, 4 + hh, sb * 128:(sb + 1) * 128]:
                     e.transpose(o_, i_, identb), [R_qkbs[bj], R_const], [PS[p]])
            B.cp("act", ktok[sb], pb[:, 0:512], reads=[PS[p]], writes=[R_ktok[sb]])
            B.dma(MK[blk], ktok[sb], reads=[R_ktok[sb]], writes=[R_MK], q="sp")

    B.dma(x1b[0].rearrange("p (k n) -> p k n", n=TN), X1[0], reads=[R_X1], writes=[Rx1[0]])
    B.dma(x1b[1].rearrange("p (k n) -> p k n", n=TN), X1[1], reads=[R_X1], writes=[Rx1[1]])
    emit_h2(0)
    pending = None
    for t in range(NT):
        own = t >= OWN0
        b = t % 2
        if t + 2 < NT:
            B.dma(x1b[(t + 2) % 3].rearrange("p (k n) -> p k n", n=TN), X1[t + 2], reads=[R_X1], writes=[Rx1[(t + 2) % 3]])
        if t + 1 < NT:
            emit_h2(t + 1)
        N = TN
        h23 = h2s3[b]
        R_h2 = R_h2s[b]
        qkb3 = qkbs3[b]
        R_qkb = R_qkbs[b]
        kcol = keep_sb[:, t:t + 1]
        nch = len(FMC) if own else 8
        for j in range(nch):
            name, c0 = FMC[j]
            p = pi % 6
            pi += 1
            for k in range(8):
                B.mm(psum[p][:, 0:N], win3[:, k, c0:c0 + 128], h23[:, k, :], k == 0, k == 7,
                     reads=[R_win, R_h2], writes=[PS[p]])
            c = j % 4 if name not in ("ga", "gb") else (j - 20) % 8
            if name == "ak":
                B.act(kTf3[:, c, :], psum[p][:, 0:N], AF.Identity, bias=bfm[:, j:j + 1], reads=[PS[p], R_c2], writes=[R_kTf])
                B.cp("dve", kTb3[:, c, :], kTf3[:, c, :], reads=[R_kTf], writes=[R_kTb])
            elif name == "mk":
                B.ts("dve", u3[:, 4 + c, 3:3 + N], psum[p][:, 0:N], bfm[:, j:j + 1], kcol, ALU.add, ALU.mult,
                     reads=[PS[p], R_c2, R_const], writes=[R_u])
            elif name == "mq":
                B.ts("dve", u3[:, c, 3:3 + N], psum[p][:, 0:N], bfm[:, j:j + 1], kcol, ALU.add, ALU.mult,
                     reads=[PS[p], R_c2, R_const], writes=[R_u])
            elif name == "aq":
                B.ts("dve", qTb3[:, c, :], psum[p][:, 0:N], bfm[:, j:j + 1], 0.125, ALU.add, ALU.mult,
                     reads=[PS[p], R_c2], writes=[R_qTb])
            else:
                jj = {"mo": 0, "ga": 4, "gb": 12}[name] + c
                B.act(sgb3[:, jj, :], psum[p][:, 0:N], AF.Sigmoid, bias=bfm[:, j:j + 1], reads=[PS[p], R_c2], writes=[R_sgb])
        p = pi % 6
        pi += 1
        for k in range(8):
            B.mm(psum[p][0:4, 0:N], win3[:, k, 3080:3084], h23[:, k, :], k == 0, k == 7, reads=[R_win, R_h2], writes=[PS[p]])
        B.ts("dve", gt3[0:4, 0, :], psum[p][0:4, 0:N], bfm[0:4, 36:37], kcol[0:4, :], ALU.add, ALU.mult,
             reads=[PS[p], R_c2, R_const], writes=[R_gt])
        B.ts("dve", gt3[0:4, 0, :], gt3[0:4, 0, :], keepbig[0:4, t:t + 1], None, ALU.add, reads=[R_gt, R_const], writes=[R_gt])
        p = pi % 6
        pi += 1
        for k in range(8):
            B.mm(psum[p][0:4, 0:N], win3[:, k, 3084:3088], h23[:, k, :], k == 0, k == 7, reads=[R_win, R_h2], writes=[PS[p]])
        B.act(gt3[0:4, 1, :], psum[p][0:4, 0:N], AF.Exp, bias=nbfm[0:4, 37:38], scale=-1.0, reads=[PS[p], R_c2], writes=[R_gt])
        B.act(gt3[0:4, 1, :], gt3[0:4, 1, :], AF.Ln, bias=1.0, reads=[R_gt], writes=[R_gt])
        B.ts("dve", gt3[0:4, 1, :], gt3[0:4, 1, :], negkeep[0:4, t:t + 1], None, ALU.mult, reads=[R_gt, R_const], writes=[R_gt])
        B.dma(GI[:, t * TN:(t + 1) * TN], gt3[0:4, 0, :], reads=[R_gt], writes=[R_G], q="sp")
        B.dma(GF[:, t * TN:(t + 1) * TN], gt3[0:4, 1, :], reads=[R_gt], writes=[R_G], q="sp")
        for sb in range(2):
            blk = t * 2 + sb
            tok = slice(sb * 128, (sb + 1) * 128)
            p = pi % 6
            pi += 1
            for k in range(8):
                B.mm(psum[p][:, 0:512], h23[:, k, tok], win3[:, k, 1024:1536], k == 0, k == 7, reads=[R_win, R_h2], writes=[PS[p]])
            B.tt("dve", vf[sb][:, 0:512], psum[p][:, 0:512], bbc[:, 0:512], ALU.add, reads=[PS[p], R_c2], writes=[R_vf[sb]])
            B.cp("act", va[sb].rearrange("p (h d) -> p h d", d=65)[:, :, 0:64],
                 vf[sb][:, 0:512].rearrange("p (h d) -> p h d", d=64), reads=[R_vf[sb]], writes=[R_va[sb]])
            B.dma(VA[:, :, blk, :].rearrange("h p d -> p h d"), va[sb].rearrange("p (h d) -> p h d", d=65), reads=[R_va[sb]], writes=[R_VA], q="sp")
            if own:
                B.dma(ov[(blk - 2 * OWN0) * 128:(blk - 2 * OWN0 + 1) * 128, :], vf[sb][:, 0:512], reads=[R_vf[sb]], final=True, q="sp")
            p = pi % 6
            pi += 1
            for k in range(8):
                B.mm(psum[p][:, 0:8], h23[:, k, tok], win3[:, k, 1536:1544], k == 0, k == 7, reads=[R_win, R_h2], writes=[PS[p]])
            B.tt("dve", lft[:, 0:8], psum[p][:, 0:8], bbc[:, 512:520], ALU.add, reads=[PS[p], R_c2], writes=[R_lft])
            B.act(lft[:, 0:8], lft[:, 0:8], AF.Exp, scale=-1.0, reads=[R_lft], writes=[R_lft])
            B.act(lft[:, 0:8], lft[:, 0:8], AF.Ln, bias=1.0, reads=[R_lft], writes=[R_lft])
            B.ts("dve", LF3[:, blk, :], lft[:, 0:8], negkeep[:, t:t + 1], None, ALU.mult, reads=[R_lft, R_const], writes=[R_LF])
            if own:
                B.dma(olf[(blk - 2 * OWN0) * 128:(blk - 2 * OWN0 + 1) * 128, :], LF3[:, blk, :], reads=[R_LF], final=True, q="sp")
            p = pi % 6
            pi += 1
            for k in range(8):
                B.mm(psum[p][:, 0:512], h23[:, k, tok], win3[:, k, 2568:3080], k == 0, k == 7, reads=[R_win, R_h2], writes=[PS[p]])
            B.tt("dve", mvb[sb].rearrange("p (h d) -> p h d", d=130)[:, :, 0:128],
                 psum[p][:, 0:512].rearrange("p (h d) -> p h d", d=128),
                 bbc[:, 520:1032].rearrange("p (h d) -> p h d", d=128), ALU.add, reads=[PS[p], R_c2], writes=[R_mvb[sb]])
            B.dma(MV[blk], mvb[sb], reads=[R_mvb[sb]], writes=[R_MV], q="sp")
        chs = list(range(8)) if own else [4, 5, 6, 7]
        for ch in chs:
            B.ts("dve", co3[:, ch, :], u3[:, ch, 0:N], cw3[:, ch, 0:1], cb[:, ch:ch + 1], ALU.mult, ALU.add,
                 reads=[R_u, R_c2], writes=[R_co])
            for j in range(1, 4):
                B.stt("dve", co3[:, ch, :], u3[:, ch, j:j + N], cw3[:, ch, j:j + 1], co3[:, ch, :], ALU.mult, ALU.add,
                      reads=[R_u, R_c2, R_co], writes=[R_co])
            if ch < 4:
                B.act(qkb3[:, ch, :], co3[:, ch, :], AF.Silu, reads=[R_co], writes=[R_qkb])
            else:
                B.act(co3[:, ch, :], co3[:, ch, :], AF.Silu, reads=[R_co], writes=[R_co])
                B.ts("dve", qkb3[:, ch, :], co3[:, ch, :], 128.0 ** -0.5, None, ALU.mult, reads=[R_co], writes=[R_qkb])
        if t == NT - 1:
            B.dma(oconv[:, :, :], u3[:, :, TN:TN + 3], reads=[R_u], final=True)
        B.cp("dve", u3[:, :, 0:3], u3[:, :, TN:TN + 3], reads=[R_u], writes=[R_u])
        if pending is not None:
            emit_ktr(pending)
        pending = t
        for h in range(8):
            B.dma(KT[h][:, t * TN:(t + 1) * TN], kTb3[(h % 2) * 64:(h % 2 + 1) * 64, h // 2, :], reads=[R_kTb], writes=[R_KT],
                  q=sq_[h % 2])
        if own:
            to = t - OWN0
            B.dma(okT[:, :, to * TN:(to + 1) * TN], kTf3, reads=[R_kTf], final=True, q="sp")
            for h in range(8):
                B.dma(QT[h][:, to * TN:(to + 1) * TN], qTb3[(h % 2) * 64:(h % 2 + 1) * 64, h // 2, :], reads=[R_qTb], writes=[R_QT],
                      q=sq_[h % 2])
            for hh in range(4):
                B.dma(MQT[hh][:, to * TN:(to + 1) * TN], qkb3[:, hh, :], reads=[R_qkb], writes=[R_MQT], q="sp")
                B.dma(MKT[hh][:, to * TN:(to + 1) * TN], qkb3[:, 4 + hh, :], reads=[R_qkb], writes=[R_MKT], q="sp")
            B.dma(SG[to], sgb, reads=[R_sgb], writes=[R_SG], q="sp")


    emit_ktr(pending)
    h23 = h2s3[0]
    R_h2 = R_h2s[0]
    qkb3 = qkbs3[0]
    R_qkb = R_qkbs[0]
    NS = 32
    xs3 = x1b[0].rearrange("p (k n) -> p k n", n=TN)
    B.dma(xs3[:, :, 0:NS], X1[NT][:, :, 0:NS], reads=[R_X1], writes=[Rx1[0]])
    for k in range(8):
        for e in range(2):
            B.act(h23[:, k, e * 16:(e + 1) * 16], xs3[:, k, e * 16:(e + 1) * 16], AF.Identity,
                  bias=modv3[:, 24 + k, 1 + e:2 + e], scale=opsc3[:, 32 + k, 1 + e:2 + e], reads=[Rx1[0], R_mod], writes=[R_h2])
    usb = A.f32(8 * 2 * 19)
    us4 = usb.rearrange("p (c e n) -> p c e n", e=2, n=19)
    R_us = B.R("us")
    B.dma(us4[:, :, :, 0:3], convs_d[:, :, :, :], writes=[R_us])
    cos = co[:, 0:8 * 32]
    cos4 = cos.rearrange("p (c e n) -> p c e n", e=2, n=16)
    R_cos = R_co
    for j in range(len(FMC)):
        name, c0 = FMC[j]
        p = pi % 8
        pi += 1
        for k in range(8):
            B.mm(psum[p][:, 0:NS], win3[:, k, c0:c0 + 128], h23[:, k, 0:NS], k == 0, k == 7, reads=[R_win, R_h2], writes=[PS[p]])
        c = j % 4 if name not in ("ga", "gb") else (j - 20) % 8
        if name == "ak":
            B.act(kTf3[:, c, 0:NS], psum[p][:, 0:NS], AF.Identity, bias=bfm[:, j:j + 1], reads=[PS[p], R_c2], writes=[R_kTf])
            B.cp("dve", kTb3[:, c, 0:NS], kTf3[:, c, 0:NS], reads=[R_kTf], writes=[R_kTb])
        elif name in ("mk", "mq"):
            ch = c + (4 if name == "mk" else 0)
            B.ts("dve", us4[:, ch, :, 3:19], psum[p][:, 0:NS].rearrange("p (e n) -> p e n", n=16), bfm[:, j:j + 1], None, ALU.add,
                 reads=[PS[p], R_c2], writes=[R_us])
        elif name == "aq":
            B.ts("dve", qTb3[:, c, 0:NS], psum[p][:, 0:NS], bfm[:, j:j + 1], 0.125, ALU.add, ALU.mult, reads=[PS[p], R_c2], writes=[R_qTb])
        else:
            jj = {"mo": 0, "ga": 4, "gb": 12}[name] + c
            B.act(sgb3[:, jj, 0:NS], psum[p][:, 0:NS], AF.Sigmoid, bias=bfm[:, j:j + 1], reads=[PS[p], R_c2], writes=[R_sgb])
    p = pi % 8
    pi += 1
    for k in range(8):
        B.mm(psum[p][0:4, 0:NS], win3[:, k, 3080:3084], h23[:, k, 0:NS], k == 0, k == 7, reads=[R_win, R_h2], writes=[PS[p]])
    B.ts("dve", gis[0:4, :], psum[p][0:4, 0:NS], bfm[0:4, 36:37], None, ALU.add, reads=[PS[p], R_c2], writes=[R_smp])
    p = pi % 8
    pi += 1
    for k in range(8):
        B.mm(psum[p][0:4, 0:NS], win3[:, k, 3084:3088], h23[:, k, 0:NS], k == 0, k == 7, reads=[R_win, R_h2], writes=[PS[p]])
    B.act(gfs[0:4, :], psum[p][0:4, 0:NS], AF.Exp, bias=nbfm[0:4, 37:38], scale=-1.0, reads=[PS[p], R_c2], writes=[R_smp])
    B.act(gfs[0:4, :], gfs[0:4, :], AF.Ln, bias=1.0, reads=[R_smp], writes=[R_smp])
    B.ts("dve", gfs[0:4, :], gfs[0:4, :], -1.0, None, ALU.mult, reads=[R_smp], writes=[R_smp])
    R_VAs = B.R("VAs"); R_MVs = B.R("MVs"); R_MKs = B.R("MKs")
    for e in range(2):
        tok = slice(e * 16, (e + 1) * 16)
        p = pi % 8
        pi += 1
        for k in range(8):
            B.mm(psum[p][0:16, 0:512], h23[:, k, tok], win3[:, k, 1024:1536], k == 0, k == 7, reads=[R_win, R_h2], writes=[PS[p]])
        B.tt("dve", vf[e][0:16, 0:512], psum[p][0:16, 0:512], bbc[0:16, 0:512], ALU.add, reads=[PS[p], R_c2], writes=[R_vf[e]])
        B.cp("act", va[e][0:16, :].rearrange("p (h d) -> p h d", d=65)[:, :, 0:64],
             vf[e][0:16, 0:512].rearrange("p (h d) -> p h d", d=64), reads=[R_vf[e]], writes=[R_va[e]])
        B.dma(VAs[e], va[e][0:16, :], reads=[R_va[e]], writes=[R_VAs], q="sp")
        B.dma(ovs[e], vf[e][0:16, 0:512], reads=[R_vf[e]], final=True, q="sp")
        p = pi % 8
        pi += 1
        for k in range(8):
            B.mm(psum[p][0:16, 0:8], h23[:, k, tok], win3[:, k, 1536:1544], k == 0, k == 7, reads=[R_win, R_h2], writes=[PS[p]])
        B.tt("dve", lft[0:16, 0:8], psum[p][0:16, 0:8], bbc[0:16, 512:520], ALU.add, reads=[PS[p], R_c2], writes=[R_lft])
        B.act(lft[0:16, 0:8], lft[0:16, 0:8], AF.Exp, scale=-1.0, reads=[R_lft], writes=[R_lft])
        B.act(lft[0:16, 0:8], lft[0:16, 0:8], AF.Ln, bias=1.0, reads=[R_lft], writes=[R_lft])
        B.ts("dve", lfn[0:16, e * 8:(e + 1) * 8], lft[0:16, 0:8], -1.0, None, ALU.mult, reads=[R_lft], writes=[R_smp])
        B.dma(olfs[e], lfn[0:16, e * 8:(e + 1) * 8], reads=[R_smp], final=True, q="sp")
        p = pi % 8
        pi += 1
        for k in range(8):
            B.mm(psum[p][0:16, 0:512], h23[:, k, tok], win3[:, k, 2568:3080], k == 0, k == 7, reads=[R_win, R_h2], writes=[PS[p]])
        B.tt("dve", mvb[e][0:16, :].rearrange("p (h d) -> p h d", d=130)[:, :, 0:128],
             psum[p][0:16, 0:512].rearrange("p (h d) -> p h d", d=128),
             bbc[0:16, 520:1032].rearrange("p (h d) -> p h d", d=128), ALU.add, reads=[PS[p], R_c2], writes=[R_mvb[e]])
        B.dma(MVs[e], mvb[e][0:16, :], reads=[R_mvb[e]], writes=[R_MVs], q="sp")
    for ch in range(8):
        B.ts("dve", cos4[:, ch, :, :], us4[:, ch, :, 0:16], cw3[:, ch, 0:1], cb[:, ch:ch + 1], ALU.mult, ALU.add,
             reads=[R_us, R_c2], writes=[R_cos])
        for j in range(1, 4):
            B.stt("dve", cos4[:, ch, :, :], us4[:, ch, :, j:j + 16], cw3[:, ch, j:j + 1], cos4[:, ch, :, :], ALU.mult, ALU.add,
                  reads=[R_us, R_c2, R_cos], writes=[R_cos])
        B.act(cos4[:, ch, :, :], cos4[:, ch, :, :], AF.Silu, reads=[R_cos], writes=[R_cos])
        B.ts("dve", qkb3[:, ch, 0:NS].rearrange("p (e n) -> p e n", n=16), cos4[:, ch, :, :], (1.0 if ch < 4 else 128.0 ** -0.5), None, ALU.mult,
             reads=[R_cos], writes=[R_qkb])
    B.dma(ocs[:, :, :, :], us4[:, :, :, 16:19], reads=[R_us], final=True)
    for e in range(2):
        p = pi % 8
        pi += 1
        pb = psum[p][:, :].bitcast(BF16)
        for hh in range(4):
            s.op("pe", lambda e_, o_=pb[0:16, hh * 128:(hh + 1) * 128], i_=qkb3[:, 4 + hh, e * 16:(e + 1) * 16]:
                 e_.transpose(o_, i_, identb), [R_qkb, R_const], [PS[p]])
        B.cp("act", ktok[e][0:16, :], pb[0:16, 0:512], reads=[PS[p]], writes=[R_ktok[e]])
        B.dma(MKs[e], ktok[e][0:16, :], reads=[R_ktok[e]], writes=[R_MKs], q="sp")
    R_KTs = B.R("KTs"); R_QTs = B.R("QTs"); R_MQTs = B.R("MQTs"); R_SGs = B.R("SGs")
    B.dma(oksT[:, :, :], kTf3[:, :, 0:NS], reads=[R_kTf], final=True, q="sp")
    for h in range(8):
        B.dma(KTs[h], kTb3[(h % 2) * 64:(h % 2 + 1) * 64, h // 2, 0:NS], reads=[R_kTb], writes=[R_KTs], q=sq_[h % 2])
        B.dma(QTs[h], qTb3[(h % 2) * 64:(h % 2 + 1) * 64, h // 2, 0:NS], reads=[R_qTb], writes=[R_QTs], q=sq_[h % 2])
    for hh in range(4):
        B.dma(MQTs[hh], qkb3[:, hh, 0:NS], reads=[R_qkb], writes=[R_MQTs], q="sp")
        B.dma(MKTs[hh], qkb3[:, 4 + hh, 0:NS], reads=[R_qkb], writes=[R_MQTs], q="sp")
    B.dma(SGs.rearrange("p (c n) -> p c n", n=NS), sgb3[:, :, 0:NS], reads=[R_sgb], writes=[R_SGs], q="sp")

    B.barrier()
    A.off = stage_off
    R_F = B.R("F")
    Fk = A.f32(1024)
    Fk3 = Fk.rearrange("p (b h) -> p b h", h=8)
    Tb = [A.f32(1024) for _ in range(2)]
    R_Tb = [B.R("Tb0"), B.R("Tb1")]
    for half in range(2):
        cs = slice(half * 512, (half + 1) * 512)
        B.mm(psum[half][:, :], trif, LF[:, cs], True, True, reads=[R_const, R_LF], writes=[PS[half]])
        B.mm(psum[2 + half][:, :], ones_f, LF[:, cs], True, True, reads=[R_const, R_LF], writes=[PS[2 + half]])
        B.cp("dve", Fk[:, cs], psum[half][:, :], reads=[PS[half]], writes=[R_F])
        B.cp("dve", Tb[0][:, cs], psum[2 + half][:, :], reads=[PS[2 + half]], writes=[R_Tb[0]])
    B.tt("dve", Fk, Fk, Tb[0], ALU.subtract, reads=[R_F, R_Tb[0]], writes=[R_F])
    cur = 0
    sh = 1
    while sh < 128:
        a3 = Tb[cur].rearrange("p (b h) -> p b h", h=8)
        n3 = Tb[1 - cur].rearrange("p (b h) -> p b h", h=8)
        B.cp("dve", n3[:, 0:sh, :], a3[:, 0:sh, :], reads=[R_Tb[cur]], writes=[R_Tb[1 - cur]])
        B.tt("dve", n3[:, sh:128, :], a3[:, sh:128, :], a3[:, 0:128 - sh, :], ALU.add, reads=[R_Tb[cur]], writes=[R_Tb[1 - cur]])
        cur = 1 - cur
        sh *= 2
    Tinc3 = Tb[cur].rearrange("p (b h) -> p b h", h=8)
    B.tt("dve", Fk, Fk, Tb[cur], ALU.add, reads=[R_F, R_Tb[cur]], writes=[R_F])
    for h in range(8):
        B.ts("dve", Fk3[:, :, h], Fk3[:, :, h], Tinc3[:, 2 * OWN0 - 1, h:h + 1], None, ALU.subtract,
             reads=[R_F, R_Tb[cur]], writes=[R_F])
    bK = A.f32(1024)
    bK3 = bK.rearrange("p (h b) -> p h b", b=128)
    R_bK = B.R("bK")
    for h in range(8):
        B.ts("dve", bK3[:, h, :], Fk3[:, :, h], -1.0, None, ALU.mult, reads=[R_F], writes=[R_bK])
    for t in range(OWN0):
        B.ts("dve", bK3[:, :, 2 * t:2 * t + 2], bK3[:, :, 2 * t:2 * t + 2], keepbig[:, t:t + 1], None, ALU.add,
             reads=[R_bK, R_const], writes=[R_bK])
    fq = A.bf16(4096)
    R_fq = B.R("fq")
    R_QF = B.R("QF")
    for g in range(8):
        p = 4 + g % 2
        for i in range(4):
            blk = 2 * OWN0 + g * 4 + i
            s.op("pe", lambda e, o_=psum[p][0:8, i * 128:(i + 1) * 128], i_=Fk3[:, blk, :]: e.transpose(o_, i_, identf),
                 [R_F, R_const], [PS[p]])
        B.cp("act", fq[0:8, g * 512:(g + 1) * 512], psum[p][0:8, :], reads=[PS[p]], writes=[R_fq])
    B.dma(QF[:, :], fq[0:8, :], reads=[R_fq], writes=[R_QF])

    Kb = [A.bf16(S) for _ in range(2)]
    Qb = [A.bf16(4096) for _ in range(2)]
    Vb = [A.bf16(128 * 65) for _ in range(2)]
    R_Kb = [B.R("Kb0"), B.R("Kb1")]
    R_Qb = [B.R("Qb0"), B.R("Qb1")]
    R_Vb = [B.R("Vb0"), B.R("Vb1")]
    for i in range(2):
        B.memset("pool", Kb[i][64:65, :], 1.0, writes=[R_Kb[i]])
    NPB = 4
    Pb = [A.bf16(512) for _ in range(NPB)]
    R_Pb = [B.R("Pb%d" % i) for i in range(NPB)]
    dtmp = [A.f32(128) for _ in range(2)]
    R_dt = [B.R("dt0"), B.R("dt1")]
    onf = A.f32(512)
    R_onf = B.R("onf")
    rden = A.f32(512)
    R_rden = B.R("rden")
    oTb = [A.bf16(512) for _ in range(2)]
    R_oTb = [B.R("oTb0"), B.R("oTb1")]
    R_OA = B.R("OA")

    def load_head(h):
        hb = h % 2
        B.dma(Kb[hb][0:64, :], KT[h], reads=[R_KT], writes=[R_Kb[hb]], q="sp")
        B.dma(Qb[hb][0:64, :], QT[h], reads=[R_QT], writes=[R_Qb[hb]], q="sp")
        B.dma(Qb[hb][64:65, :], QF[h:h + 1, :], reads=[R_QF], writes=[R_Qb[hb]], q="sp")
        B.dma(Vb[hb].rearrange("p (b d) -> p b d", d=65), VA[h], reads=[R_VA], writes=[R_Vb[hb]], q="pool")

    load_head(0)
    LA = 3
    ntile = 0
    for h in range(8):
        hb = h % 2
        if h + 1 < 8:
            load_head(h + 1)
        V3 = Vb[hb].rearrange("p (b d) -> p b d", d=65)
        items = []
        for lt in range(8):
            nkb = 2 * OWN0 + 4 * lt + 4
            for kb in range(nkb):
                d = kb - (2 * OWN0 + 4 * lt)
                c0 = 0 if d < 0 else d * 128
                items.append((lt, kb, c0, d, kb == 0, kb == nkb - 1))
        def emit_mm1(i):
            lt, kb, c0, d, first, last = items[i]
            sbk = i % 4
            B.mm(psum[sbk][:, c0:512], Kb[hb][0:65, kb * 128:(kb + 1) * 128], Qb[hb][0:65, lt * 512 + c0:(lt + 1) * 512],
                 True, True, reads=[R_Kb[hb], R_Qb[hb]], writes=[PS[sbk]])
        for i in range(min(LA, len(items))):
            emit_mm1(i)
        for i, (lt, kb, c0, d, first, last) in enumerate(items):
            if i + LA < len(items):
                emit_mm1(i + LA)
            sbk = i % 4
            pb_ = i % NPB
            ob = 4 + (ntile % 2)
            bias = bK3[:, h, kb:kb + 1]
            if d >= 0:
                di = i % 2
                B.tt("dve", dtmp[di], psum[sbk][:, c0:c0 + 128], cmask, ALU.add, reads=[PS[sbk], R_const], writes=[R_dt[di]])
                B.act(Pb[pb_][:, c0:c0 + 128], dtmp[di], AF.Exp, bias=bias, reads=[R_dt[di], R_bK], writes=[R_Pb[pb_]])
                if c0 + 128 < 512:
                    B.act(Pb[pb_][:, c0 + 128:512], psum[sbk][:, c0 + 128:512], AF.Exp, bias=bias,
                          reads=[PS[sbk], R_bK], writes=[R_Pb[pb_]])
            else:
                B.act(Pb[pb_][:, :], psum[sbk][:, :], AF.Exp, bias=bias, reads=[PS[sbk], R_bK], writes=[R_Pb[pb_]])
            B.mm(psum[ob][0:65, c0:512], V3[:, kb, :], Pb[pb_][:, c0:512], first, last,
                 reads=[R_Vb[hb], R_Pb[pb_]], writes=[PS[ob]])
            if last:
                B.cp("act", onf[0:64, :], psum[ob][0:64, :], reads=[PS[ob]], writes=[R_onf])
                s.op("dve", lambda e, o_=rden[64:65, :], i_=psum[ob][64:65, :]: e.reciprocal(o_, i_), [PS[ob]], [R_rden])
                B.mm(psum[6][0:64, :], ones_f[64:65, 0:64], rden[64:65, :], True, True, reads=[R_const, R_rden], writes=[PS[6]])
                ot = ntile % 2
                B.tt("dve", oTb[ot][0:64, :], onf[0:64, :], psum[6][0:64, :], ALU.mult, reads=[R_onf, PS[6]], writes=[R_oTb[ot]])
                B.dma(OA[h][:, lt * 512:(lt + 1) * 512], oTb[ot][0:64, :], reads=[R_oTb[ot]], writes=[R_OA], q="pool")
                ntile += 1


    B.barrier()
    A.off = stage_off
    R_p4 = B.R("p4")
    segm = A.f32(128)
    ngs = A.f32(4)
    B.dma(segm, segm_d[:, :], writes=[R_p4])
    B.dma(ngs, ng_d[:, :], writes=[R_p4])
    gi = A.f32(512)
    gf = A.f32(512)
    B.dma(gi, GI.rearrange("h (s j) -> (h s) j", j=512), reads=[R_G], writes=[R_p4])
    B.dma(gf, GF.rearrange("h (s j) -> (h s) j", j=512), reads=[R_G], writes=[R_p4])
    cbuf = [gf, A.f32(512)]
    cur = 0
    sh = 1
    while sh < 512:
        B.cp("dve", cbuf[1 - cur][:, 0:sh], cbuf[cur][:, 0:sh], reads=[R_p4], writes=[R_p4])
        B.tt("dve", cbuf[1 - cur][:, sh:512], cbuf[cur][:, sh:512], cbuf[cur][:, 0:512 - sh], ALU.add, reads=[R_p4], writes=[R_p4])
        cur = 1 - cur
        sh *= 2
    Bc = cbuf[cur]
    other = cbuf[1 - cur]
    offs = A.f32(1)
    B.mm(psum[0][:, 0:1], segm, Bc[:, 511:512], True, True, reads=[R_p4], writes=[PS[0]])
    B.cp("dve", offs, psum[0][:, 0:1], reads=[PS[0]], writes=[R_p4])
    B.ts("dve", Bc, Bc, offs[:, 0:1], None, ALU.add, reads=[R_p4], writes=[R_p4])
    gg = other
    B.tt("dve", gg, gi, Bc, ALU.subtract, reads=[R_p4], writes=[R_p4])
    gmax = A.f32(8)
    s.op("dve", lambda e: e.tensor_reduce(gmax, gg.rearrange("p (j t) -> p j t", t=64), AX.X, ALU.max), [R_p4], [R_p4])
    for j in range(1, 8):
        B.tt("dve", gmax[:, j:j + 1], gmax[:, j:j + 1], gmax[:, j - 1:j], ALU.max, reads=[R_p4], writes=[R_p4])
    s.op("pe", lambda e: e.transpose(psum[1][0:1, 0:128], gmax[:, 7:8], identf), [R_p4, R_const], [PS[1]])
    rowa = A.f32(132)
    rowb = A.f32(132)
    B.memset("dve", rowa[0:1, :], 0.0, writes=[R_p4])
    B.memset("dve", rowb[0:1, :], 0.0, writes=[R_p4])
    ra3 = rowa[0:1, :].rearrange("p (h s) -> p h s", s=33)
    rb3 = rowb[0:1, :].rearrange("p (h s) -> p h s", s=33)
    B.cp("dve", ra3[:, :, 1:33], psum[1][0:1, 0:128].rearrange("p (h s) -> p h s", s=32), reads=[PS[1]], writes=[R_p4])
    rc, rn = ra3, rb3
    sh = 1
    while sh < 33:
        B.cp("dve", rn[:, :, 0:sh], rc[:, :, 0:sh], reads=[R_p4], writes=[R_p4])
        B.tt("dve", rn[:, :, sh:33], rc[:, :, sh:33], rc[:, :, 0:33 - sh], ALU.max, reads=[R_p4], writes=[R_p4])
        rc, rn = rn, rc
        sh *= 2
    prow = A.f32(128)
    B.cp("dve", prow[0:1, :].rearrange("p (h s) -> p h s", s=32), rc[:, :, 0:32], reads=[R_p4], writes=[R_p4])
    B.mm(psum[2][:, 0:1], prow[0:1, :], ones_f[0:1, 0:1], True, True, reads=[R_p4, R_const], writes=[PS[2]])
    pm = A.f32(1)
    B.cp("dve", pm, psum[2][:, 0:1], reads=[PS[2]], writes=[R_p4])
    mu = A.f32(8)
    mup = A.f32(8)
    B.ts("dve", mu, gmax, pm[:, 0:1], None, ALU.max, reads=[R_p4], writes=[R_p4])
    B.cp("dve", mup[:, 1:8], mu[:, 0:7], reads=[R_p4], writes=[R_p4])
    B.cp("dve", mup[:, 0:1], pm, reads=[R_p4], writes=[R_p4])
    nmu = A.f32(8)
    B.ts("dve", nmu, mu, -1.0, None, ALU.mult, reads=[R_p4], writes=[R_p4])
    alph = A.f32(8)
    B.tt("dve", alph, mup, mu, ALU.subtract, reads=[R_p4], writes=[R_p4])
    B.act(alph, alph, AF.Exp, reads=[R_p4], writes=[R_p4])
    mo_ = A.f32(1)
    B.tt("dve", mo_, mu[:, 7:8], Bc[:, 511:512], ALU.add, reads=[R_p4], writes=[R_p4])
    B.dma(om[:, :], mo_, reads=[R_p4], final=True)
    for j in range(8):
        B.act(gg[:, j * 64:(j + 1) * 64], gg[:, j * 64:(j + 1) * 64], AF.Exp, bias=nmu[:, j:j + 1], reads=[R_p4], writes=[R_p4])
        B.act(gi[:, j * 64:(j + 1) * 64], Bc[:, j * 64:(j + 1) * 64], AF.Exp, bias=nmu[:, j:j + 1], scale=-1.0, reads=[R_p4], writes=[R_p4])
    aT = A.f32(1024)
    eT = A.f32(1024)
    abc = A.f32(1024)
    aT3 = aT.rearrange("p (j q) -> p j q", q=128)
    eT3 = eT.rearrange("p (j q) -> p j q", q=128)
    abc3 = abc.rearrange("p (j q) -> p j q", q=128)
    R_aT = B.R("aT")
    dg = A.f32(128)
    for (src, dst) in ((gg, aT), (gi, eT)):
        for half in range(2):
            p = 3 + half
            for jj in range(4):
                j = half * 4 + jj
                s.op("pe", lambda e, o_=psum[p][0:64, jj * 128:(jj + 1) * 128], i_=src[:, j * 64:(j + 1) * 64]: e.transpose(o_, i_, identf),
                     [R_p4, R_const], [PS[p]])
            B.cp("dve", dst[0:64, half * 512:(half + 1) * 512], psum[p][0:64, :], reads=[PS[p]], writes=[R_aT])
    for half in range(2):
        p = 5 + half
        for jj in range(4):
            j = half * 4 + jj
            B.ts("dve", dg, identf, alph[:, j:j + 1], None, ALU.mult, reads=[R_p4, R_const], writes=[R_p4])
            B.mm(psum[p][:, jj * 128:(jj + 1) * 128], ones_f, dg, True, True, reads=[R_p4, R_const], writes=[PS[p]])
        B.cp("dve", abc[:, half * 512:(half + 1) * 512], psum[p][:, :], reads=[PS[p]], writes=[R_aT])

    Cst = A.f32(4 * 129)
    Cst3 = Cst.rearrange("p (h d) -> p h d", d=129)
    R_C = [B.R("C%d" % h) for h in range(4)]
    B.memset("dve", Cst, 0.0, writes=R_C)
    kg = [A.bf16(8 * 512) for _ in range(2)]
    vg = [A.bf16(8 * 520) for _ in range(2)]
    R_kg = [B.R("kg0"), B.R("kg1")]
    R_vg = [B.R("vg0"), B.R("vg1")]
    qTg = [A.bf16(4 * 512) for _ in range(2)]
    kTg = [A.bf16(4 * 512) for _ in range(2)]
    mog = [A.bf16(4 * 512) for _ in range(2)]
    R_qTg = [B.R("qTg0"), B.R("qTg1")]
    ka = [A.bf16(128) for _ in range(4)]
    R_ka = [B.R("ka%d" % i) for i in range(4)]
    AT = [A.bf16(64) for _ in range(2)]
    R_AT = [B.R("AT0"), B.R("AT1")]
    Cab = [A.bf16(130) for _ in range(2)]
    R_Cab = [B.R("Cab0"), B.R("Cab1")]
    for i in range(2):
        B.memset("pool", Cab[i], 0.0, writes=[R_Cab[i]])
    cl = A.f32(8)
    R_cl = B.R("cl")
    hx = [A.f32(128) for _ in range(2)]
    R_hx = [B.R("hx0"), B.R("hx1")]
    bst = A.f32(4 * 6)
    bag = A.f32(4 * 2)
    R_bst = B.R("bst")
    rstd = A.f32(4)
    hnb = A.bf16(4 * 128)
    hnb3 = hnb.rearrange("p (h d) -> p h d", d=128)
    R_hnb = B.R("hnb")
    obT = [A.bf16(4 * 512) for _ in range(2)]
    R_obT = [B.R("obT0"), B.R("obT1")]
    R_OB = B.R("OB")
    MKc = MK.rearrange("b (u p) f -> (b u) p f", u=2)
    MVc = MV.rearrange("b (u p) f -> (b u) p f", u=2)
    pi = 0

    def load_group(g):
        gb = g % 2
        B.dma(kg[gb][0:64, :].rearrange("p (c f) -> p c f", f=512), MKc[g * 8:(g + 1) * 8].rearrange("c p f -> p c f"),
              reads=[R_MK], writes=[R_kg[gb]], q="sp")
        B.dma(vg[gb][0:64, :].rearrange("p (c f) -> p c f", f=520), MVc[g * 8:(g + 1) * 8].rearrange("c p f -> p c f"),
              reads=[R_MV], writes=[R_vg[gb]], q="pool")
        if g >= 24:
            go = g - 24
            B.dma(qTg[gb].rearrange("p (h n) -> p h n", n=512), MQT[:, :, go * 512:(go + 1) * 512].rearrange("h p n -> p h n"),
                  reads=[R_MQT], writes=[R_qTg[gb]], q="sp")
            B.dma(kTg[gb].rearrange("p (h n) -> p h n", n=512), MKT[:, :, go * 512:(go + 1) * 512].rearrange("h p n -> p h n"),
                  reads=[R_MKT], writes=[R_qTg[gb]], q="sp")
            for u in range(2):
                B.dma(mog[gb].rearrange("p (h n) -> p h n", n=512)[:, :, u * 256:(u + 1) * 256],
                      SG[2 * go + u].rearrange("p (c n) -> p c n", n=TN)[:, 0:4, :], reads=[R_SG], writes=[R_qTg[gb]], q="pool")

    load_group(0)
    for g in range(32):
        gb = g % 2
        if g + 1 < 32:
            load_group(g + 1)
        own = g >= 24
        k4 = kg[gb].rearrange("p (c h d) -> p c h d", h=4, d=128)
        v4 = vg[gb].rearrange("p (c h d) -> p c h d", h=4, d=130)
        q3 = qTg[gb].rearrange("p (h n) -> p h n", n=512)
        kT3 = kTg[gb].rearrange("p (h n) -> p h n", n=512)
        mo3 = mog[gb].rearrange("p (h n) -> p h n", n=512)
        ob3 = obT[gb].rearrange("p (h n) -> p h n", n=512)
        seg = g
        for j in range(8):
            tk = slice(j * 64, (j + 1) * 64)
            for h in range(4):
                col = h * 32 + seg
                a_s = aT3[0:64, j, col:col + 1]
                al = abc3[:, j, col:col + 1]
                ki = (j * 4 + h) % 4
                B.act(ka[ki][0:64, :], k4[0:64, j, h, :], AF.Copy, scale=a_s, reads=[R_kg[gb], R_aT], writes=[R_ka[ki]])
                pu = pi % 3
                pi += 1
                B.mm(psum[pu][:, 0:130], ka[ki][0:64, :], v4[0:64, j, h, :], True, True, reads=[R_ka[ki], R_vg[gb]], writes=[PS[pu]])
                if own:
                    ai = (j * 4 + h) % 2
                    B.mm(psum[3][0:64, 0:64], kT3[:, h, tk], q3[:, h, tk], True, True, reads=[R_qTg[gb]], writes=[PS[3]])
                    B.stt("dve", AT[ai][0:64, :], psum[3][0:64, 0:64], a_s, trif[0:64, 0:64], ALU.mult, ALU.mult,
                          reads=[PS[3], R_aT, R_const], writes=[R_AT[ai]])
                    B.ts("dve", Cab[ai][:, 0:129], Cst3[:, h, :], al, None, ALU.mult, reads=[R_C[h], R_aT], writes=[R_Cab[ai]])
                    ph = 4 + ai
                    B.mm(psum[ph][0:64, 0:130], AT[ai][0:64, :], v4[0:64, j, h, :], True, False,
                         reads=[R_AT[ai], R_vg[gb]], writes=[PS[ph]])
                    B.mm(psum[ph][0:64, 0:130], q3[:, h, tk], Cab[ai], False, True, reads=[R_qTg[gb], R_Cab[ai]], writes=[PS[ph]])
                    e_t = eT3[0:64, j, col:col + 1]
                    B.act(cl[0:64, h:h + 1], psum[ph][0:64, 128:129], AF.Abs, reads=[PS[ph]], writes=[R_cl])
                    B.ts("dve", cl[0:64, h:h + 1], cl[0:64, h:h + 1], e_t, None, ALU.max, reads=[R_cl, R_aT], writes=[R_cl])
                    s.op("dve", lambda e, a_=cl[0:64, h:h + 1]: e.reciprocal(a_, a_), [R_cl], [R_cl])
                    B.ts("dve", hx[ai][0:64, :], psum[ph][0:64, 0:128], cl[0:64, h:h + 1], None, ALU.mult,
                         reads=[PS[ph], R_cl], writes=[R_hx[ai]])
                    s.op("dve", lambda e, o_=bst[0:64, h * 6:(h + 1) * 6], i_=hx[ai][0:64, :]: e.bn_stats(o_, i_), [R_hx[ai]], [R_bst])
                    s.op("dve", lambda e, o_=bag[0:64, h * 2:(h + 1) * 2], i_=bst[0:64, h * 6:(h + 1) * 6]: e.bn_aggr(o_, i_), [R_bst], [R_bst])
                    B.ts("dve", rstd[0:64, h:h + 1], bag[0:64, h * 2 + 1:h * 2 + 2], EPS, None, ALU.add, reads=[R_bst], writes=[R_bst])
                    B.act(rstd[0:64, h:h + 1], rstd[0:64, h:h + 1], AF.Sqrt, reads=[R_bst], writes=[R_bst])
                    s.op("dve", lambda e, a_=rstd[0:64, h:h + 1]: e.reciprocal(a_, a_), [R_bst], [R_bst])
                    B.ts("dve", hnb3[0:64, h, :], hx[ai][0:64, :], bag[0:64, h * 2:h * 2 + 1], rstd[0:64, h:h + 1], ALU.subtract, ALU.mult,
                         reads=[R_hx[ai], R_bst], writes=[R_hnb])
                B.stt("dve", Cst3[:, h, :], Cst3[:, h, :], al, psum[pu][:, 0:129], ALU.mult, ALU.add,
                      reads=[R_C[h], R_aT, PS[pu]], writes=[R_C[h]])
            if own:
                pt = 6 + j % 2
                ptb = psum[pt][:, :].bitcast(BF16)
                for h in range(4):
                    s.op("pe", lambda e, o_=ptb[:, h * 64:(h + 1) * 64], i_=hnb3[0:64, h, :]: e.transpose(o_, i_, identb[0:64, 0:64]),
                         [R_hnb, R_const], [PS[pt]])
                for h in range(4):
                    B.stt("dve", ob3[:, h, tk], ptb[:, h * 64:(h + 1) * 64], ngs[:, h:h + 1], mo3[:, h, tk], ALU.mult, ALU.mult,
                          reads=[PS[pt], R_p4, R_qTg[gb]], writes=[R_obT[gb]])
        if own:
            go = g - 24
            B.dma(OB[:, :, go * 512:(go + 1) * 512].rearrange("h p n -> p h n"), ob3, reads=[R_obT[gb]], writes=[R_OB], q="sp")
    B.dma(oC[:, :], Cst, reads=R_C, final=True)


    B.barrier()
    R_s4 = B.R("s4")
    Kc = A.bf16(2064)
    Kst = A.f32(2048)
    Vst = A.f32(16 * 64)
    Vc = A.bf16(17 * 65)
    Vc3 = Vc.rearrange("p (b d) -> p b d", d=65)
    Qa = A.bf16(16)
    R_Kc = B.R("Kc"); R_Kst = B.R("Kst"); R_Vst = B.R("Vst"); R_Vc = B.R("Vc"); R_Qa = B.R("Qa")
    B.memset("pool", Kc[64:65, :], 1.0, writes=[R_Kc])
    B.memset("pool", Vc, 1.0, writes=[R_Vc])
    LFs = A.f32(17 * 8)
    LFs3 = LFs.rearrange("p (b h) -> p b h", h=8)
    Fs = A.f32(17 * 8)
    Fs3 = Fs.rearrange("p (b h) -> p b h", h=8)
    Ts = [A.f32(17 * 8) for _ in range(2)]
    bKs = A.f32(8 * 17)
    bKs3 = bKs.rearrange("p (h b) -> p h b", b=17)
    fqs = A.bf16(16)
    Pbs = [A.bf16(16) for _ in range(2)]
    R_Pbs = [B.R("Pbs0"), B.R("Pbs1")]
    dts = A.f32(16)
    onfs = A.f32(16)
    rdens = A.f32(16)
    oTs = A.bf16(16)
    R_OAs = B.R("OAs"); R_OBs = B.R("OBs"); R_QFs = B.R("QFs")
    for e in range(2):
        B.memset("dve", LFs, 0.0, writes=[R_s4])
        B.dma(LFs[:, 0:128], lfc_d[e], writes=[R_s4])
        B.cp("dve", LFs3[0:16, 16, :], lfn[0:16, e * 8:(e + 1) * 8], reads=[R_smp, R_s4], writes=[R_s4])
        B.mm(psum[0][:, 0:136], trif, LFs, True, True, reads=[R_const, R_s4], writes=[PS[0]])
        B.mm(psum[1][:, 0:136], ones_f, LFs, True, True, reads=[R_const, R_s4], writes=[PS[1]])
        B.cp("dve", Fs, psum[0][:, 0:136], reads=[PS[0]], writes=[R_s4])
        B.cp("dve", Ts[0], psum[1][:, 0:136], reads=[PS[1]], writes=[R_s4])
        B.tt("dve", Fs, Fs, Ts[0], ALU.subtract, reads=[R_s4], writes=[R_s4])
        cur = 0
        sh = 1
        while sh < 17:
            a3 = Ts[cur].rearrange("p (b h) -> p b h", h=8)
            n3 = Ts[1 - cur].rearrange("p (b h) -> p b h", h=8)
            B.cp("dve", n3[:, 0:sh, :], a3[:, 0:sh, :], reads=[R_s4], writes=[R_s4])
            B.tt("dve", n3[:, sh:17, :], a3[:, sh:17, :], a3[:, 0:17 - sh, :], ALU.add, reads=[R_s4], writes=[R_s4])
            cur = 1 - cur
            sh *= 2
        Ti3 = Ts[cur].rearrange("p (b h) -> p b h", h=8)
        B.tt("dve", Fs, Fs, Ts[cur], ALU.add, reads=[R_s4], writes=[R_s4])
        for h in range(8):
            B.ts("dve", Fs3[:, :, h], Fs3[:, :, h], Ti3[:, 15, h:h + 1], None, ALU.subtract, reads=[R_s4], writes=[R_s4])
            B.ts("dve", bKs3[:, h, :], Fs3[:, :, h], -1.0, None, ALU.mult, reads=[R_s4], writes=[R_s4])
        s.op("pe", lambda e_, o_=psum[2][0:8, 0:16], i_=Fs3[0:16, 16, :]: e_.transpose(o_, i_, identf[0:16, 0:16]), [R_s4, R_const], [PS[2]])
        B.cp("act", fqs[0:8, :], psum[2][0:8, 0:16], reads=[PS[2]], writes=[R_s4])
        B.dma(QFs[e], fqs[0:8, :], reads=[R_s4], writes=[R_QFs])
        for h in range(8):
            B.dma(Kst[0:64, :], kcT_d[e, h], writes=[R_Kst], q="sp")
            B.dma(Vst.rearrange("p (b d) -> p b d", d=64), vc_d[e, h], writes=[R_Vst], q="pool")
            B.cp("dve", Kc[0:64, 0:2048], Kst[0:64, :], reads=[R_Kst], writes=[R_Kc])
            B.dma(Kc[0:64, 2048:2064], KTs[h][:, e * 16:(e + 1) * 16], reads=[R_KTs], writes=[R_Kc], q="sp")
            B.cp("act", Vc3[:, 0:16, 0:64], Vst.rearrange("p (b d) -> p b d", d=64), reads=[R_Vst], writes=[R_Vc])
            B.dma(Vc3[0:16, 16, :], VAs[e][:, h * 65:(h + 1) * 65], reads=[R_VAs], writes=[R_Vc], q="pool")
            B.dma(Qa[0:64, :], QTs[h][:, e * 16:(e + 1) * 16], reads=[R_QTs], writes=[R_Qa], q="sp")
            B.dma(Qa[64:65, :], QFs[e][h:h + 1, :], reads=[R_QFs], writes=[R_Qa], q="sp")
            for kb in range(17):
                sbk = kb % 4
                pb_ = kb % 2
                if kb < 16:
                    B.mm(psum[sbk][:, 0:16], Kc[0:65, kb * 128:(kb + 1) * 128], Qa[0:65, :], True, True, reads=[R_Kc, R_Qa], writes=[PS[sbk]])
                    B.act(Pbs[pb_][:, :], psum[sbk][:, 0:16], AF.Exp, bias=bKs3[:, h, kb:kb + 1], reads=[PS[sbk], R_s4], writes=[R_Pbs[pb_]])
                    B.mm(psum[4][0:65, 0:16], Vc3[:, kb, :], Pbs[pb_][:, :], kb == 0, False, reads=[R_Vc, R_Pbs[pb_]], writes=[PS[4]])
                else:
                    B.mm(psum[sbk][0:16, 0:16], Kc[0:65, 2048:2064], Qa[0:65, :], True, True, reads=[R_Kc, R_Qa], writes=[PS[sbk]])
                    B.tt("dve", dts[0:16, :], psum[sbk][0:16, 0:16], cmask[0:16, 0:16], ALU.add, reads=[PS[sbk], R_const], writes=[R_s4])
                    B.act(Pbs[pb_][0:16, :], dts[0:16, :], AF.Exp, bias=bKs3[0:16, h, 16:17], reads=[R_s4], writes=[R_Pbs[pb_]])
                    B.mm(psum[4][0:65, 0:16], Vc3[0:16, 16, :], Pbs[pb_][0:16, :], False, True, reads=[R_Vc, R_Pbs[pb_]], writes=[PS[4]])
            B.cp("act", onfs[0:64, :], psum[4][0:64, 0:16], reads=[PS[4]], writes=[R_s4])
            s.op("dve", lambda e_, o_=rdens[64:65, :], i_=psum[4][64:65, 0:16]: e_.reciprocal(o_, i_), [PS[4]], [R_s4])
            B.mm(psum[5][0:64, 0:16], ones_f[64:65, 0:64], rdens[64:65, :], True, True, reads=[R_const, R_s4], writes=[PS[5]])
            B.tt("dve", oTs[0:64, :], onfs[0:64, :], psum[5][0:64, 0:16], ALU.mult, reads=[R_s4, PS[5]], writes=[R_s4])
            B.dma(OAs[h][:, e * 16:(e + 1) * 16], oTs[0:64, :], reads=[R_s4], writes=[R_OAs], q="pool")
    Cs = A.f32(4 * 129)
    Cs3 = Cs.rearrange("p (h d) -> p h d", d=129)
    m0t = A.f32(1)
    bcs = [A.f32(16), A.f32(16)]
    ggs = A.f32(16)
    ees = A.f32(16)
    gmx = A.f32(1)
    mus = A.f32(1)
    nmus = A.f32(1)
    als = A.f32(1)
    mos = A.f32(1)
    aTs = A.f32(4)
    eTs = A.f32(4)
    abcs = A.f32(4)
    dgs = A.f32(4)
    kts = A.bf16(512)
    kts3 = kts.rearrange("p (h d) -> p h d", d=128)
    vts = A.bf16(520)
    vts3 = vts.rearrange("p (h d) -> p h d", d=130)
    qTs_ = A.bf16(4 * 16)
    kTs_ = A.bf16(4 * 16)
    mos_ = A.bf16(4 * 16)
    qTs3 = qTs_.rearrange("p (h n) -> p h n", n=16)
    kTs3 = kTs_.rearrange("p (h n) -> p h n", n=16)
    mos3 = mos_.rearrange("p (h n) -> p h n", n=16)
    obs = A.bf16(4 * 16)
    obs3 = obs.rearrange("p (h n) -> p h n", n=16)
    R_m4 = B.R("m4s")
    for e in range(2):
        ec = slice(e * 16, (e + 1) * 16)
        B.dma(Cs, Cs0_d[e], writes=[R_m4])
        B.dma(m0t[0:4, :], m0_d[e], writes=[R_m4])
        B.dma(kts[0:16, :], MKs[e], reads=[R_MKs], writes=[R_m4])
        B.dma(vts[0:16, :], MVs[e], reads=[R_MVs], writes=[R_m4])
        B.dma(qTs3, MQTs[:, :, ec].rearrange("h p n -> p h n"), reads=[R_MQTs], writes=[R_m4])
        B.dma(kTs3, MKTs[:, :, ec].rearrange("h p n -> p h n"), reads=[R_MQTs], writes=[R_m4])
        B.dma(mos3, SGs.rearrange("p (c n) -> p c n", n=32)[:, 0:4, ec], reads=[R_SGs], writes=[R_m4])
        B.cp("dve", bcs[0][0:4, :], gfs[0:4, ec], reads=[R_smp], writes=[R_m4])
        cur = 0
        sh = 1
        while sh < 16:
            B.cp("dve", bcs[1 - cur][0:4, 0:sh], bcs[cur][0:4, 0:sh], reads=[R_m4], writes=[R_m4])
            B.tt("dve", bcs[1 - cur][0:4, sh:16], bcs[cur][0:4, sh:16], bcs[cur][0:4, 0:16 - sh], ALU.add, reads=[R_m4], writes=[R_m4])
            cur = 1 - cur
            sh *= 2
        bb = bcs[cur]
        B.tt("dve", ggs[0:4, :], gis[0:4, ec], bb[0:4, :], ALU.subtract, reads=[R_smp, R_m4], writes=[R_m4])
        s.op("dve", lambda e_: e_.tensor_reduce(gmx[0:4, :], ggs[0:4, :], AX.X, ALU.max), [R_m4], [R_m4])
        B.tt("dve", mus[0:4, :], gmx[0:4, :], m0t[0:4, :], ALU.max, reads=[R_m4], writes=[R_m4])
        B.ts("dve", nmus[0:4, :], mus[0:4, :], -1.0, None, ALU.mult, reads=[R_m4], writes=[R_m4])
        B.tt("dve", als[0:4, :], m0t[0:4, :], mus[0:4, :], ALU.subtract, reads=[R_m4], writes=[R_m4])
        B.act(als[0:4, :], als[0:4, :], AF.Exp, reads=[R_m4], writes=[R_m4])
        B.tt("dve", mos[0:4, :], mus[0:4, :], bb[0:4, 15:16], ALU.add, reads=[R_m4], writes=[R_m4])
        B.dma(oms[e], mos[0:4, :], reads=[R_m4], final=True)
        B.act(ggs[0:4, :], ggs[0:4, :], AF.Exp, bias=nmus[0:4, 0:1], reads=[R_m4], writes=[R_m4])
        B.act(ees[0:4, :], bb[0:4, :], AF.Exp, bias=nmus[0:4, 0:1], scale=-1.0, reads=[R_m4], writes=[R_m4])
        s.op("pe", lambda e_: e_.transpose(psum[0][0:16, 0:4], ggs[0:4, :], identf[0:4, 0:4]), [R_m4, R_const], [PS[0]])
        s.op("pe", lambda e_: e_.transpose(psum[1][0:16, 0:4], ees[0:4, :], identf[0:4, 0:4]), [R_m4, R_const], [PS[1]])
        B.cp("dve", aTs[0:16, :], psum[0][0:16, 0:4], reads=[PS[0]], writes=[R_m4])
        B.cp("dve", eTs[0:16, :], psum[1][0:16, 0:4], reads=[PS[1]], writes=[R_m4])
        B.ts("dve", dgs[0:4, :], identf[0:4, 0:4], als[0:4, 0:1], None, ALU.mult, reads=[R_m4, R_const], writes=[R_m4])
        B.mm(psum[2][:, 0:4], ones_f[0:4, :], dgs[0:4, :], True, True, reads=[R_m4, R_const], writes=[PS[2]])
        B.cp("dve", abcs, psum[2][:, 0:4], reads=[PS[2]], writes=[R_m4])
        for h in range(4):
            a_s = aTs[0:16, h:h + 1]
            al = abcs[:, h:h + 1]
            B.act(ka[0][0:16, :], kts3[0:16, h, :], AF.Copy, scale=a_s, reads=[R_m4], writes=[R_ka[0]])
            B.mm(psum[3][:, 0:130], ka[0][0:16, :], vts3[0:16, h, :], True, True, reads=[R_ka[0], R_m4], writes=[PS[3]])
            B.mm(psum[4][0:16, 0:16], kTs3[:, h, :], qTs3[:, h, :], True, True, reads=[R_m4], writes=[PS[4]])
            B.stt("dve", AT[0][0:16, 0:16], psum[4][0:16, 0:16], a_s, trif[0:16, 0:16], ALU.mult, ALU.mult,
                  reads=[PS[4], R_m4, R_const], writes=[R_AT[0]])
            B.ts("dve", Cab[0][:, 0:129], Cs3[:, h, :], al, None, ALU.mult, reads=[R_m4], writes=[R_Cab[0]])
            B.mm(psum[5][0:16, 0:130], AT[0][0:16, 0:16], vts3[0:16, h, :], True, False, reads=[R_AT[0], R_m4], writes=[PS[5]])
            B.mm(psum[5][0:16, 0:130], qTs3[:, h, :], Cab[0], False, True, reads=[R_m4, R_Cab[0]], writes=[PS[5]])
            B.act(cl[0:16, h:h + 1], psum[5][0:16, 128:129], AF.Abs, reads=[PS[5]], writes=[R_cl])
            B.ts("dve", cl[0:16, h:h + 1], cl[0:16, h:h + 1], eTs[0:16, h:h + 1], None, ALU.max, reads=[R_cl, R_m4], writes=[R_cl])
            s.op("dve", lambda e_, a_=cl[0:16, h:h + 1]: e_.reciprocal(a_, a_), [R_cl], [R_cl])
            B.ts("dve", hx[0][0:16, :], psum[5][0:16, 0:128], cl[0:16, h:h + 1], None, ALU.mult, reads=[PS[5], R_cl], writes=[R_hx[0]])
            s.op("dve", lambda e_, o_=bst[0:16, h * 6:(h + 1) * 6], i_=hx[0][0:16, :]: e_.bn_stats(o_, i_), [R_hx[0]], [R_bst])
            s.op("dve", lambda e_, o_=bag[0:16, h * 2:(h + 1) * 2], i_=bst[0:16, h * 6:(h + 1) * 6]: e_.bn_aggr(o_, i_), [R_bst], [R_bst])
            B.ts("dve", rstd[0:16, h:h + 1], bag[0:16, h * 2 + 1:h * 2 + 2], EPS, None, ALU.add, reads=[R_bst], writes=[R_bst])
            B.act(rstd[0:16, h:h + 1], rstd[0:16, h:h + 1], AF.Sqrt, reads=[R_bst], writes=[R_bst])
            s.op("dve", lambda e_, a_=rstd[0:16, h:h + 1]: e_.reciprocal(a_, a_), [R_bst], [R_bst])
            B.ts("dve", hnb3[0:16, h, :], hx[0][0:16, :], bag[0:16, h * 2:h * 2 + 1], rstd[0:16, h:h + 1], ALU.subtract, ALU.mult,
                 reads=[R_hx[0], R_bst], writes=[R_hnb])
            B.stt("dve", Cs3[:, h, :], Cs3[:, h, :], al, psum[3][:, 0:129], ALU.mult, ALU.add, reads=[R_m4, PS[3]], writes=[R_m4])
        ptb = psum[6][:, :].bitcast(BF16)
        for h in range(4):
            s.op("pe", lambda e_, o_=ptb[:, h * 16:(h + 1) * 16], i_=hnb3[0:16, h, :]: e_.transpose(o_, i_, identb[0:16, 0:16]),
                 [R_hnb, R_const], [PS[6]])
        for h in range(4):
            B.stt("dve", obs3[:, h, :], ptb[:, h * 16:(h + 1) * 16], ngs[:, h:h + 1], mos3[:, h, :], ALU.mult, ALU.mult,
                  reads=[PS[6], R_p4, R_m4], writes=[R_m4])
        B.dma(OBs[:, :, ec].rearrange("h p n -> p h n"), obs3, reads=[R_m4], writes=[R_OBs], q="sp")
        B.dma(oCs[e], Cs, reads=[R_m4], final=True)

    B.barrier()
    A.off = stage_off
    wab = A.bf16(4 * D)
    wbb = A.bf16(4 * D)
    wob = A.bf16(8 * D)
    wab3 = wab.rearrange("p (k n) -> p k n", n=D)
    wbb3 = wbb.rearrange("p (k n) -> p k n", n=D)
    wob3 = wob.rearrange("p (k n) -> p k n", n=D)
    R_w5 = B.R("w5")
    B.dma(wab3, w_ba, writes=[R_w5], q="pool")
    B.dma(wbb3, w_bb, writes=[R_w5], q="pool")
    for k0 in range(0, 8, 2):
        B.dma(wob3[:, k0:k0 + 2, :], w_o[:, k0:k0 + 2, :], writes=[R_w5], q="pool")
    x5 = [A.f32(8 * TN) for _ in range(2)]
    R_x5 = [B.R("x5_0"), B.R("x5_1")]
    oa5 = [A.bf16(4 * TN) for _ in range(2)]
    ob5 = [A.bf16(4 * TN) for _ in range(2)]
    sg5 = [A.bf16(20 * TN) for _ in range(2)]
    R_in5 = [B.R("in5_0"), B.R("in5_1")]
    m5 = A.bf16(8 * TN)
    m53 = m5.rearrange("p (k n) -> p k n", n=TN)
    R_m5 = B.R("m5")
    t1 = [A.f32(TN) for _ in range(2)]
    R_t1 = [B.R("t1_0"), B.R("t1_1")]
    sq5 = A.bf16(8 * TN)
    rb5 = A.bf16(8 * TN)
    R_sq5 = B.R("sq5")
    st1 = A.f32(TN)
    st2 = A.f32(TN)
    st3 = A.f32(TN)
    R_st5 = B.R("st5")
    R_X2 = B.R("X2")
    tiles5 = [(to, TN, 0) for to in range(16)] + [(16, 32, [(0, 16, 1), (16, 32, 2)])]

    def load5(ti):
        to, N, cond = tiles5[ti]
        b = ti % 2
        oa3 = oa5[b].rearrange("p (c n) -> p c n", n=TN)
        if to == 16:
            B.dma(x5[b].rearrange("p (k n) -> p k n", n=TN)[:, :, 0:32], X1[NT][:, :, 0:32], reads=[R_X1], writes=[R_x5[b]], q="sp")
            for hh in range(2):
                B.dma(oa3[hh * 64:(hh + 1) * 64, :, 0:32], OAs[hh::2].rearrange("c d n -> d c n"), reads=[R_OAs], writes=[R_in5[b]], q="pool")
            B.dma(ob5[b].rearrange("p (c n) -> p c n", n=TN)[:, :, 0:32], OBs.rearrange("h p n -> p h n"), reads=[R_OBs], writes=[R_in5[b]], q="pool")
            B.dma(sg5[b].rearrange("p (c n) -> p c n", n=TN)[:, :, 0:32], SGs.rearrange("p (c n) -> p c n", n=32), reads=[R_SGs], writes=[R_in5[b]], q="sp")
            return
        B.dma(x5[b].rearrange("p (k n) -> p k n", n=TN), X1[OWN0 + to], reads=[R_X1], writes=[R_x5[b]], q="sp")
        for hh in range(2):
            B.dma(oa3[hh * 64:(hh + 1) * 64, :, :], OA[hh::2][:, :, to * TN:(to + 1) * TN].rearrange("c d n -> d c n"),
                  reads=[R_OA], writes=[R_in5[b]], q="pool")
        B.dma(ob5[b].rearrange("p (c n) -> p c n", n=TN), OB[:, :, to * TN:(to + 1) * TN].rearrange("h p n -> p h n"),
              reads=[R_OB], writes=[R_in5[b]], q="pool")
        B.dma(sg5[b], SG[to], reads=[R_SG], writes=[R_in5[b]], q="sp")

    load5(0)
    pi = 0
    for ti, (to, N, cond) in enumerate(tiles5):
        b = ti % 2
        if ti + 1 < len(tiles5):
            load5(ti + 1)
        x3 = x5[b].rearrange("p (k n) -> p k n", n=TN)
        oa3 = oa5[b].rearrange("p (c n) -> p c n", n=TN)
        ob3 = ob5[b].rearrange("p (c n) -> p c n", n=TN)
        sg3 = sg5[b].rearrange("p (c n) -> p c n", n=TN)
        q3 = sq5.rearrange("p (k n) -> p k n", n=TN)
        rb3 = rb5.rearrange("p (k n) -> p k n", n=TN)
        groups = [(0, N, cond)] if isinstance(cond, int) else cond
        B.act(x3[:, :, 0:N], x3[:, :, 0:N], AF.Copy, scale=ALPHA, reads=[R_x5[b]], writes=[R_x5[b]])
        for d in range(8):
            pa = pi % 8
            pb = (pi + 1) % 8
            pi += 2
            for c in range(4):
                B.mm(psum[pa][:, 0:N], wab3[:, c, d * 128:(d + 1) * 128], oa3[:, c, 0:N], c == 0, c == 3,
                     reads=[R_w5, R_in5[b]], writes=[PS[pa]])
            for c in range(4):
                B.mm(psum[pb][:, 0:N], wbb3[:, c, d * 128:(d + 1) * 128], ob3[:, c, 0:N], c == 0, c == 3,
                     reads=[R_w5, R_in5[b]], writes=[PS[pb]])
            ti_ = d % 2
            B.tt("dve", t1[ti_][:, 0:N], psum[pa][:, 0:N], sg3[:, 4 + d, 0:N], ALU.mult, reads=[PS[pa], R_in5[b]], writes=[R_t1[ti_]])
            B.tt("dve", t1[1 - ti_][:, 0:N], psum[pb][:, 0:N], sg3[:, 12 + d, 0:N], ALU.mult, reads=[PS[pb], R_in5[b]], writes=[R_t1[1 - ti_]])
            B.tt("dve", m53[:, d, 0:N], t1[0][:, 0:N], t1[1][:, 0:N], ALU.add, reads=[R_t1[0], R_t1[1]], writes=[R_m5])
        for d in range(8):
            pd = pi % 8
            pi += 1
            for c in range(8):
                B.mm(psum[pd][:, 0:N], wob3[:, c, d * 128:(d + 1) * 128], m53[:, c, 0:N], c == 0, c == 7,
                     reads=[R_w5, R_m5], writes=[PS[pd]])
            for (c0, c1, ci) in groups:
                B.stt("dve", x3[:, d, c0:c1], psum[pd][:, c0:c1], modv3[:, 40 + d, ci:ci + 1], x3[:, d, c0:c1],
                      ALU.mult, ALU.add, reads=[PS[pd], R_mod], writes=[R_x5[b]])
            B.act(q3[:, d, 0:N], x3[:, d, 0:N], AF.Square, reads=[R_x5[b]], writes=[R_sq5])
            B.act(rb3[:, d, 0:N], x3[:, d, 0:N], AF.Copy, reads=[R_x5[b]], writes=[R_sq5])
        p1 = pi % 8
        p2 = (pi + 1) % 8
        pi += 2
        for d in range(8):
            B.mm(psum[p1][:, 0:N], ones_b, rb3[:, d, 0:N], d == 0, d == 7, reads=[R_sq5, R_const], writes=[PS[p1]])
        for d in range(8):
            B.mm(psum[p2][:, 0:N], ones_b, q3[:, d, 0:N], d == 0, d == 7, reads=[R_sq5, R_const], writes=[PS[p2]])
        B.ts("dve", st1[:, 0:N], psum[p1][:, 0:N], -1.0 / D, None, ALU.mult, reads=[PS[p1]], writes=[R_st5])
        B.tt("dve", st2[:, 0:N], st1[:, 0:N], st1[:, 0:N], ALU.mult, reads=[R_st5], writes=[R_st5])
        B.stt("dve", st3[:, 0:N], psum[p2][:, 0:N], 1.0 / D, st2[:, 0:N], ALU.mult, ALU.subtract, reads=[PS[p2], R_st5], writes=[R_st5])
        B.ts("dve", st3[:, 0:N], st3[:, 0:N], EPS, None, ALU.add, reads=[R_st5], writes=[R_st5])
        B.act(st3[:, 0:N], st3[:, 0:N], AF.Sqrt, reads=[R_st5], writes=[R_st5])
        s.op("dve", lambda e, a=st3[:, 0:N]: e.reciprocal(a, a), [R_st5], [R_st5])
        for d in range(8):
            B.tt("dve", x3[:, d, 0:N], x3[:, d, 0:N], st1[:, 0:N], ALU.add, reads=[R_x5[b], R_st5], writes=[R_x5[b]])
            B.tt("dve", x3[:, d, 0:N], x3[:, d, 0:N], st3[:, 0:N], ALU.mult, reads=[R_x5[b], R_st5], writes=[R_x5[b]])
            B.act(x3[:, d, 0:N], x3[:, d, 0:N], AF.Identity, bias=lnb3[:, 1, d:d + 1], scale=lng3[:, 1, d:d + 1],
                  reads=[R_x5[b], R_const], writes=[R_x5[b]])
        B.dma(X2[to][:, :, 0:N], x3[:, :, 0:N], reads=[R_x5[b]], writes=[R_X2], q="pool")

    def load2(tid, dst, res):
        Nn = TN if tid < 16 else 32
        B.dma(dst, X2[tid][:, :, 0:Nn], reads=[R_X2], writes=[res])

    def store2(tid, src, res):
        if tid < 16:
            B.dma(yT[:, :, tid * TN:(tid + 1) * TN], src, reads=[res], final=True, q="pool")
        else:
            B.dma(ysT[:, :, :], src, reads=[res], final=True, q="pool")

    tiles2 = [(t, TN, 0) for t in range(16)] + [(16, 32, [(0, 16, 1), (16, 32, 2)])]
    ffn_stage(1, w_gu[1], w_dn[1], tiles2, 48, 56, 64, 2, load2, store2, "f2")

    s.emit(B.out_dmas)
    es.close()
    return nc


def _fm(a):
    F, N = a.shape
    return np.ascontiguousarray(a.reshape(F // 128, 128, N).transpose(1, 0, 2))


def kernel(**inp):
    f32 = np.float32
    g = {k: np.asarray(v) for k, v in inp.items()}
    nc = build()
    in_maps = []
    w_ada = _fm(g["w_ada"][0])
    b_ada = np.ascontiguousarray(g["b_ada"][0].reshape(72, 128).T)
    wgu = [_fm(g["ffn1_w_gu"][0]), _fm(g["ffn2_w_gu"][0])]
    wdn = [_fm(g["ffn1_w_down"][0]), _fm(g["ffn2_w_down"][0])]
    w_in = _fm(g["w_in"][0])
    b_in = g["b_in"][0]
    fmc = ([512 + 128 * c for c in range(4)] + [2056 + 128 * c for c in range(4)] + [128 * c for c in range(4)] +
           [1544 + 128 * c for c in range(4)] + [3088 + 128 * c for c in range(4)] + [3600 + 128 * c for c in range(8)] +
           [4624 + 128 * c for c in range(8)])
    b_in_fm = np.zeros((128, 40), f32)
    for j, c0 in enumerate(fmc):
        b_in_fm[:, j] = b_in[c0:c0 + 128]
    b_in_fm[0:4, 36] = b_in[3080:3084]
    b_in_fm[0:4, 37] = b_in[3084:3088]
    tm = np.concatenate([b_in[1024:1544], b_in[2568:3080]])
    b_in_bc = np.ascontiguousarray(np.broadcast_to(tm[None, :], (128, 1032)))
    ident = np.eye(128, dtype=f32)
    w_ba = _fm(g["w_branch_a"][0])
    w_bb = _fm(g["w_branch_b"][0])
    w_o = _fm(g["w_out"][0])
    tri = np.triu(np.ones((128, 128), f32))
    pp = np.arange(128)
    segm = ((pp[:, None] // 32 == pp[None, :] // 32) & (pp[:, None] < pp[None, :])).astype(f32)
    ng = np.ascontiguousarray(g["mlstm_norm_g"][0].reshape(4, 128).T)
    cmask = np.where(np.arange(128)[:, None] <= np.arange(128)[None, :], 0.0, -BIG).astype(f32)
    ln_g = np.ascontiguousarray(g["ln_g"][0].reshape(3, 8, 128).transpose(2, 0, 1))
    ln_b = np.ascontiguousarray(g["ln_b"][0].reshape(3, 8, 128).transpose(2, 0, 1))
    conv_w = np.ascontiguousarray(g["conv_w"][0].reshape(4, 8, 128).transpose(2, 1, 0))
    conv_b = np.ascontiguousarray(g["conv_b"][0].reshape(8, 128).T)
    ck = g["cache_fox_k"][0]; cvv = g["cache_fox_v"][0]; clf = g["cache_fox_logf"][0]
    sC = g["state_mlstm_C"][0]; sn = g["state_mlstm_n"][0]; sm = g["state_mlstm_m"][0]; scv = g["state_conv"][0]
    for core in range(8):
        b, q = core // 4, core % 4
        es_ = [2 * core, 2 * core + 1]
        convs = np.ascontiguousarray(np.stack([scv[e].reshape(3, 8, 128) for e in es_], 0).transpose(3, 2, 0, 1))
        kcT = np.ascontiguousarray(np.stack([ck[e].transpose(1, 2, 0) for e in es_], 0))
        vc = np.ascontiguousarray(np.stack([cvv[e].reshape(16, 128, 8, 64).transpose(2, 1, 0, 3) for e in es_], 0))
        lfc = np.ascontiguousarray(np.stack([clf[e].reshape(16, 128, 8).transpose(1, 0, 2).reshape(128, 128) for e in es_], 0))
        Cs0 = np.zeros((2, 128, 4, 129), f32)
        for i_, e in enumerate(es_):
            Cs0[i_, :, :, 0:128] = sC[e].transpose(2, 0, 1)
            Cs0[i_, :, :, 128] = sn[e].T
        Cs0 = Cs0.reshape(2, 128, 516)
        m0c = np.ascontiguousarray(np.stack([sm[e].reshape(4, 1) for e in es_], 0))
        nreal = 4096 * (q + 1)
        xt = np.zeros((D, S), f32)
        xt[:, S - nreal:] = g["x_prompt"][b, :nreal, :].T
        keep = np.zeros((128, NT), f32)
        keep[:, NT - 16 * (q + 1):] = 1.0
        xs = g["x_sample"][2 * core:2 * core + 2].reshape(32, D).T
        c3 = np.stack([g["c_prompt"][b], g["c_sample"][2 * core], g["c_sample"][2 * core + 1]], axis=1)
        in_maps.append({
            "xT": _fm(xt), "xsT": _fm(np.ascontiguousarray(xs)), "keep": keep, "cT": _fm(np.ascontiguousarray(c3)),
            "w_ada": w_ada, "b_ada": b_ada, "w_gu1": wgu[0], "w_gu2": wgu[1], "w_dn1": wdn[0], "w_dn2": wdn[1],
            "w_in": w_in, "b_in_fm": b_in_fm, "b_in_bc": b_in_bc, "ln_g": ln_g, "ln_b": ln_b,
            "conv_w": conv_w, "conv_b": conv_b, "ident": ident, "tri": tri, "cmask": cmask, "segm": segm, "ng": ng, "w_ba": w_ba, "w_bb": w_bb, "w_o": w_o, "convs": convs, "kcT": kcT, "vc": vc, "lfc": lfc, "Cs0": Cs0, "m0c": m0c,
        })
    res = run_bass_kernel_spmd(nc, in_maps, core_ids=list(range(8)))
    R = res.results
    Bn, DB, L = 2, 16, 16
    y_p = np.zeros((Bn, S, D), f32)
    for core in range(8):
        b, q = core // 4, core % 4
        yt = R[core]["yT"]
        y_p[b, q * 4096:(q + 1) * 4096, :] = yt.transpose(2, 1, 0).reshape(4096, D)
    fk = np.zeros((1, Bn, S, 8, 64), f32)
    fv = np.zeros((1, Bn, S, 8, 64), f32)
    fl = np.zeros((1, Bn, S, 8), f32)
    cv = np.zeros((1, Bn, 3, 1024), f32)
    for core in range(8):
        b, q = core // 4, core % 4
        sl = slice(q * 4096, (q + 1) * 4096)
        fk[0, b, sl] = R[core]["okT"].transpose(2, 1, 0).reshape(4096, 8, 64)
        fv[0, b, sl] = R[core]["ov"].reshape(4096, 8, 64)
        fl[0, b, sl] = R[core]["olf"]
        if q == 3:
            cv[0, b] = R[core]["oconv"].transpose(2, 1, 0).reshape(3, 1024)
    mC = np.zeros((1, Bn, 4, 128, 128), f32)
    mn = np.zeros((1, Bn, 4, 128), f32)
    mm_ = np.zeros((1, Bn, 4), f32)
    for b in range(Bn):
        c = R[4 * b + 3]["oC"].reshape(128, 4, 129)
        mC[0, b] = c[:, :, 0:128].transpose(1, 2, 0)
        mn[0, b] = c[:, :, 128].T
        mm_[0, b] = R[4 * b + 3]["om"][31::32, 0]
    y_s = np.zeros((DB, L, D), f32)
    sk = np.zeros((1, DB, L, 8, 64), f32); sv = np.zeros((1, DB, L, 8, 64), f32); sl_ = np.zeros((1, DB, L, 8), f32)
    sCo = np.zeros((1, DB, 4, 128, 128), f32); sno = np.zeros((1, DB, 4, 128), f32); smo = np.zeros((1, DB, 4), f32)
    sco = np.zeros((1, DB, 3, 1024), f32)
    for core in range(8):
        r = R[core]
        ys = r["ysT"].transpose(2, 1, 0).reshape(32, D)
        ks = r["oksT"].transpose(2, 1, 0).reshape(32, 8, 64)
        for i_ in range(2):
            e = 2 * core + i_
            y_s[e] = ys[i_ * 16:(i_ + 1) * 16]
            sk[0, e] = ks[i_ * 16:(i_ + 1) * 16]
            sv[0, e] = r["ovs"][i_].reshape(16, 8, 64)
            sl_[0, e] = r["olfs"][i_]
            c = r["oCs"][i_].reshape(128, 4, 129)
            sCo[0, e] = c[:, :, 0:128].transpose(1, 2, 0)
            sno[0, e] = c[:, :, 128].T
            smo[0, e] = r["oms"][i_][:, 0]
            sco[0, e] = r["ocs"][:, :, i_, :].transpose(2, 1, 0).reshape(3, 1024)
    outs = [y_p, y_s,
            fk, fv, fl,
            mC, mn, mm_,
            cv,
            sk, sv, sl_, sCo, sno, smo, sco]
    return tuple(outs)
```

```python
import numpy as np
import concourse.bass as bass
import concourse.mybir as mybir
from concourse.bass_utils import run_bass_kernel_spmd

F32 = mybir.dt.float32
BF16 = mybir.dt.bfloat16
ALU = mybir.AluOpType
AF = mybir.ActivationFunctionType
AX = mybir.AxisListType

D = 1024
S = 16384
NT = 64
TN = 256
OWN0 = 48
DFF = 2816
NFF = 22
DIN = 5648
ALPHA = 2.0 ** 0.25
EPS = 1e-5
BIG = 30000.0


class Res:
    __slots__ = ("name", "w", "r")

    def __init__(self, name):
        self.name = name
        self.w = None
        self.r = []


class Op:
    __slots__ = ("id", "eng", "fn", "deps", "dma", "sig", "need")

    def __init__(self, i, eng, fn, dma):
        self.id = i
        self.eng = eng
        self.fn = fn
        self.deps = set()
        self.dma = dma
        self.sig = None
        self.need = False


class Sched:
    ENGS = ("pe", "act", "dve", "pool", "sp")

    def __init__(self, nc):
        self.nc = nc
        self.ops = []
        self.last_barrier = None

    def op(self, eng, fn, reads=(), writes=(), dma=False):
        o = Op(len(self.ops), eng, fn, dma)
        for r in reads:
            if r.w is not None:
                o.deps.add(r.w)
        for w in writes:
            if w.w is not None:
                o.deps.add(w.w)
            for x in w.r:
                o.deps.add(x)
        for r in reads:
            r.r.append(o.id)
        for w in writes:
            w.w = o.id
            w.r = []
        o.deps.discard(o.id)
        self.ops.append(o)
        return o

    def emit(self, out_dma_ops):
        nc = self.nc
        ops = self.ops
        for o in ops:
            best = {}
            keep = set()
            for d in o.deps:
                p = ops[d]
                if p.dma:
                    keep.add(d)
                    continue
                if p.eng == "pe" and o.eng == "pe" and not o.dma:
                    continue
                if d > best.get(p.eng, -1):
                    best[p.eng] = d
            keep.update(best.values())
            o.deps = keep
            for d in keep:
                ops[d].need = True
        for d in out_dma_ops:
            ops[d].need = True
        import contextlib
        es = contextlib.ExitStack()
        NDS = 12
        csem = {e: es.enter_context(nc.semaphore("c_" + e)) for e in ("pe", "act", "dve", "pool")}
        dsem = {e: [es.enter_context(nc.semaphore("d_%s%d" % (e, i))) for i in range(NDS)] for e in ("sp", "pool", "act")}
        ccount = {e: 0 for e in csem}
        dcount = {e: [0] * NDS for e in dsem}
        drr = {e: 0 for e in dsem}
        streams = {e: [] for e in self.ENGS}
        known = {e: {} for e in self.ENGS}
        for o in ops:
            want = {}
            for d in o.deps:
                p = ops[d]
                if p.sig is None:
                    continue
                sem, val = p.sig
                if val > want.get(id(sem), (None, 0))[1]:
                    want[id(sem)] = (sem, val)
            waits = []
            for k_, (sem, val) in want.items():
                if known[o.eng].get(k_, 0) >= val:
                    continue
                known[o.eng][k_] = val
                waits.append((sem, val))
            if o.dma:
                i = drr[o.eng]
                drr[o.eng] = (i + 1) % NDS
                sem = dsem[o.eng][i]
                prev = dcount[o.eng][i]
                if prev > 0 and known[o.eng].get(id(sem), 0) < prev:
                    known[o.eng][id(sem)] = prev
                    waits.append((sem, prev))
                dcount[o.eng][i] = prev + 16
                o.sig = (sem, prev + 16)
                inc = 16
            elif o.need:
                ccount[o.eng] += 1
                o.sig = (csem[o.eng], ccount[o.eng])
                inc = 1
            else:
                inc = 0
            streams[o.eng].append((o, waits, inc))
        finals = [ops[d].sig for d in out_dma_ops]
        with es, nc.Block() as block:
            def run(engname):
                def body(eng):
                    for (o, waits, inc) in streams[engname]:
                        for (sem, val) in waits:
                            eng.wait_ge(sem, val)
                        ins = o.fn(eng)
                        if inc:
                            ins.then_inc(o.sig[0], inc)
                    if engname == "sp":
                        for (sem, val) in finals:
                            eng.wait_ge(sem, val)
                return body
            block.tensor(run("pe"))
            block.scalar(run("act"))
            block.vector(run("dve"))
            block.gpsimd(run("pool"))
            block.sync(run("sp"))


class Builder:
    def __init__(self):
        self.nc = bass.Bass("TRN2", target_bir_lowering=False)
        self.s = Sched(self.nc)
        self.out_dmas = []
        self.rescache = {}
        self.dq = 0
        self.bar = None
        self.bar_tile = None

    def R(self, name):
        if name not in self.rescache:
            r = Res(name)
            r.w = self.bar
            self.rescache[name] = r
        return self.rescache[name]

    def barrier(self):
        allr = list(self.rescache.values())
        scr = self.bar_tile
        o = self.s.op("pool", lambda e: e.memset(scr, 0.0), (), allr)
        self.bar = o.id

    def dma(self, out, in_, reads=(), writes=(), final=False, q=None):
        if q is None:
            q = "sp"
        o = self.s.op(q, lambda e, out=out, in_=in_: e.dma_start(out=out, in_=in_), reads, writes, dma=True)
        if final:
            self.out_dmas.append(o.id)
        return o

    def mm(self, out, lhsT, rhs, start, stop, reads=(), writes=()):
        return self.s.op("pe", lambda e: e.matmul(out, lhsT, rhs, start=start, stop=stop), reads, writes)

    def act(self, out, in_, func, bias=None, scale=None, reads=(), writes=(), accum_out=None):
        def f(e):
            kw = {}
            if bias is not None:
                kw["bias"] = bias
            if scale is not None:
                kw["scale"] = scale
            if accum_out is not None:
                kw["accum_out"] = accum_out
            return e.activation(out, in_, func, **kw)
        return self.s.op("act", f, reads, writes)

    def ts(self, eng, out, in0, s1, s2, op0, op1=None, reads=(), writes=()):
        def f(e):
            if op1 is None:
                return e.tensor_scalar(out, in0, s1, None, op0)
            return e.tensor_scalar(out, in0, s1, s2, op0, op1)
        return self.s.op(eng, f, reads, writes)

    def tt(self, eng, out, in0, in1, op, reads=(), writes=()):
        return self.s.op(eng, lambda e: e.tensor_tensor(out, in0, in1, op), reads, writes)

    def stt(self, eng, out, in0, scalar, in1, op0, op1, reads=(), writes=()):
        return self.s.op(eng, lambda e: e.scalar_tensor_tensor(out, in0, scalar, in1, op0, op1), reads, writes)

    def cp(self, eng, out, in_, reads=(), writes=()):
        if eng == "act":
            return self.s.op("act", lambda e: e.copy(out, in_), reads, writes)
        return self.s.op(eng, lambda e: e.tensor_copy(out, in_), reads, writes)

    def memset(self, eng, ap, val, writes=()):
        return self.s.op(eng, lambda e: e.memset(ap, val), (), writes)


def build():
    B = Builder()
    nc = B.nc
    s = B.s

    def din(name, shape, dt=F32):
        return nc.dram_tensor(name, list(shape), dt, kind="ExternalInput").ap()

    def dout(name, shape, dt=F32):
        return nc.dram_tensor(name, list(shape), dt, kind="ExternalOutput").ap()

    def dscr(name, shape, dt=F32):
        return nc.dram_tensor(name, list(shape), dt).ap()

    xT = din("xT", [128, 8, S])
    xsT = din("xsT", [128, 8, 32])
    keep = din("keep", [128, NT])
    cT = din("cT", [128, 8, 3])
    w_ada = din("w_ada", [128, 8, 9 * D])
    b_ada = din("b_ada", [128, 72])
    w_gu = [din("w_gu%d" % i, [128, 8, 2 * DFF]) for i in (1, 2)]
    w_dn = [din("w_dn%d" % i, [128, NFF, D]) for i in (1, 2)]
    w_in = din("w_in", [128, 8, DIN])
    b_in_fm = din("b_in_fm", [128, 40])
    b_in_bc = din("b_in_bc", [128, 1032])
    ident_d = din("ident", [128, 128])
    tri_d = din("tri", [128, 128])
    cmask_d = din("cmask", [128, 128])
    segm_d = din("segm", [128, 128])
    ng_d = din("ng", [128, 4])
    convs_d = din("convs", [128, 8, 2, 3])
    kcT_d = din("kcT", [2, 8, 64, 2048])
    vc_d = din("vc", [2, 8, 128, 16, 64])
    lfc_d = din("lfc", [2, 128, 16 * 8])
    Cs0_d = din("Cs0", [2, 128, 4 * 129])
    m0_d = din("m0c", [2, 4, 1])
    w_ba = din("w_ba", [128, 4, D])
    w_bb = din("w_bb", [128, 4, D])
    w_o = din("w_o", [128, 8, D])
    ln_g = din("ln_g", [128, 3, 8])
    ln_b = din("ln_b", [128, 3, 8])
    conv_w = din("conv_w", [128, 8, 4])
    conv_b = din("conv_b", [128, 8])

    yT = dout("yT", [128, 8, 4096])
    ysT = dout("ysT", [128, 8, 32])
    okT = dout("okT", [128, 4, 4096])
    ov = dout("ov", [4096, 512])
    olf = dout("olf", [4096, 8])
    oconv = dout("oconv", [128, 8, 3])
    oC = dout("oC", [128, 4 * 129])
    oksT = dout("oksT", [128, 4, 32])
    ovs = dout("ovs", [2, 16, 512])
    olfs = dout("olfs", [2, 16, 8])
    ocs = dout("ocs", [128, 8, 2, 3])
    oCs = dout("oCs", [2, 128, 4 * 129])
    oms = dout("oms", [2, 4, 1])
    om = dout("om", [128, 1])
    KT = dscr("KT", [8, 64, S], BF16)
    VA = dscr("VA", [8, 128, 128, 65], BF16)
    MK = dscr("MK", [128, 128, 512], BF16)
    MV = dscr("MV", [128, 128, 4 * 130], BF16)
    GI = dscr("GI", [4, S])
    GF = dscr("GF", [4, S])
    QT = dscr("QT", [8, 64, 4096], BF16)
    MQT = dscr("MQT", [4, 128, 4096], BF16)
    MKT = dscr("MKT", [4, 128, 4096], BF16)
    SG = dscr("SG", [16, 128, 20 * TN], BF16)
    QF = dscr("QF", [8, 4096], BF16)
    OA = dscr("OA", [8, 64, 4096], BF16)
    OB = dscr("OB", [4, 128, 4096], BF16)
    X2 = dscr("X2", [17, 128, 8, TN])
    KTs = dscr("KTs", [8, 64, 32], BF16)
    QTs = dscr("QTs", [8, 64, 32], BF16)
    VAs = dscr("VAs", [2, 16, 8 * 65], BF16)
    MQTs = dscr("MQTs", [4, 128, 32], BF16)
    MKTs = dscr("MKTs", [4, 128, 32], BF16)
    MKs = dscr("MKs", [2, 16, 512], BF16)
    MVs = dscr("MVs", [2, 16, 520], BF16)
    SGs = dscr("SGs", [128, 20 * 32], BF16)
    OAs = dscr("OAs", [8, 64, 32], BF16)
    OBs = dscr("OBs", [4, 128, 32], BF16)
    QFs = dscr("QFs", [2, 8, 16], BF16)

    X1 = dscr("X1", [NT + 1, 128, 8, TN])

    import contextlib
    es = contextlib.ExitStack()
    ARENA_F = 50432
    arena = es.enter_context(nc.sbuf_tensor("arena", [128, 2 * ARENA_F], BF16))
    psum = [es.enter_context(nc.psum_tensor("ps%d" % i, [128, 512], F32)) for i in range(8)]
    PS = [B.R("ps%d" % i) for i in range(8)]

    class Arena:
        def __init__(self):
            self.off = 0

        def f32(self, n):
            a = arena[:, 2 * self.off:2 * (self.off + n)].bitcast(F32)
            self.off += n
            assert self.off <= ARENA_F, self.off
            return a

        def bf16(self, n):
            m = (n + 1) // 2
            a = arena[:, 2 * self.off:2 * (self.off + m)]
            self.off += m
            assert self.off <= ARENA_F, self.off
            return a[:, 0:n]

    A = Arena()
    modv = A.f32(72 * 3)
    modv3 = modv.rearrange("p (j c) -> p j c", c=3)
    opsc = A.f32(72 * 3)
    opsc3 = opsc.rearrange("p (j c) -> p j c", c=3)
    keep_sb = A.f32(NT)
    lng = A.f32(24)
    lnb = A.f32(24)
    lng3 = lng.rearrange("p (a c) -> p a c", c=8)
    lnb3 = lnb.rearrange("p (a c) -> p a c", c=8)
    ones_f = A.f32(128)
    ones_b = A.bf16(128)
    R_const = B.R("const")
    B.memset("pool", ones_f, 1.0, writes=[R_const])
    B.memset("pool", ones_b, 1.0, writes=[R_const])
    B.dma(keep_sb, keep[:, :], writes=[R_const])
    B.dma(lng, ln_g.rearrange("p a c -> p (a c)"), writes=[R_const])
    B.dma(lnb, ln_b.rearrange("p a c -> p (a c)"), writes=[R_const])
    hg = A.f32(72 * 3)
    hg3 = hg.rearrange("p (j c) -> p j c", c=3)
    B.bar_tile = A.f32(2)
    LF = A.f32(128 * 8)
    LF3 = LF.rearrange("p (b h) -> p b h", h=8)
    R_LF = B.R("LF")
    identf = A.f32(128)
    identb = A.bf16(128)
    negkeep = A.f32(NT)
    keepbig = A.f32(NT)
    B.dma(identf, ident_d[:, :], writes=[R_const])
    trif = A.f32(128)
    cmask = A.f32(128)
    B.dma(trif, tri_d[:, :], writes=[R_const])
    B.dma(cmask, cmask_d[:, :], writes=[R_const])
    B.cp("dve", identb, identf, reads=[R_const], writes=[R_const])
    B.ts("dve", negkeep, keep_sb, -1.0, None, ALU.mult, reads=[R_const], writes=[R_const])
    B.ts("dve", keepbig, keep_sb, -1.0, BIG, ALU.add, ALU.mult, reads=[R_const], writes=[R_const])
    gis = A.f32(32)
    gfs = A.f32(32)
    lfn = A.f32(16)
    R_smp = B.R("smp")
    stage_off = A.off

    ct = A.f32(24)
    ct3 = ct.rearrange("p (k c) -> p k c", c=3)
    sct = A.f32(24)
    sct3 = sct.rearrange("p (k c) -> p k c", c=3)
    bada = A.f32(72)
    R_ct = B.R("ct")
    B.dma(ct, cT.rearrange("p k c -> p (k c)"), writes=[R_ct])
    B.dma(bada, b_ada[:, :], writes=[R_ct])
    B.act(sct, ct, AF.Silu, reads=[R_ct], writes=[R_ct])
    WA = 1152
    wbuf = [A.f32(8 * WA) for _ in range(2)]
    Rw = [B.R("wada0"), B.R("wada1")]
    R_mod = B.R("mod")
    for g in range(8):
        wb = wbuf[g % 2]
        wb3 = wb.rearrange("p (k n) -> p k n", n=WA)
        B.dma(wb3, w_ada[:, :, g * WA:(g + 1) * WA], writes=[Rw[g % 2]], q=("sp" if g % 2 == 0 else "pool"))
        pt = psum[g % 2]
        for j in range(9):
            for k in range(8):
                B.mm(pt[:, j * 3:j * 3 + 3], wb3[:, k, j * 128:(j + 1) * 128], sct3[:, k, :], k == 0, k == 7,
                     reads=[Rw[g % 2], R_ct], writes=[PS[g % 2]])
        for j in range(9):
            jj = g * 9 + j
            B.ts("dve", modv3[:, jj, :], pt[:, j * 3:j * 3 + 3], bada[:, jj:jj + 1], None, ALU.add,
                 reads=[PS[g % 2], R_ct], writes=[R_mod])
    B.ts("dve", opsc, modv, 1.0, None, ALU.add, reads=[R_mod], writes=[R_mod])
    B.ts("dve", hg, modv, 0.5, None, ALU.mult, reads=[R_mod], writes=[R_mod])

    def barrier_all(tag):
        r = B.R("bar_" + tag)
        allres = list(B.rescache.values())
        for e in ("pe", "act", "dve", "pool"):
            pass
        return r

    def ffn_stage(idx, wgu_d, wdn_d, tiles, sh_j, sc_j, g_j, ln_i, load_fn, store_fn, tagp):
        B.barrier()
        A.off = stage_off
        wgu = A.bf16(8 * 2 * DFF)
        wgu3 = wgu.rearrange("p (k n) -> p k n", n=2 * DFF)
        wdn = A.bf16(NFF * D)
        wdn3 = wdn.rearrange("p (k n) -> p k n", n=D)
        R_wgu = B.R(tagp + "wgu")
        R_wdn = B.R(tagp + "wdn")
        for k in range(8):
            B.dma(wgu3[:, k, :], wgu_d[:, k, :], writes=[R_wgu], q="pool")
        for k0 in range(0, NFF, 2):
            B.dma(wdn3[:, k0:k0 + 2, :], wdn_d[:, k0:k0 + 2, :], writes=[R_wdn], q="pool")
        xin = [A.f32(8 * TN) for _ in range(2)]
        Rx = [B.R(tagp + "x0"), B.R(tagp + "x1")]
        hb = A.bf16(8 * TN)
        R_h = B.R(tagp + "h")
        actb = A.bf16(NFF * TN)
        R_act = B.R(tagp + "actb")
        sg = [A.f32(TN) for _ in range(2)]
        Rsg = [B.R(tagp + "sg0"), B.R(tagp + "sg1")]
        sq = A.bf16(8 * TN)
        R_sq = B.R(tagp + "sq")
        rb = A.bf16(8 * TN)
        R_rb = B.R(tagp + "rb")
        st1 = A.f32(TN)
        st2 = A.f32(TN)
        st3 = A.f32(TN)
        R_st = B.R(tagp + "st")
        pi = 0
        for ti, (tid, N, cond) in enumerate(tiles):
            b = ti % 2
            x3 = xin[b].rearrange("p (k n) -> p k n", n=TN)
            if ti == 0:
                load_fn(tid, x3[:, :, 0:N], Rx[b])
            if ti + 1 < len(tiles):
                nt_, nN, _ = tiles[ti + 1]
                load_fn(nt_, xin[1 - b].rearrange("p (k n) -> p k n", n=TN)[:, :, 0:nN], Rx[1 - b])
            h3 = hb.rearrange("p (k n) -> p k n", n=TN)
            a3 = actb.rearrange("p (k n) -> p k n", n=TN)
            r3 = x3
            R_r = Rx[b]
            q3 = sq.rearrange("p (k n) -> p k n", n=TN)
            rb3 = rb.rearrange("p (k n) -> p k n", n=TN)
            def grp(cond_, N_):
                return [(0, N_, cond_)] if isinstance(cond_, int) else cond_
            groups = grp(cond, N)

            def emit_h(tj):
                _, Nj, condj = tiles[tj]
                bj = tj % 2
                xj = xin[bj].rearrange("p (k n) -> p k n", n=TN)
                for k in range(8):
                    for (c0, c1, ci) in grp(condj, Nj):
                        B.act(h3[:, k, c0:c1], xj[:, k, c0:c1], AF.Identity,
                              bias=modv3[:, sh_j + k, ci:ci + 1], scale=opsc3[:, sc_j + k, ci:ci + 1],
                              reads=[Rx[bj], R_mod], writes=[R_h])
                B.act(xj[:, :, 0:Nj], xj[:, :, 0:Nj], AF.Copy, scale=ALPHA, reads=[Rx[bj]], writes=[Rx[bj]])
            if ti == 0:
                emit_h(0)
            for m in range(NFF):
                pg = pi % 8
                pu = (pi + 1) % 8
                pi += 2
                for k in range(8):
                    B.mm(psum[pg][:, 0:N], wgu3[:, k, m * 128:(m + 1) * 128], h3[:, k, 0:N], k == 0, k == 7,
                         reads=[R_wgu, R_h], writes=[PS[pg]])
                for k in range(8):
                    B.mm(psum[pu][:, 0:N], wgu3[:, k, DFF + m * 128:DFF + (m + 1) * 128], h3[:, k, 0:N], k == 0, k == 7,
                         reads=[R_wgu, R_h], writes=[PS[pu]])
                si = m % 2
                B.act(sg[si][:, 0:N], psum[pg][:, 0:N], AF.Silu, reads=[PS[pg]], writes=[Rsg[si]])
                B.tt("dve", a3[:, m, 0:N], sg[si][:, 0:N], psum[pu][:, 0:N], ALU.mult,
                     reads=[Rsg[si], PS[pu]], writes=[R_act])
            if ti + 1 < len(tiles):
                emit_h(ti + 1)
            for d in range(8):
                pd = pi % 8
                pi += 1
                for m in range(NFF):
                    B.mm(psum[pd][:, 0:N], wdn3[:, m, d * 128:(d + 1) * 128], a3[:, m, 0:N], m == 0, m == NFF - 1,
                         reads=[R_wdn, R_act], writes=[PS[pd]])
                for (c0, c1, ci) in groups:
                    B.stt("dve", r3[:, d, c0:c1], psum[pd][:, c0:c1], hg3[:, g_j + d, ci:ci + 1], x3[:, d, c0:c1],
                          ALU.mult, ALU.add, reads=[PS[pd], R_mod, Rx[b]], writes=[R_r])
                B.act(q3[:, d, 0:N], r3[:, d, 0:N], AF.Square, reads=[R_r], writes=[R_sq])
                B.act(rb3[:, d, 0:N], r3[:, d, 0:N], AF.Copy, reads=[R_r], writes=[R_rb])
            p1 = pi % 8
            p2 = (pi + 1) % 8
            pi += 2
            for d in range(8):
                B.mm(psum[p1][:, 0:N], ones_b.rearrange("p (a n) -> p a n", a=1)[:, 0, :], rb3[:, d, 0:N], d == 0, d == 7,
                     reads=[R_rb, R_const], writes=[PS[p1]])
            for d in range(8):
                B.mm(psum[p2][:, 0:N], ones_b.rearrange("p (a n) -> p a n", a=1)[:, 0, :], q3[:, d, 0:N], d == 0, d == 7,
                     reads=[R_sq, R_const], writes=[PS[p2]])
            B.ts("dve", st1[:, 0:N], psum[p1][:, 0:N], -1.0 / D, None, ALU.mult, reads=[PS[p1]], writes=[R_st])
            B.tt("dve", st2[:, 0:N], st1[:, 0:N], st1[:, 0:N], ALU.mult, reads=[R_st], writes=[R_st])
            B.stt("dve", st3[:, 0:N], psum[p2][:, 0:N], 1.0 / D, st2[:, 0:N], ALU.mult, ALU.subtract,
                  reads=[PS[p2], R_st], writes=[R_st])
            B.ts("dve", st3[:, 0:N], st3[:, 0:N], EPS, None, ALU.add, reads=[R_st], writes=[R_st])
            B.act(st3[:, 0:N], st3[:, 0:N], AF.Sqrt, reads=[R_st], writes=[R_st])
            B.s.op("dve", lambda e, a=st3[:, 0:N]: e.reciprocal(a, a), [R_st], [R_st])
            for d in range(8):
                B.tt("dve", r3[:, d, 0:N], r3[:, d, 0:N], st1[:, 0:N], ALU.add, reads=[R_r, R_st], writes=[R_r])
                B.tt("dve", r3[:, d, 0:N], r3[:, d, 0:N], st3[:, 0:N], ALU.mult, reads=[R_r, R_st], writes=[R_r])
                B.ts("dve", r3[:, d, 0:N], r3[:, d, 0:N], lng3[:, ln_i, d:d + 1], lnb3[:, ln_i, d:d + 1], ALU.mult, ALU.add,
                     reads=[R_r, R_const], writes=[R_r])
            store_fn(tid, r3[:, :, 0:N], R_r)

    R_X1 = B.R("X1")

    def load1(tid, dst, res):
        if tid < NT:
            B.dma(dst, xT[:, :, tid * TN:(tid + 1) * TN], writes=[res])
        else:
            B.dma(dst, xsT[:, :, :], writes=[res])

    def store1(tid, src, res):
        Nn = TN if tid < NT else 32
        B.dma(X1[tid][:, :, 0:Nn], src, reads=[res], writes=[R_X1], q="pool")

    tiles1 = [(t, TN, 0) for t in range(NT)] + [(NT, 32, [(0, 16, 1), (16, 32, 2)])]
    ffn_stage(0, w_gu[0], w_dn[0], tiles1, 0, 8, 16, 0, load1, store1, "f1")


    FMC = ([("ak", 512 + 128 * c) for c in range(4)] + [("mk", 2056 + 128 * c) for c in range(4)] +
           [("aq", 128 * c) for c in range(4)] + [("mq", 1544 + 128 * c) for c in range(4)] +
           [("mo", 3088 + 128 * c) for c in range(4)] + [("ga", 3600 + 128 * c) for c in range(8)] +
           [("gb", 4624 + 128 * c) for c in range(8)])
    B.barrier()
    A.off = stage_off
    win = A.bf16(8 * DIN)
    win3 = win.rearrange("p (k n) -> p k n", n=DIN)
    R_win = B.R("win")
    for k in range(8):
        B.dma(win3[:, k, :], w_in[:, k, :], writes=[R_win], q="pool")
    bbc = A.f32(1032)
    bfm = A.f32(40)
    nbfm = A.f32(40)
    cw = A.f32(32)
    cw3 = cw.rearrange("p (c j) -> p c j", j=4)
    cb = A.f32(8)
    R_c2 = B.R("c2")
    B.dma(bbc, b_in_bc[:, :], writes=[R_c2])
    B.dma(bfm, b_in_fm[:, :], writes=[R_c2])
    B.dma(cw, conv_w.rearrange("p c j -> p (c j)"), writes=[R_c2])
    B.dma(cb, conv_b[:, :], writes=[R_c2])
    B.ts("dve", nbfm, bfm, -1.0, None, ALU.mult, reads=[R_c2], writes=[R_c2])
    x1b = [A.f32(8 * TN) for _ in range(3)]
    Rx1 = [B.R("s2x0"), B.R("s2x1"), B.R("s2x2")]
    h2s = [A.bf16(8 * TN) for _ in range(2)]
    h2s3 = [h.rearrange("p (k n) -> p k n", n=TN) for h in h2s]
    R_h2s = [B.R("h2_0"), B.R("h2_1")]
    UW = TN + 3
    ub = A.f32(8 * UW)
    u3 = ub.rearrange("p (c n) -> p c n", n=UW)
    R_u = B.R("u")
    B.memset("pool", ub, 0.0, writes=[R_u])
    co = A.f32(8 * TN)
    co3 = co.rearrange("p (c n) -> p c n", n=TN)
    R_co = B.R("co")
    qkbs = [A.bf16(8 * TN) for _ in range(2)]
    qkbs3 = [q_.rearrange("p (c n) -> p c n", n=TN) for q_ in qkbs]
    R_qkbs = [B.R("qkb0"), B.R("qkb1")]
    kTf = A.f32(4 * TN)
    kTf3 = kTf.rearrange("p (c n) -> p c n", n=TN)
    R_kTf = B.R("kTf")
    kTb = A.bf16(4 * TN)
    kTb3 = kTb.rearrange("p (c n) -> p c n", n=TN)
    R_kTb = B.R("kTb")
    qTb = A.bf16(4 * TN)
    qTb3 = qTb.rearrange("p (c n) -> p c n", n=TN)
    R_qTb = B.R("qTb")
    sgb = A.bf16(20 * TN)
    sgb3 = sgb.rearrange("p (c n) -> p c n", n=TN)
    R_sgb = B.R("sgb")
    vf = [A.f32(520) for _ in range(2)]
    R_vf = [B.R("vf0"), B.R("vf1")]
    va = [A.bf16(8 * 65) for _ in range(2)]
    R_va = [B.R("va0"), B.R("va1")]
    mvb = [A.bf16(4 * 130) for _ in range(2)]
    R_mvb = [B.R("mvb0"), B.R("mvb1")]
    ktok = [A.bf16(512) for _ in range(2)]
    R_ktok = [B.R("ktok0"), B.R("ktok1")]
    lft = A.f32(16)
    R_lft = B.R("lft")
    gt = A.f32(2 * TN)
    gt3 = gt.rearrange("p (a n) -> p a n", n=TN)
    R_gt = B.R("gt")
    for i in range(2):
        B.memset("pool", va[i], 1.0, writes=[R_va[i]])
        B.memset("pool", mvb[i], 1.0, writes=[R_mvb[i]])
    R_KT = B.R("KT"); R_VA = B.R("VA"); R_MK = B.R("MK"); R_MV = B.R("MV"); R_G = B.R("GIF")
    R_QT = B.R("QT"); R_MQT = B.R("MQT"); R_MKT = B.R("MKT"); R_SG = B.R("SG")
    pi = 0
    sq_ = ["sp", "sp"]
    def emit_h2(tj):
        bj = tj % 2
        xj = x1b[tj % 3].rearrange("p (k n) -> p k n", n=TN)
        for k in range(8):
            B.act(h2s3[bj][:, k, :], xj[:, k, :], AF.Identity, bias=modv3[:, 24 + k, 0:1], scale=opsc3[:, 32 + k, 0:1],
                  reads=[Rx1[tj % 3], R_mod], writes=[R_h2s[bj]])

    def emit_ktr(tj):
        bj = tj % 2
        nonlocal_pi = [0]
        for sb in range(2):
            blk = tj * 2 + sb
            p = 6 + sb
            pb = psum[p][:, :].bitcast(BF16)
            for hh in range(4):
                s.op("pe", lambda e, o_=pb[:, hh * 128:(hh + 1) * 128], i_=qkbs3[bj][:, 4 + hh, sb * 128:(sb + 1) * 128]:
                     e.transpose(o_, i_, identb), [R_qkbs[bj], R_const], [PS[p]])
            B.cp("act", ktok[sb], pb[:, 0:512], reads=[PS[p]], writes=[R_ktok[sb]])
            B.dma(MK[blk], ktok[sb], reads=[R_ktok[sb]], writes=[R_MK], q="sp")

    B.dma(x1b[0].rearrange("p (k n) -> p k n", n=TN), X1[0], reads=[R_X1], writes=[Rx1[0]])
    B.dma(x1b[1].rearrange("p (k n) -> p k n", n=TN), X1[1], reads=[R_X1], writes=[Rx1[1]])
    emit_h2(0)
    pending = None
    for t in range(NT):
        own = t >= OWN0
        b = t % 2
        if t + 2 < NT:
            B.dma(x1b[(t + 2) % 3].rearrange("p (k n) -> p k n", n=TN), X1[t + 2], reads=[R_X1], writes=[Rx1[(t + 2) % 3]])
        if t + 1 < NT:
            emit_h2(t + 1)
        N = TN
        h23 = h2s3[b]
        R_h2 = R_h2s[b]
        qkb3 = qkbs3[b]
        R_qkb = R_qkbs[b]
        kcol = keep_sb[:, t:t + 1]
        nch = len(FMC) if own else 8
        for j in range(nch):
            name, c0 = FMC[j]
            p = pi % 6
            pi += 1
            for k in range(8):
                B.mm(psum[p][:, 0:N], win3[:, k, c0:c0 + 128], h23[:, k, :], k == 0, k == 7,
                     reads=[R_win, R_h2], writes=[PS[p]])
            c = j % 4 if name not in ("ga", "gb") else (j - 20) % 8
            if name == "ak":
                B.act(kTf3[:, c, :], psum[p][:, 0:N], AF.Identity, bias=bfm[:, j:j + 1], reads=[PS[p], R_c2], writes=[R_kTf])
                B.cp("dve", kTb3[:, c, :], kTf3[:, c, :], reads=[R_kTf], writes=[R_kTb])
            elif name == "mk":
                B.ts("dve", u3[:, 4 + c, 3:3 + N], psum[p][:, 0:N], bfm[:, j:j + 1], kcol, ALU.add, ALU.mult,
                     reads=[PS[p], R_c2, R_const], writes=[R_u])
            elif name == "mq":
                B.ts("dve", u3[:, c, 3:3 + N], psum[p][:, 0:N], bfm[:, j:j + 1], kcol, ALU.add, ALU.mult,
                     reads=[PS[p], R_c2, R_const], writes=[R_u])
            elif name == "aq":
                B.ts("dve", qTb3[:, c, :], psum[p][:, 0:N], bfm[:, j:j + 1], 0.125, ALU.add, ALU.mult,
                     reads=[PS[p], R_c2], writes=[R_qTb])
            else:
                jj = {"mo": 0, "ga": 4, "gb": 12}[name] + c
                B.act(sgb3[:, jj, :], psum[p][:, 0:N], AF.Sigmoid, bias=bfm[:, j:j + 1], reads=[PS[p], R_c2], writes=[R_sgb])
        p = pi % 6
        pi += 1
        for k in range(8):
            B.mm(psum[p][0:4, 0:N], win3[:, k, 3080:3084], h23[:, k, :], k == 0, k == 7, reads=[R_win, R_h2], writes=[PS[p]])
        B.ts("dve", gt3[0:4, 0, :], psum[p][0:4, 0:N], bfm[0:4, 36:37], kcol[0:4, :], ALU.add, ALU.mult,
             reads=[PS[p], R_c2, R_const], writes=[R_gt])
        B.ts("dve", gt3[0:4, 0, :], gt3[0:4, 0, :], keepbig[0:4, t:t + 1], None, ALU.add, reads=[R_gt, R_const], writes=[R_gt])
        p = pi % 6
        pi += 1
        for k in range(8):
            B.mm(psum[p][0:4, 0:N], win3[:, k, 3084:3088], h23[:, k, :], k == 0, k == 7, reads=[R_win, R_h2], writes=[PS[p]])
        B.act(gt3[0:4, 1, :], psum[p][0:4, 0:N], AF.Exp, bias=nbfm[0:4, 37:38], scale=-1.0, reads=[PS[p], R_c2], writes=[R_gt])
        B.act(gt3[0:4, 1, :], gt3[0:4, 1, :], AF.Ln, bias=1.0, reads=[R_gt], writes=[R_gt])
        B.ts("dve", gt3[0:4, 1, :], gt3[0:4, 1, :], negkeep[0:4, t:t + 1], None, ALU.mult, reads=[R_gt, R_const], writes=[R_gt])
        B.dma(GI[:, t * TN:(t + 1) * TN], gt3[0:4, 0, :], reads=[R_gt], writes=[R_G], q="sp")
        B.dma(GF[:, t * TN:(t + 1) * TN], gt3[0:4, 1, :], reads=[R_gt], writes=[R_G], q="sp")
        for sb in range(2):
            blk = t * 2 + sb
            tok = slice(sb * 128, (sb + 1) * 128)
            p = pi % 6
            pi += 1
            for k in range(8):
                B.mm(psum[p][:, 0:512], h23[:, k, tok], win3[:, k, 1024:1536], k == 0, k == 7, reads=[R_win, R_h2], writes=[PS[p]])
            B.tt("dve", vf[sb][:, 0:512], psum[p][:, 0:512], bbc[:, 0:512], ALU.add, reads=[PS[p], R_c2], writes=[R_vf[sb]])
            B.cp("act", va[sb].rearrange("p (h d) -> p h d", d=65)[:, :, 0:64],
                 vf[sb][:, 0:512].rearrange("p (h d) -> p h d", d=64), reads=[R_vf[sb]], writes=[R_va[sb]])
            B.dma(VA[:, :, blk, :].rearrange("h p d -> p h d"), va[sb].rearrange("p (h d) -> p h d", d=65), reads=[R_va[sb]], writes=[R_VA], q="sp")
            if own:
                B.dma(ov[(blk - 2 * OWN0) * 128:(blk - 2 * OWN0 + 1) * 128, :], vf[sb][:, 0:512], reads=[R_vf[sb]], final=True, q="sp")
            p = pi % 6
            pi += 1
            for k in range(8):
                B.mm(psum[p][:, 0:8], h23[:, k, tok], win3[:, k, 1536:1544], k == 0, k == 7, reads=[R_win, R_h2], writes=[PS[p]])
            B.tt("dve", lft[:, 0:8], psum[p][:, 0:8], bbc[:, 512:520], ALU.add, reads=[PS[p], R_c2], writes=[R_lft])
            B.act(lft[:, 0:8], lft[:, 0:8], AF.Exp, scale=-1.0, reads=[R_lft], writes=[R_lft])
            B.act(lft[:, 0:8], lft[:, 0:8], AF.Ln, bias=1.0, reads=[R_lft], writes=[R_lft])
            B.ts("dve", LF3[:, blk, :], lft[:, 0:8], negkeep[:, t:t + 1], None, ALU.mult, reads=[R_lft, R_const], writes=[R_LF])
            if own:
                B.dma(olf[(blk - 2 * OWN0) * 128:(blk - 2 * OWN0 + 1) * 128, :], LF3[:, blk, :], reads=[R_LF], final=True, q="sp")
            p = pi % 6
            pi += 1
            for k in range(8):
                B.mm(psum[p][:, 0:512], h23[:, k, tok], win3[:, k, 2568:3080], k == 0, k == 7, reads=[R_win, R_h2], writes=[PS[p]])
            B.tt("dve", mvb[sb].rearrange("p (h d) -> p h d", d=130)[:, :, 0:128],
                 psum[p][:, 0:512].rearrange("p (h d) -> p h d", d=128),
                 bbc[:, 520:1032].rearrange("p (h d) -> p h d", d=128), ALU.add, reads=[PS[p], R_c2], writes=[R_mvb[sb]])
            B.dma(MV[blk], mvb[sb], reads=[R_mvb[sb]], writes=[R_MV], q="sp")
        chs = list(range(8)) if own else [4, 5, 6, 7]
        for ch in chs:
            B.ts("dve", co3[:, ch, :], u3[:, ch, 0:N], cw3[:, ch, 0:1], cb[:, ch:ch + 1], ALU.mult, ALU.add,
                 reads=[R_u, R_c2], writes=[R_co])
            for j in range(1, 4):
                B.stt("dve", co3[:, ch, :], u3[:, ch, j:j + N], cw3[:, ch, j:j + 1], co3[:, ch, :], ALU.mult, ALU.add,
                      reads=[R_u, R_c2, R_co], writes=[R_co])
            if ch < 4:
                B.act(qkb3[:, ch, :], co3[:, ch, :], AF.Silu, reads=[R_co], writes=[R_qkb])
            else:
                B.act(co3[:, ch, :], co3[:, ch, :], AF.Silu, reads=[R_co], writes=[R_co])
                B.ts("dve", qkb3[:, ch, :], co3[:, ch, :], 128.0 ** -0.5, None, ALU.mult, reads=[R_co], writes=[R_qkb])
        if t == NT - 1:
            B.dma(oconv[:, :, :], u3[:, :, TN:TN + 3], reads=[R_u], final=True)
        B.cp("dve", u3[:, :, 0:3], u3[:, :, TN:TN + 3], reads=[R_u], writes=[R_u])
        if pending is not None:
            emit_ktr(pending)
        pending = t
        for h in range(8):
            B.dma(KT[h][:, t * TN:(t + 1) * TN], kTb3[(h % 2) * 64:(h % 2 + 1) * 64, h // 2, :], reads=[R_kTb], writes=[R_KT],
                  q=sq_[h % 2])
        if own:
            to = t - OWN0
            B.dma(okT[:, :, to * TN:(to + 1) * TN], kTf3, reads=[R_kTf], final=True, q="sp")
            for h in range(8):
                B.dma(QT[h][:, to * TN:(to + 1) * TN], qTb3[(h % 2) * 64:(h % 2 + 1) * 64, h // 2, :], reads=[R_qTb], writes=[R_QT],
                      q=sq_[h % 2])
            for hh in range(4):
                B.dma(MQT[hh][:, to * TN:(to + 1) * TN], qkb3[:, hh, :], reads=[R_qkb], writes=[R_MQT], q="sp")
                B.dma(MKT[hh][:, to * TN:(to + 1) * TN], qkb3[:, 4 + hh, :], reads=[R_qkb], writes=[R_MKT], q="sp")
            B.dma(SG[to], sgb, reads=[R_sgb], writes=[R_SG], q="sp")


    emit_ktr(pending)
    h23 = h2s3[0]
    R_h2 = R_h2s[0]
    qkb3 = qkbs3[0]
    R_qkb = R_qkbs[0]
    NS = 32
    xs3 = x1b[0].rearrange("p (k n) -> p k n", n=TN)
    B.dma(xs3[:, :, 0:NS], X1[NT][:, :, 0:NS], reads=[R_X1], writes=[Rx1[0]])
    for k in range(8):
        for e in range(2):
            B.act(h23[:, k, e * 16:(e + 1) * 16], xs3[:, k, e * 16:(e + 1) * 16], AF.Identity,
                  bias=modv3[:, 24 + k, 1 + e:2 + e], scale=opsc3[:, 32 + k, 1 + e:2 + e], reads=[Rx1[0], R_mod], writes=[R_h2])
    usb = A.f32(8 * 2 * 19)
    us4 = usb.rearrange("p (c e n) -> p c e n", e=2, n=19)
    R_us = B.R("us")
    B.dma(us4[:, :, :, 0:3], convs_d[:, :, :, :], writes=[R_us])
    cos = co[:, 0:8 * 32]
    cos4 = cos.rearrange("p (c e n) -> p c e n", e=2, n=16)
    R_cos = R_co
    for j in range(len(FMC)):
        name, c0 = FMC[j]
        p = pi % 8
        pi += 1
        for k in range(8):
            B.mm(psum[p][:, 0:NS], win3[:, k, c0:c0 + 128], h23[:, k, 0:NS], k == 0, k == 7, reads=[R_win, R_h2], writes=[PS[p]])
        c = j % 4 if name not in ("ga", "gb") else (j - 20) % 8
        if name == "ak":
            B.act(kTf3[:, c, 0:NS], psum[p][:, 0:NS], AF.Identity, bias=bfm[:, j:j + 1], reads=[PS[p], R_c2], writes=[R_kTf])
            B.cp("dve", kTb3[:, c, 0:NS], kTf3[:, c, 0:NS], reads=[R_kTf], writes=[R_kTb])
        elif name in ("mk", "mq"):
            ch = c + (4 if name == "mk" else 0)
            B.ts("dve", us4[:, ch, :, 3:19], psum[p][:, 0:NS].rearrange("p (e n) -> p e n", n=16), bfm[:, j:j + 1], None, ALU.add,
                 reads=[PS[p], R_c2], writes=[R_us])
        elif name == "aq":
            B.ts("dve", qTb3[:, c, 0:NS], psum[p][:, 0:NS], bfm[:, j:j + 1], 0.125, ALU.add, ALU.mult, reads=[PS[p], R_c2], writes=[R_qTb])
        else:
            jj = {"mo": 0, "ga": 4, "gb": 12}[name] + c
            B.act(sgb3[:, jj, 0:NS], psum[p][:, 0:NS], AF.Sigmoid, bias=bfm[:, j:j + 1], reads=[PS[p], R_c2], writes=[R_sgb])
    p = pi % 8
    pi += 1
    for k in range(8):
        B.mm(psum[p][0:4, 0:NS], win3[:, k, 3080:3084], h23[:, k, 0:NS], k == 0, k == 7, reads=[R_win, R_h2], writes=[PS[p]])
    B.ts("dve", gis[0:4, :], psum[p][0:4, 0:NS], bfm[0:4, 36:37], None, ALU.add, reads=[PS[p], R_c2], writes=[R_smp])
    p = pi % 8
    pi += 1
    for k in range(8):
        B.mm(psum[p][0:4, 0:NS], win3[:, k, 3084:3088], h23[:, k, 0:NS], k == 0, k == 7, reads=[R_win, R_h2], writes=[PS[p]])
    B.act(gfs[0:4, :], psum[p][0:4, 0:NS], AF.Exp, bias=nbfm[0:4, 37:38], scale=-1.0, reads=[PS[p], R_c2], writes=[R_smp])
    B.act(gfs[0:4, :], gfs[0:4, :], AF.Ln, bias=1.0, reads=[R_smp], writes=[R_smp])
    B.ts("dve", gfs[0:4, :], gfs[0:4, :], -1.0, None, ALU.mult, reads=[R_smp], writes=[R_smp])
    R_VAs = B.R("VAs"); R_MVs = B.R("MVs"); R_MKs = B.R("MKs")
    for e in range(2):
        tok = slice(e * 16, (e + 1) * 16)
        p = pi % 8
        pi += 1
        for k in range(8):
            B.mm(psum[p][0:16, 0:512], h23[:, k, tok], win3[:, k, 1024:1536], k == 0, k == 7, reads=[R_win, R_h2], writes=[PS[p]])
        B.tt("dve", vf[e][0:16, 0:512], psum[p][0:16, 0:512], bbc[0:16, 0:512], ALU.add, reads=[PS[p], R_c2], writes=[R_vf[e]])
        B.cp("act", va[e][0:16, :].rearrange("p (h d) -> p h d", d=65)[:, :, 0:64],
             vf[e][0:16, 0:512].rearrange("p (h d) -> p h d", d=64), reads=[R_vf[e]], writes=[R_va[e]])
        B.dma(VAs[e], va[e][0:16, :], reads=[R_va[e]], writes=[R_VAs], q="sp")
        B.dma(ovs[e], vf[e][0:16, 0:512], reads=[R_vf[e]], final=True, q="sp")
        p = pi % 8
        pi += 1
        for k in range(8):
            B.mm(psum[p][0:16, 0:8], h23[:, k, tok], win3[:, k, 1536:1544], k == 0, k == 7, reads=[R_win, R_h2], writes=[PS[p]])
        B.tt("dve", lft[0:16, 0:8], psum[p][0:16, 0:8], bbc[0:16, 512:520], ALU.add, reads=[PS[p], R_c2], writes=[R_lft])
        B.act(lft[0:16, 0:8], lft[0:16, 0:8], AF.Exp, scale=-1.0, reads=[R_lft], writes=[R_lft])
        B.act(lft[0:16, 0:8], lft[0:16, 0:8], AF.Ln, bias=1.0, reads=[R_lft], writes=[R_lft])
        B.ts("dve", lfn[0:16, e * 8:(e + 1) * 8], lft[0:16, 0:8], -1.0, None, ALU.mult, reads=[R_lft], writes=[R_smp])
        B.dma(olfs[e], lfn[0:16, e * 8:(e + 1) * 8], reads=[R_smp], final=True, q="sp")
        p = pi % 8
        pi += 1
        for k in range(8):
            B.mm(psum[p][0:16, 0:512], h23[:, k, tok], win3[:, k, 2568:3080], k == 0, k == 7, reads=[R_win, R_h2], writes=[PS[p]])
        B.tt("dve", mvb[e][0:16, :].rearrange("p (h d) -> p h d", d=130)[:, :, 0:128],
             psum[p][0:16, 0:512].rearrange("p (h d) -> p h d", d=128),
             bbc[0:16, 520:1032].rearrange("p (h d) -> p h d", d=128), ALU.add, reads=[PS[p], R_c2], writes=[R_mvb[e]])
        B.dma(MVs[e], mvb[e][0:16, :], reads=[R_mvb[e]], writes=[R_MVs], q="sp")
    for ch in range(8):
        B.ts("dve", cos4[:, ch, :, :], us4[:, ch, :, 0:16], cw3[:, ch, 0:1], cb[:, ch:ch + 1], ALU.mult, ALU.add,
             reads=[R_us, R_c2], writes=[R_cos])
        for j in range(1, 4):
            B.stt("dve", cos4[:, ch, :, :], us4[:, ch, :, j:j + 16], cw3[:, ch, j:j + 1], cos4[:, ch, :, :], ALU.mult, ALU.add,
                  reads=[R_us, R_c2, R_cos], writes=[R_cos])
        B.act(cos4[:, ch, :, :], cos4[:, ch, :, :], AF.Silu, reads=[R_cos], writes=[R_cos])
        B.ts("dve", qkb3[:, ch, 0:NS].rearrange("p (e n) -> p e n", n=16), cos4[:, ch, :, :], (1.0 if ch < 4 else 128.0 ** -0.5), None, ALU.mult,
             reads=[R_cos], writes=[R_qkb])
    B.dma(ocs[:, :, :, :], us4[:, :, :, 16:19], reads=[R_us], final=True)
    for e in range(2):
        p = pi % 8
        pi += 1
        pb = psum[p][:, :].bitcast(BF16)
        for hh in range(4):
            s.op("pe", lambda e_, o_=pb[0:16, hh * 128:(hh + 1) * 128], i_=qkb3[:, 4 + hh, e * 16:(e + 1) * 16]:
                 e_.transpose(o_, i_, identb), [R_qkb, R_const], [PS[p]])
        B.cp("act", ktok[e][0:16, :], pb[0:16, 0:512], reads=[PS[p]], writes=[R_ktok[e]])
        B.dma(MKs[e], ktok[e][0:16, :], reads=[R_ktok[e]], writes=[R_MKs], q="sp")
    R_KTs = B.R("KTs"); R_QTs = B.R("QTs"); R_MQTs = B.R("MQTs"); R_SGs = B.R("SGs")
    B.dma(oksT[:, :, :], kTf3[:, :, 0:NS], reads=[R_kTf], final=True, q="sp")
    for h in range(8):
        B.dma(KTs[h], kTb3[(h % 2) * 64:(h % 2 + 1) * 64, h // 2, 0:NS], reads=[R_kTb], writes=[R_KTs], q=sq_[h % 2])
        B.dma(QTs[h], qTb3[(h % 2) * 64:(h % 2 + 1) * 64, h // 2, 0:NS], reads=[R_qTb], writes=[R_QTs], q=sq_[h % 2])
    for hh in range(4):
        B.dma(MQTs[hh], qkb3[:, hh, 0:NS], reads=[R_qkb], writes=[R_MQTs], q="sp")
        B.dma(MKTs[hh], qkb3[:, 4 + hh, 0:NS], reads=[R_qkb], writes=[R_MQTs], q="sp")
    B.dma(SGs.rearrange("p (c n) -> p c n", n=NS), sgb3[:, :, 0:NS], reads=[R_sgb], writes=[R_SGs], q="sp")

    B.barrier()
    A.off = stage_off
    R_F = B.R("F")
    Fk = A.f32(1024)
    Fk3 = Fk.rearrange("p (b h) -> p b h", h=8)
    Tb = [A.f32(1024) for _ in range(2)]
    R_Tb = [B.R("Tb0"), B.R("Tb1")]
    for half in range(2):
        cs = slice(half * 512, (half + 1) * 512)
        B.mm(psum[half][:, :], trif, LF[:, cs], True, True, reads=[R_const, R_LF], writes=[PS[half]])
        B.mm(psum[2 + half][:, :], ones_f, LF[:, cs], True, True, reads=[R_const, R_LF], writes=[PS[2 + half]])
        B.cp("dve", Fk[:, cs], psum[half][:, :], reads=[PS[half]], writes=[R_F])
        B.cp("dve", Tb[0][:, cs], psum[2 + half][:, :], reads=[PS[2 + half]], writes=[R_Tb[0]])
    B.tt("dve", Fk, Fk, Tb[0], ALU.subtract, reads=[R_F, R_Tb[0]], writes=[R_F])
    cur = 0
    sh = 1
    while sh < 128:
        a3 = Tb[cur].rearrange("p (b h) -> p b h", h=8)
        n3 = Tb[1 - cur].rearrange("p (b h) -> p b h", h=8)
        B.cp("dve", n3[:, 0:sh, :], a3[:, 0:sh, :], reads=[R_Tb[cur]], writes=[R_Tb[1 - cur]])
        B.tt("dve", n3[:, sh:128, :], a3[:, sh:128, :], a3[:, 0:128 - sh, :], ALU.add, reads=[R_Tb[cur]], writes=[R_Tb[1 - cur]])
        cur = 1 - cur
        sh *= 2
    Tinc3 = Tb[cur].rearrange("p (b h) -> p b h", h=8)
    B.tt("dve", Fk, Fk, Tb[cur], ALU.add, reads=[R_F, R_Tb[cur]], writes=[R_F])
    for h in range(8):
        B.ts("dve", Fk3[:, :, h], Fk3[:, :, h], Tinc3[:, 2 * OWN0 - 1, h:h + 1], None, ALU.subtract,
             reads=[R_F, R_Tb[cur]], writes=[R_F])
    bK = A.f32(1024)
    bK3 = bK.rearrange("p (h b) -> p h b", b=128)
    R_bK = B.R("bK")
    for h in range(8):
        B.ts("dve", bK3[:, h, :], Fk3[:, :, h], -1.0, None, ALU.mult, reads=[R_F], writes=[R_bK])
    for t in range(OWN0):
        B.ts("dve", bK3[:, :, 2 * t:2 * t + 2], bK3[:, :, 2 * t:2 * t + 2], keepbig[:, t:t + 1], None, ALU.add,
             reads=[R_bK, R_const], writes=[R_bK])
    fq = A.bf16(4096)
    R_fq = B.R("fq")
    R_QF = B.R("QF")
    for g in range(8):
        p = 4 + g % 2
        for i in range(4):
            blk = 2 * OWN0 + g * 4 + i
            s.op("pe", lambda e, o_=psum[p][0:8, i * 128:(i + 1) * 128], i_=Fk3[:, blk, :]: e.transpose(o_, i_, identf),
                 [R_F, R_const], [PS[p]])
        B.cp("act", fq[0:8, g * 512:(g + 1) * 512], psum[p][0:8, :], reads=[PS[p]], writes=[R_fq])
    B.dma(QF[:, :], fq[0:8, :], reads=[R_fq], writes=[R_QF])

    Kb = [A.bf16(S) for _ in range(2)]
    Qb = [A.bf16(4096) for _ in range(2)]
    Vb = [A.bf16(128 * 65) for _ in range(2)]
    R_Kb = [B.R("Kb0"), B.R("Kb1")]
    R_Qb = [B.R("Qb0"), B.R("Qb1")]
    R_Vb = [B.R("Vb0"), B.R("Vb1")]
    for i in range(2):
        B.memset("pool", Kb[i][64:65, :], 1.0, writes=[R_Kb[i]])
    NPB = 4
    Pb = [A.bf16(512) for _ in range(NPB)]
    R_Pb = [B.R("Pb%d" % i) for i in range(NPB)]
    dtmp = [A.f32(128) for _ in range(2)]
    R_dt = [B.R("dt0"), B.R("dt1")]
    onf = A.f32(512)
    R_onf = B.R("onf")
    rden = A.f32(512)
    R_rden = B.R("rden")
    oTb = [A.bf16(512) for _ in range(2)]
    R_oTb = [B.R("oTb0"), B.R("oTb1")]
    R_OA = B.R("OA")

    def load_head(h):
        hb = h % 2
        B.dma(Kb[hb][0:64, :], KT[h], reads=[R_KT], writes=[R_Kb[hb]], q="sp")
        B.dma(Qb[hb][0:64, :], QT[h], reads=[R_QT], writes=[R_Qb[hb]], q="sp")
        B.dma(Qb[hb][64:65, :], QF[h:h + 1, :], reads=[R_QF], writes=[R_Qb[hb]], q="sp")
        B.dma(Vb[hb].rearrange("p (b d) -> p b d", d=65), VA[h], reads=[R_VA], writes=[R_Vb[hb]], q="pool")

    load_head(0)
    LA = 3
    ntile = 0
    for h in range(8):
        hb = h % 2
        if h + 1 < 8:
            load_head(h + 1)
        V3 = Vb[hb].rearrange("p (b d) -> p b d", d=65)
        items = []
        for lt in range(8):
            nkb = 2 * OWN0 + 4 * lt + 4
            for kb in range(nkb):
                d = kb - (2 * OWN0 + 4 * lt)
                c0 = 0 if d < 0 else d * 128
                items.append((lt, kb, c0, d, kb == 0, kb == nkb - 1))
        def emit_mm1(i):
            lt, kb, c0, d, first, last = items[i]
            sbk = i % 4
            B.mm(psum[sbk][:, c0:512], Kb[hb][0:65, kb * 128:(kb + 1) * 128], Qb[hb][0:65, lt * 512 + c0:(lt + 1) * 512],
                 True, True, reads=[R_Kb[hb], R_Qb[hb]], writes=[PS[sbk]])
        for i in range(min(LA, len(items))):
            emit_mm1(i)
        for i, (lt, kb, c0, d, first, last) in enumerate(items):
            if i + LA < len(items):
                emit_mm1(i + LA)
            sbk = i % 4
            pb_ = i % NPB
            ob = 4 + (ntile % 2)
            bias = bK3[:, h, kb:kb + 1]
            if d >= 0:
                di = i % 2
                B.tt("dve", dtmp[di], psum[sbk][:, c0:c0 + 128], cmask, ALU.add, reads=[PS[sbk], R_const], writes=[R_dt[di]])
                B.act(Pb[pb_][:, c0:c0 + 128], dtmp[di], AF.Exp, bias=bias, reads=[R_dt[di], R_bK], writes=[R_Pb[pb_]])
                if c0 + 128 < 512:
                    B.act(Pb[pb_][:, c0 + 128:512], psum[sbk][:, c0 + 128:512], AF.Exp, bias=bias,
                          reads=[PS[sbk], R_bK], writes=[R_Pb[pb_]])
            else:
                B.act(Pb[pb_][:, :], psum[sbk][:, :], AF.Exp, bias=bias, reads=[PS[sbk], R_bK], writes=[R_Pb[pb_]])
            B.mm(psum[ob][0:65, c0:512], V3[:, kb, :], Pb[pb_][:, c0:512], first, last,
                 reads=[R_Vb[hb], R_Pb[pb_]], writes=[PS[ob]])
            if last:
                B.cp("act", onf[0:64, :], psum[ob][0:64, :], reads=[PS[ob]], writes=[R_onf])
                s.op("dve", lambda e, o_=rden[64:65, :], i_=psum[ob][64:65, :]: e.reciprocal(o_, i_), [PS[ob]], [R_rden])
                B.mm(psum[6][0:64, :], ones_f[64:65, 0:64], rden[64:65, :], True, True, reads=[R_const, R_rden], writes=[PS[6]])
                ot = ntile % 2
                B.tt("dve", oTb[ot][0:64, :], onf[0:64, :], psum[6][0:64, :], ALU.mult, reads=[R_onf, PS[6]], writes=[R_oTb[ot]])
                B.dma(OA[h][:, lt * 512:(lt + 1) * 512], oTb[ot][0:64, :], reads=[R_oTb[ot]], writes=[R_OA], q="pool")
                ntile += 1


    B.barrier()
    A.off = stage_off
    R_p4 = B.R("p4")
    segm = A.f32(128)
    ngs = A.f32(4)
    B.dma(segm, segm_d[:, :], writes=[R_p4])
    B.dma(ngs, ng_d[:, :], writes=[R_p4])
    gi = A.f32(512)
    gf = A.f32(512)
    B.dma(gi, GI.rearrange("h (s j) -> (h s) j", j=512), reads=[R_G], writes=[R_p4])
    B.dma(gf, GF.rearrange("h (s j) -> (h s) j", j=512), reads=[R_G], writes=[R_p4])
    cbuf = [gf, A.f32(512)]
    cur = 0
    sh = 1
    while sh < 512:
        B.cp("dve", cbuf[1 - cur][:, 0:sh], cbuf[cur][:, 0:sh], reads=[R_p4], writes=[R_p4])
        B.tt("dve", cbuf[1 - cur][:, sh:512], cbuf[cur][:, sh:512], cbuf[cur][:, 0:512 - sh], ALU.add, reads=[R_p4], writes=[R_p4])
        cur = 1 - cur
        sh *= 2
    Bc = cbuf[cur]
    other = cbuf[1 - cur]
    offs = A.f32(1)
    B.mm(psum[0][:, 0:1], segm, Bc[:, 511:512], True, True, reads=[R_p4], writes=[PS[0]])
    B.cp("dve", offs, psum[0][:, 0:1], reads=[PS[0]], writes=[R_p4])
    B.ts("dve", Bc, Bc, offs[:, 0:1], None, ALU.add, reads=[R_p4], writes=[R_p4])
    gg = other
    B.tt("dve", gg, gi, Bc, ALU.subtract, reads=[R_p4], writes=[R_p4])
    gmax = A.f32(8)
    s.op("dve", lambda e: e.tensor_reduce(gmax, gg.rearrange("p (j t) -> p j t", t=64), AX.X, ALU.max), [R_p4], [R_p4])
    for j in range(1, 8):
        B.tt("dve", gmax[:, j:j + 1], gmax[:, j:j + 1], gmax[:, j - 1:j], ALU.max, reads=[R_p4], writes=[R_p4])
    s.op("pe", lambda e: e.transpose(psum[1][0:1, 0:128], gmax[:, 7:8], identf), [R_p4, R_const], [PS[1]])
    rowa = A.f32(132)
    rowb = A.f32(132)
    B.memset("dve", rowa[0:1, :], 0.0, writes=[R_p4])
    B.memset("dve", rowb[0:1, :], 0.0, writes=[R_p4])
    ra3 = rowa[0:1, :].rearrange("p (h s) -> p h s", s=33)
    rb3 = rowb[0:1, :].rearrange("p (h s) -> p h s", s=33)
    B.cp("dve", ra3[:, :, 1:33], psum[1][0:1, 0:128].rearrange("p (h s) -> p h s", s=32), reads=[PS[1]], writes=[R_p4])
    rc, rn = ra3, rb3
    sh = 1
    while sh < 33:
        B.cp("dve", rn[:, :, 0:sh], rc[:, :, 0:sh], reads=[R_p4], writes=[R_p4])
        B.tt("dve", rn[:, :, sh:33], rc[:, :, sh:33], rc[:, :, 0:33 - sh], ALU.max, reads=[R_p4], writes=[R_p4])
        rc, rn = rn, rc
        sh *= 2
    prow = A.f32(128)
    B.cp("dve", prow[0:1, :].rearrange("p (h s) -> p h s", s=32), rc[:, :, 0:32], reads=[R_p4], writes=[R_p4])
    B.mm(psum[2][:, 0:1], prow[0:1, :], ones_f[0:1, 0:1], True, True, reads=[R_p4, R_const], writes=[PS[2]])
    pm = A.f32(1)
    B.cp("dve", pm, psum[2][:, 0:1], reads=[PS[2]], writes=[R_p4])
    mu = A.f32(8)
    mup = A.f32(8)
    B.ts("dve", mu, gmax, pm[:, 0:1], None, ALU.max, reads=[R_p4], writes=[R_p4])
    B.cp("dve", mup[:, 1:8], mu[:, 0:7], reads=[R_p4], writes=[R_p4])
    B.cp("dve", mup[:, 0:1], pm, reads=[R_p4], writes=[R_p4])
    nmu = A.f32(8)
    B.ts("dve", nmu, mu, -1.0, None, ALU.mult, reads=[R_p4], writes=[R_p4])
    alph = A.f32(8)
    B.tt("dve", alph, mup, mu, ALU.subtract, reads=[R_p4], writes=[R_p4])
    B.act(alph, alph, AF.Exp, reads=[R_p4], writes=[R_p4])
    mo_ = A.f32(1)
    B.tt("dve", mo_, mu[:, 7:8], Bc[:, 511:512], ALU.add, reads=[R_p4], writes=[R_p4])
    B.dma(om[:, :], mo_, reads=[R_p4], final=True)
    for j in range(8):
        B.act(gg[:, j * 64:(j + 1) * 64], gg[:, j * 64:(j + 1) * 64], AF.Exp, bias=nmu[:, j:j + 1], reads=[R_p4], writes=[R_p4])
        B.act(gi[:, j * 64:(j + 1) * 64], Bc[:, j * 64:(j + 1) * 64], AF.Exp, bias=nmu[:, j:j + 1], scale=-1.0, reads=[R_p4], writes=[R_p4])
    aT = A.f32(1024)
    eT = A.f32(1024)
    abc = A.f32(1024)
    aT3 = aT.rearrange("p (j q) -> p j q", q=128)
    eT3 = eT.rearrange("p (j q) -> p j q", q=128)
    abc3 = abc.rearrange("p (j q) -> p j q", q=128)
    R_aT = B.R("aT")
    dg = A.f32(128)
    for (src, dst) in ((gg, aT), (gi, eT)):
        for half in range(2):
            p = 3 + half
            for jj in range(4):
                j = half * 4 + jj
                s.op("pe", lambda e, o_=psum[p][0:64, jj * 128:(jj + 1) * 128], i_=src[:, j * 64:(j + 1) * 64]: e.transpose(o_, i_, identf),
                     [R_p4, R_const], [PS[p]])
            B.cp("dve", dst[0:64, half * 512:(half + 1) * 512], psum[p][0:64, :], reads=[PS[p]], writes=[R_aT])
    for half in range(2):
        p = 5 + half
        for jj in range(4):
            j = half * 4 + jj
            B.ts("dve", dg, identf, alph[:, j:j + 1], None, ALU.mult, reads=[R_p4, R_const], writes=[R_p4])
            B.mm(psum[p][:, jj * 128:(jj + 1) * 128], ones_f, dg, True, True, reads=[R_p4, R_const], writes=[PS[p]])
        B.cp("dve", abc[:, half * 512:(half + 1) * 512], psum[p][:, :], reads=[PS[p]], writes=[R_aT])

    Cst = A.f32(4 * 129)
    Cst3 = Cst.rearrange("p (h d) -> p h d", d=129)
    R_C = [B.R("C%d" % h) for h in range(4)]
    B.memset("dve", Cst, 0.0, writes=R_C)
    kg = [A.bf16(8 * 512) for _ in range(2)]
    vg = [A.bf16(8 * 520) for _ in range(2)]
    R_kg = [B.R("kg0"), B.R("kg1")]
    R_vg = [B.R("vg0"), B.R("vg1")]
    qTg = [A.bf16(4 * 512) for _ in range(2)]
    kTg = [A.bf16(4 * 512) for _ in range(2)]
    mog = [A.bf16(4 * 512) for _ in range(2)]
    R_qTg = [B.R("qTg0"), B.R("qTg1")]
    ka = [A.bf16(128) for _ in range(4)]
    R_ka = [B.R("ka%d" % i) for i in range(4)]
    AT = [A.bf16(64) for _ in range(2)]
    R_AT = [B.R("AT0"), B.R("AT1")]
    Cab = [A.bf16(130) for _ in range(2)]
    R_Cab = [B.R("Cab0"), B.R("Cab1")]
    for i in range(2):
        B.memset("pool", Cab[i], 0.0, writes=[R_Cab[i]])
    cl = A.f32(8)
    R_cl = B.R("cl")
    hx = [A.f32(128) for _ in range(2)]
    R_hx = [B.R("hx0"), B.R("hx1")]
    bst = A.f32(4 * 6)
    bag = A.f32(4 * 2)
    R_bst = B.R("bst")
    rstd = A.f32(4)
    hnb = A.bf16(4 * 128)
    hnb3 = hnb.rearrange("p (h d) -> p h d", d=128)
    R_hnb = B.R("hnb")
    obT = [A.bf16(4 * 512) for _ in range(2)]
    R_obT = [B.R("obT0"), B.R("obT1")]
    R_OB = B.R("OB")
    MKc = MK.rearrange("b (u p) f -> (b u) p f", u=2)
    MVc = MV.rearrange("b (u p) f -> (b u) p f", u=2)
    pi = 0

    def load_group(g):
        gb = g % 2
        B.dma(kg[gb][0:64, :].rearrange("p (c f) -> p c f", f=512), MKc[g * 8:(g + 1) * 8].rearrange("c p f -> p c f"),
              reads=[R_MK], writes=[R_kg[gb]], q="sp")
        B.dma(vg[gb][0:64, :].rearrange("p (c f) -> p c f", f=520), MVc[g * 8:(g + 1) * 8].rearrange("c p f -> p c f"),
              reads=[R_MV], writes=[R_vg[gb]], q="pool")
        if g >= 24:
            go = g - 24
            B.dma(qTg[gb].rearrange("p (h n) -> p h n", n=512), MQT[:, :, go * 512:(go + 1) * 512].rearrange("h p n -> p h n"),
                  reads=[R_MQT], writes=[R_qTg[gb]], q="sp")
            B.dma(kTg[gb].rearrange("p (h n) -> p h n", n=512), MKT[:, :, go * 512:(go + 1) * 512].rearrange("h p n -> p h n"),
                  reads=[R_MKT], writes=[R_qTg[gb]], q="sp")
            for u in range(2):
                B.dma(mog[gb].rearrange("p (h n) -> p h n", n=512)[:, :, u * 256:(u + 1) * 256],
                      SG[2 * go + u].rearrange("p (c n) -> p c n", n=TN)[:, 0:4, :], reads=[R_SG], writes=[R_qTg[gb]], q="pool")

    load_group(0)
    for g in range(32):
        gb = g % 2
        if g + 1 < 32:
            load_group(g + 1)
        own = g >= 24
        k4 = kg[gb].rearrange("p (c h d) -> p c h d", h=4, d=128)
        v4 = vg[gb].rearrange("p (c h d) -> p c h d", h=4, d=130)
        q3 = qTg[gb].rearrange("p (h n) -> p h n", n=512)
        kT3 = kTg[gb].rearrange("p (h n) -> p h n", n=512)
        mo3 = mog[gb].rearrange("p (h n) -> p h n", n=512)
        ob3 = obT[gb].rearrange("p (h n) -> p h n", n=512)
        seg = g
        for j in range(8):
            tk = slice(j * 64, (j + 1) * 64)
            for h in range(4):
                col = h * 32 + seg
                a_s = aT3[0:64, j, col:col + 1]
                al = abc3[:, j, col:col + 1]
                ki = (j * 4 + h) % 4
                B.act(ka[ki][0:64, :], k4[0:64, j, h, :], AF.Copy, scale=a_s, reads=[R_kg[gb], R_aT], writes=[R_ka[ki]])
                pu = pi % 3
                pi += 1
                B.mm(psum[pu][:, 0:130], ka[ki][0:64, :], v4[0:64, j, h, :], True, True, reads=[R_ka[ki], R_vg[gb]], writes=[PS[pu]])
                if own:
                    ai = (j * 4 + h) % 2
                    B.mm(psum[3][0:64, 0:64], kT3[:, h, tk], q3[:, h, tk], True, True, reads=[R_qTg[gb]], writes=[PS[3]])
                    B.stt("dve", AT[ai][0:64, :], psum[3][0:64, 0:64], a_s, trif[0:64, 0:64], ALU.mult, ALU.mult,
                          reads=[PS[3], R_aT, R_const], writes=[R_AT[ai]])
                    B.ts("dve", Cab[ai][:, 0:129], Cst3[:, h, :], al, None, ALU.mult, reads=[R_C[h], R_aT], writes=[R_Cab[ai]])
                    ph = 4 + ai
                    B.mm(psum[ph][0:64, 0:130], AT[ai][0:64, :], v4[0:64, j, h, :], True, False,
                         reads=[R_AT[ai], R_vg[gb]], writes=[PS[ph]])
                    B.mm(psum[ph][0:64, 0:130], q3[:, h, tk], Cab[ai], False, True, reads=[R_qTg[gb], R_Cab[ai]], writes=[PS[ph]])
                    e_t = eT3[0:64, j, col:col + 1]
                    B.act(cl[0:64, h:h + 1], psum[ph][0:64, 128:129], AF.Abs, reads=[PS[ph]], writes=[R_cl])
                    B.ts("dve", cl[0:64, h:h + 1], cl[0:64, h:h + 1], e_t, None, ALU.max, reads=[R_cl, R_aT], writes=[R_cl])
                    s.op("dve", lambda e, a_=cl[0:64, h:h + 1]: e.reciprocal(a_, a_), [R_cl], [R_cl])
                    B.ts("dve", hx[ai][0:64, :], psum[ph][0:64, 0:128], cl[0:64, h:h + 1], None, ALU.mult,
                         reads=[PS[ph], R_cl], writes=[R_hx[ai]])
                    s.op("dve", lambda e, o_=bst[0:64, h * 6:(h + 1) * 6], i_=hx[ai][0:64, :]: e.bn_stats(o_, i_), [R_hx[ai]], [R_bst])
                    s.op("dve", lambda e, o_=bag[0:64, h * 2:(h + 1) * 2], i_=bst[0:64, h * 6:(h + 1) * 6]: e.bn_aggr(o_, i_), [R_bst], [R_bst])
                    B.ts("dve", rstd[0:64, h:h + 1], bag[0:64, h * 2 + 1:h * 2 + 2], EPS, None, ALU.add, reads=[R_bst], writes=[R_bst])
                    B.act(rstd[0:64, h:h + 1], rstd[0:64, h:h + 1], AF.Sqrt, reads=[R_bst], writes=[R_bst])
                    s.op("dve", lambda e, a_=rstd[0:64, h:h + 1]: e.reciprocal(a_, a_), [R_bst], [R_bst])
                    B.ts("dve", hnb3[0:64, h, :], hx[ai][0:64, :], bag[0:64, h * 2:h * 2 + 1], rstd[0:64, h:h + 1], ALU.subtract, ALU.mult,
                         reads=[R_hx[ai], R_bst], writes=[R_hnb])
                B.stt("dve", Cst3[:, h, :], Cst3[:, h, :], al, psum[pu][:, 0:129], ALU.mult, ALU.add,
                      reads=[R_C[h], R_aT, PS[pu]], writes=[R_C[h]])
            if own:
                pt = 6 + j % 2
                ptb = psum[pt][:, :].bitcast(BF16)
                for h in range(4):
                    s.op("pe", lambda e, o_=ptb[:, h * 64:(h + 1) * 64], i_=hnb3[0:64, h, :]: e.transpose(o_, i_, identb[0:64, 0:64]),
                         [R_hnb, R_const], [PS[pt]])
                for h in range(4):
                    B.stt("dve", ob3[:, h, tk], ptb[:, h * 64:(h + 1) * 64], ngs[:, h:h + 1], mo3[:, h, tk], ALU.mult, ALU.mult,
                          reads=[PS[pt], R_p4, R_qTg[gb]], writes=[R_obT[gb]])
        if own:
            go = g - 24
            B.dma(OB[:, :, go * 512:(go + 1) * 512].rearrange("h p n -> p h n"), ob3, reads=[R_obT[gb]], writes=[R_OB], q="sp")
    B.dma(oC[:, :], Cst, reads=R_C, final=True)


    B.barrier()
    R_s4 = B.R("s4")
    Kc = A.bf16(2064)
    Kst = A.f32(2048)
    Vst = A.f32(16 * 64)
    Vc = A.bf16(17 * 65)
    Vc3 = Vc.rearrange("p (b d) -> p b d", d=65)
    Qa = A.bf16(16)
    R_Kc = B.R("Kc"); R_Kst = B.R("Kst"); R_Vst = B.R("Vst"); R_Vc = B.R("Vc"); R_Qa = B.R("Qa")
    B.memset("pool", Kc[64:65, :], 1.0, writes=[R_Kc])
    B.memset("pool", Vc, 1.0, writes=[R_Vc])
    LFs = A.f32(17 * 8)
    LFs3 = LFs.rearrange("p (b h) -> p b h", h=8)
    Fs = A.f32(17 * 8)
    Fs3 = Fs.rearrange("p (b h) -> p b h", h=8)
    Ts = [A.f32(17 * 8) for _ in range(2)]
    bKs = A.f32(8 * 17)
    bKs3 = bKs.rearrange("p (h b) -> p h b", b=17)
    fqs = A.bf16(16)
    Pbs = [A.bf16(16) for _ in range(2)]
    R_Pbs = [B.R("Pbs0"), B.R("Pbs1")]
    dts = A.f32(16)
    onfs = A.f32(16)
    rdens = A.f32(16)
    oTs = A.bf16(16)
    R_OAs = B.R("OAs"); R_OBs = B.R("OBs"); R_QFs = B.R("QFs")
    for e in range(2):
        B.memset("dve", LFs, 0.0, writes=[R_s4])
        B.dma(LFs[:, 0:128], lfc_d[e], writes=[R_s4])
        B.cp("dve", LFs3[0:16, 16, :], lfn[0:16, e * 8:(e + 1) * 8], reads=[R_smp, R_s4], writes=[R_s4])
        B.mm(psum[0][:, 0:136], trif, LFs, True, True, reads=[R_const, R_s4], writes=[PS[0]])
        B.mm(psum[1][:, 0:136], ones_f, LFs, True, True, reads=[R_const, R_s4], writes=[PS[1]])
        B.cp("dve", Fs, psum[0][:, 0:136], reads=[PS[0]], writes=[R_s4])
        B.cp("dve", Ts[0], psum[1][:, 0:136], reads=[PS[1]], writes=[R_s4])
        B.tt("dve", Fs, Fs, Ts[0], ALU.subtract, reads=[R_s4], writes=[R_s4])
        cur = 0
        sh = 1
        while sh < 17:
            a3 = Ts[cur].rearrange("p (b h) -> p b h", h=8)
            n3 = Ts[1 - cur].rearrange("p (b h) -> p b h", h=8)
            B.cp("dve", n3[:, 0:sh, :], a3[:, 0:sh, :], reads=[R_s4], writes=[R_s4])
            B.tt("dve", n3[:, sh:17, :], a3[:, sh:17, :], a3[:, 0:17 - sh, :], ALU.add, reads=[R_s4], writes=[R_s4])
            cur = 1 - cur
            sh *= 2
        Ti3 = Ts[cur].rearrange("p (b h) -> p b h", h=8)
        B.tt("dve", Fs, Fs, Ts[cur], ALU.add, reads=[R_s4], writes=[R_s4])
        for h in range(8):
            B.ts("dve", Fs3[:, :, h], Fs3[:, :, h], Ti3[:, 15, h:h + 1], None, ALU.subtract, reads=[R_s4], writes=[R_s4])
            B.ts("dve", bKs3[:, h, :], Fs3[:, :, h], -1.0, None, ALU.mult, reads=[R_s4], writes=[R_s4])
        s.op("pe", lambda e_, o_=psum[2][0:8, 0:16], i_=Fs3[0:16, 16, :]: e_.transpose(o_, i_, identf[0:16, 0:16]), [R_s4, R_const], [PS[2]])
        B.cp("act", fqs[0:8, :], psum[2][0:8, 0:16], reads=[PS[2]], writes=[R_s4])
        B.dma(QFs[e], fqs[0:8, :], reads=[R_s4], writes=[R_QFs])
        for h in range(8):
            B.dma(Kst[0:64, :], kcT_d[e, h], writes=[R_Kst], q="sp")
            B.dma(Vst.rearrange("p (b d) -> p b d", d=64), vc_d[e, h], writes=[R_Vst], q="pool")
            B.cp("dve", Kc[0:64, 0:2048], Kst[0:64, :], reads=[R_Kst], writes=[R_Kc])
            B.dma(Kc[0:64, 2048:2064], KTs[h][:, e * 16:(e + 1) * 16], reads=[R_KTs], writes=[R_Kc], q="sp")
            B.cp("act", Vc3[:, 0:16, 0:64], Vst.rearrange("p (b d) -> p b d", d=64), reads=[R_Vst], writes=[R_Vc])
            B.dma(Vc3[0:16, 16, :], VAs[e][:, h * 65:(h + 1) * 65], reads=[R_VAs], writes=[R_Vc], q="pool")
            B.dma(Qa[0:64, :], QTs[h][:, e * 16:(e + 1) * 16], reads=[R_QTs], writes=[R_Qa], q="sp")
            B.dma(Qa[64:65, :], QFs[e][h:h + 1, :], reads=[R_QFs], writes=[R_Qa], q="sp")
            for kb in range(17):
                sbk = kb % 4
                pb_ = kb % 2
                if kb < 16:
                    B.mm(psum[sbk][:, 0:16], Kc[0:65, kb * 128:(kb + 1) * 128], Qa[0:65, :], True, True, reads=[R_Kc, R_Qa], writes=[PS[sbk]])
                    B.act(Pbs[pb_][:, :], psum[sbk][:, 0:16], AF.Exp, bias=bKs3[:, h, kb:kb + 1], reads=[PS[sbk], R_s4], writes=[R_Pbs[pb_]])
                    B.mm(psum[4][0:65, 0:16], Vc3[:, kb, :], Pbs[pb_][:, :], kb == 0, False, reads=[R_Vc, R_Pbs[pb_]], writes=[PS[4]])
                else:
                    B.mm(psum[sbk][0:16, 0:16], Kc[0:65, 2048:2064], Qa[0:65, :], True, True, reads=[R_Kc, R_Qa], writes=[PS[sbk]])
                    B.tt("dve", dts[0:16, :], psum[sbk][0:16, 0:16], cmask[0:16, 0:16], ALU.add, reads=[PS[sbk], R_const], writes=[R_s4])
                    B.act(Pbs[pb_][0:16, :], dts[0:16, :], AF.Exp, bias=bKs3[0:16, h, 16:17], reads=[R_s4], writes=[R_Pbs[pb_]])
                    B.mm(psum[4][0:65, 0:16], Vc3[0:16, 16, :], Pbs[pb_][0:16, :], False, True, reads=[R_Vc, R_Pbs[pb_]], writes=[PS[4]])
            B.cp("act", onfs[0:64, :], psum[4][0:64, 0:16], reads=[PS[4]], writes=[R_s4])
            s.op("dve", lambda e_, o_=rdens[64:65, :], i_=psum[4][64:65, 0:16]: e_.reciprocal(o_, i_), [PS[4]], [R_s4])
            B.mm(psum[5][0:64, 0:16], ones_f[64:65, 0:64], rdens[64:65, :], True, True, reads=[R_const, R_s4], writes=[PS[5]])
            B.tt("dve", oTs[0:64, :], onfs[0:64, :], psum[5][0:64, 0:16], ALU.mult, reads=[R_s4, PS[5]], writes=[R_s4])
            B.dma(OAs[h][:, e * 16:(e + 1) * 16], oTs[0:64, :], reads=[R_s4], writes=[R_OAs], q="pool")
    Cs = A.f32(4 * 129)
    Cs3 = Cs.rearrange("p (h d) -> p h d", d=129)
    m0t = A.f32(1)
    bcs = [A.f32(16), A.f32(16)]
    ggs = A.f32(16)
    ees = A.f32(16)
    gmx = A.f32(1)
    mus = A.f32(1)
    nmus = A.f32(1)
    als = A.f32(1)
    mos = A.f32(1)
    aTs = A.f32(4)
    eTs = A.f32(4)
    abcs = A.f32(4)
    dgs = A.f32(4)
    kts = A.bf16(512)
    kts3 = kts.rearrange("p (h d) -> p h d", d=128)
    vts = A.bf16(520)
    vts3 = vts.rearrange("p (h d) -> p h d", d=130)
    qTs_ = A.bf16(4 * 16)
    kTs_ = A.bf16(4 * 16)
    mos_ = A.bf16(4 * 16)
    qTs3 = qTs_.rearrange("p (h n) -> p h n", n=16)
    kTs3 = kTs_.rearrange("p (h n) -> p h n", n=16)
    mos3 = mos_.rearrange("p (h n) -> p h n", n=16)
    obs = A.bf16(4 * 16)
    obs3 = obs.rearrange("p (h n) -> p h n", n=16)
    R_m4 = B.R("m4s")
    for e in range(2):
        ec = slice(e * 16, (e + 1) * 16)
        B.dma(Cs, Cs0_d[e], writes=[R_m4])
        B.dma(m0t[0:4, :], m0_d[e], writes=[R_m4])
        B.dma(kts[0:16, :], MKs[e], reads=[R_MKs], writes=[R_m4])
        B.dma(vts[0:16, :], MVs[e], reads=[R_MVs], writes=[R_m4])
        B.dma(qTs3, MQTs[:, :, ec].rearrange("h p n -> p h n"), reads=[R_MQTs], writes=[R_m4])
        B.dma(kTs3, MKTs[:, :, ec].rearrange("h p n -> p h n"), reads=[R_MQTs], writes=[R_m4])
        B.dma(mos3, SGs.rearrange("p (c n) -> p c n", n=32)[:, 0:4, ec], reads=[R_SGs], writes=[R_m4])
        B.cp("dve", bcs[0][0:4, :], gfs[0:4, ec], reads=[R_smp], writes=[R_m4])
        cur = 0
        sh = 1
        while sh < 16:
            B.cp("dve", bcs[1 - cur][0:4, 0:sh], bcs[cur][0:4, 0:sh], reads=[R_m4], writes=[R_m4])
            B.tt("dve", bcs[1 - cur][0:4, sh:16], bcs[cur][0:4, sh:16], bcs[cur][0:4, 0:16 - sh], ALU.add, reads=[R_m4], writes=[R_m4])
            cur = 1 - cur
            sh *= 2
        bb = bcs[cur]
        B.tt("dve", ggs[0:4, :], gis[0:4, ec], bb[0:4, :], ALU.subtract, reads=[R_smp, R_m4], writes=[R_m4])
        s.op("dve", lambda e_: e_.tensor_reduce(gmx[0:4, :], ggs[0:4, :], AX.X, ALU.max), [R_m4], [R_m4])
        B.tt("dve", mus[0:4, :], gmx[0:4, :], m0t[0:4, :], ALU.max, reads=[R_m4], writes=[R_m4])
        B.ts("dve", nmus[0:4, :], mus[0:4, :], -1.0, None, ALU.mult, reads=[R_m4], writes=[R_m4])
        B.tt("dve", als[0:4, :], m0t[0:4, :], mus[0:4, :], ALU.subtract, reads=[R_m4], writes=[R_m4])
        B.act(als[0:4, :], als[0:4, :], AF.Exp, reads=[R_m4], writes=[R_m4])
        B.tt("dve", mos[0:4, :], mus[0:4, :], bb[0:4, 15:16], ALU.add, reads=[R_m4], writes=[R_m4])
        B.dma(oms[e], mos[0:4, :], reads=[R_m4], final=True)
        B.act(ggs[0:4, :], ggs[0:4, :], AF.Exp, bias=nmus[0:4, 0:1], reads=[R_m4], writes=[R_m4])
        B.act(ees[0:4, :], bb[0:4, :], AF.Exp, bias=nmus[0:4, 0:1], scale=-1.0, reads=[R_m4], writes=[R_m4])
        s.op("pe", lambda e_: e_.transpose(psum[0][0:16, 0:4], ggs[0:4, :], identf[0:4, 0:4]), [R_m4, R_const], [PS[0]])
        s.op("pe", lambda e_: e_.transpose(psum[1][0:16, 0:4], ees[0:4, :], identf[0:4, 0:4]), [R_m4, R_const], [PS[1]])
        B.cp("dve", aTs[0:16, :], psum[0][0:16, 0:4], reads=[PS[0]], writes=[R_m4])
        B.cp("dve", eTs[0:16, :], psum[1][0:16, 0:4], reads=[PS[1]], writes=[R_m4])
        B.ts("dve", dgs[0:4, :], identf[0:4, 0:4], als[0:4, 0:1], None, ALU.mult, reads=[R_m4, R_const], writes=[R_m4])
        B.mm(psum[2][:, 0:4], ones_f[0:4, :], dgs[0:4, :], True, True, reads=[R_m4, R_const], writes=[PS[2]])
        B.cp("dve", abcs, psum[2][:, 0:4], reads=[PS[2]], writes=[R_m4])
        for h in range(4):
            a_s = aTs[0:16, h:h + 1]
            al = abcs[:, h:h + 1]
            B.act(ka[0][0:16, :], kts3[0:16, h, :], AF.Copy, scale=a_s, reads=[R_m4], writes=[R_ka[0]])
            B.mm(psum[3][:, 0:130], ka[0][0:16, :], vts3[0:16, h, :], True, True, reads=[R_ka[0], R_m4], writes=[PS[3]])
            B.mm(psum[4][0:16, 0:16], kTs3[:, h, :], qTs3[:, h, :], True, True, reads=[R_m4], writes=[PS[4]])
            B.stt("dve", AT[0][0:16, 0:16], psum[4][0:16, 0:16], a_s, trif[0:16, 0:16], ALU.mult, ALU.mult,
                  reads=[PS[4], R_m4, R_const], writes=[R_AT[0]])
            B.ts("dve", Cab[0][:, 0:129], Cs3[:, h, :], al, None, ALU.mult, reads=[R_m4], writes=[R_Cab[0]])
            B.mm(psum[5][0:16, 0:130], AT[0][0:16, 0:16], vts3[0:16, h, :], True, False, reads=[R_AT[0], R_m4], writes=[PS[5]])
            B.mm(psum[5][0:16, 0:130], qTs3[:, h, :], Cab[0], False, True, reads=[R_m4, R_Cab[0]], writes=[PS[5]])
            B.act(cl[0:16, h:h + 1], psum[5][0:16, 128:129], AF.Abs, reads=[PS[5]], writes=[R_cl])
            B.ts("dve", cl[0:16, h:h + 1], cl[0:16, h:h + 1], eTs[0:16, h:h + 1], None, ALU.max, reads=[R_cl, R_m4], writes=[R_cl])
            s.op("dve", lambda e_, a_=cl[0:16, h:h + 1]: e_.reciprocal(a_, a_), [R_cl], [R_cl])
            B.ts("dve", hx[0][0:16, :], psum[5][0:16, 0:128], cl[0:16, h:h + 1], None, ALU.mult, reads=[PS[5], R_cl], writes=[R_hx[0]])
            s.op("dve", lambda e_, o_=bst[0:16, h * 6:(h + 1) * 6], i_=hx[0][0:16, :]: e_.bn_stats(o_, i_), [R_hx[0]], [R_bst])
            s.op("dve", lambda e_, o_=bag[0:16, h * 2:(h + 1) * 2], i_=bst[0:16, h * 6:(h + 1) * 6]: e_.bn_aggr(o_, i_), [R_bst], [R_bst])
            B.ts("dve", rstd[0:16, h:h + 1], bag[0:16, h * 2 + 1:h * 2 + 2], EPS, None, ALU.add, reads=[R_bst], writes=[R_bst])
            B.act(rstd[0:16, h:h + 1], rstd[0:16, h:h + 1], AF.Sqrt, reads=[R_bst], writes=[R_bst])
            s.op("dve", lambda e_, a_=rstd[0:16, h:h + 1]: e_.reciprocal(a_, a_), [R_bst], [R_bst])
            B.ts("dve", hnb3[0:16, h, :], hx[0][0:16, :], bag[0:16, h * 2:h * 2 + 1], rstd[0:16, h:h + 1], ALU.subtract, ALU.mult,
                 reads=[R_hx[0], R_bst], writes=[R_hnb])
            B.stt("dve", Cs3[:, h, :], Cs3[:, h, :], al, psum[3][:, 0:129], ALU.mult, ALU.add, reads=[R_m4, PS[3]], writes=[R_m4])
        ptb = psum[6][:, :].bitcast(BF16)
        for h in range(4):
            s.op("pe", lambda e_, o_=ptb[:, h * 16:(h + 1) * 16], i_=hnb3[0:16, h, :]: e_.transpose(o_, i_, identb[0:16, 0:16]),
                 [R_hnb, R_const], [PS[6]])
        for h in range(4):
            B.stt("dve", obs3[:, h, :], ptb[:, h * 16:(h + 1) * 16], ngs[:, h:h + 1], mos3[:, h, :], ALU.mult, ALU.mult,
                  reads=[PS[6], R_p4, R_m4], writes=[R_m4])
        B.dma(OBs[:, :, ec].rearrange("h p n -> p h n"), obs3, reads=[R_m4], writes=[R_OBs], q="sp")
        B.dma(oCs[e], Cs, reads=[R_m4], final=True)

    B.barrier()
    A.off = stage_off
    wab = A.bf16(4 * D)
    wbb = A.bf16(4 * D)
    wob = A.bf16(8 * D)
    wab3 = wab.rearrange("p (k n) -> p k n", n=D)
    wbb3 = wbb.rearrange("p (k n) -> p k n", n=D)
    wob3 = wob.rearrange("p (k n) -> p k n", n=D)
    R_w5 = B.R("w5")
    B.dma(wab3, w_ba, writes=[R_w5], q="pool")
    B.dma(wbb3, w_bb, writes=[R_w5], q="pool")
    for k0 in range(0, 8, 2):
        B.dma(wob3[:, k0:k0 + 2, :], w_o[:, k0:k0 + 2, :], writes=[R_w5], q="pool")
    x5 = [A.f32(8 * TN) for _ in range(2)]
    R_x5 = [B.R("x5_0"), B.R("x5_1")]
    oa5 = [A.bf16(4 * TN) for _ in range(2)]
    ob5 = [A.bf16(4 * TN) for _ in range(2)]
    sg5 = [A.bf16(20 * TN) for _ in range(2)]
    R_in5 = [B.R("in5_0"), B.R("in5_1")]
    m5 = A.bf16(8 * TN)
    m53 = m5.rearrange("p (k n) -> p k n", n=TN)
    R_m5 = B.R("m5")
    t1 = [A.f32(TN) for _ in range(2)]
    R_t1 = [B.R("t1_0"), B.R("t1_1")]
    sq5 = A.bf16(8 * TN)
    rb5 = A.bf16(8 * TN)
    R_sq5 = B.R("sq5")
    st1 = A.f32(TN)
    st2 = A.f32(TN)
    st3 = A.f32(TN)
    R_st5 = B.R("st5")
    R_X2 = B.R("X2")
    tiles5 = [(to, TN, 0) for to in range(16)] + [(16, 32, [(0, 16, 1), (16, 32, 2)])]

    def load5(ti):
        to, N, cond = tiles5[ti]
        b = ti % 2
        oa3 = oa5[b].rearrange("p (c n) -> p c n", n=TN)
        if to == 16:
            B.dma(x5[b].rearrange("p (k n) -> p k n", n=TN)[:, :, 0:32], X1[NT][:, :, 0:32], reads=[R_X1], writes=[R_x5[b]], q="sp")
            for hh in range(2):
                B.dma(oa3[hh * 64:(hh + 1) * 64, :, 0:32], OAs[hh::2].rearrange("c d n -> d c n"), reads=[R_OAs], writes=[R_in5[b]], q="pool")
            B.dma(ob5[b].rearrange("p (c n) -> p c n", n=TN)[:, :, 0:32], OBs.rearrange("h p n -> p h n"), reads=[R_OBs], writes=[R_in5[b]], q="pool")
            B.dma(sg5[b].rearrange("p (c n) -> p c n", n=TN)[:, :, 0:32], SGs.rearrange("p (c n) -> p c n", n=32), reads=[R_SGs], writes=[R_in5[b]], q="sp")
            return
        B.dma(x5[b].rearrange("p (k n) -> p k n", n=TN), X1[OWN0 + to], reads=[R_X1], writes=[R_x5[b]], q="sp")
        for hh in range(2):
            B.dma(oa3[hh * 64:(hh + 1) * 64, :, :], OA[hh::2][:, :, to * TN:(to + 1) * TN].rearrange("c d n -> d c n"),
                  reads=[R_OA], writes=[R_in5[b]], q="pool")
        B.dma(ob5[b].rearrange("p (c n) -> p c n", n=TN), OB[:, :, to * TN:(to + 1) * TN].rearrange("h p n -> p h n"),
              reads=[R_OB], writes=[R_in5[b]], q="pool")
        B.dma(sg5[b], SG[to], reads=[R_SG], writes=[R_in5[b]], q="sp")

    load5(0)
    pi = 0
    for ti, (to, N, cond) in enumerate(tiles5):
        b = ti % 2
        if ti + 1 < len(tiles5):
            load5(ti + 1)
        x3 = x5[b].rearrange("p (k n) -> p k n", n=TN)
        oa3 = oa5[b].rearrange("p (c n) -> p c n", n=TN)
        ob3 = ob5[b].rearrange("p (c n) -> p c n", n=TN)
        sg3 = sg5[b].rearrange("p (c n) -> p c n", n=TN)
        q3 = sq5.rearrange("p (k n) -> p k n", n=TN)
        rb3 = rb5.rearrange("p (k n) -> p k n", n=TN)
        groups = [(0, N, cond)] if isinstance(cond, int) else cond
        B.act(x3[:, :, 0:N], x3[:, :, 0:N], AF.Copy, scale=ALPHA, reads=[R_x5[b]], writes=[R_x5[b]])
        for d in range(8):
            pa = pi % 8
            pb = (pi + 1) % 8
            pi += 2
            for c in range(4):
                B.mm(psum[pa][:, 0:N], wab3[:, c, d * 128:(d + 1) * 128], oa3[:, c, 0:N], c == 0, c == 3,
                     reads=[R_w5, R_in5[b]], writes=[PS[pa]])
            for c in range(4):
                B.mm(psum[pb][:, 0:N], wbb3[:, c, d * 128:(d + 1) * 128], ob3[:, c, 0:N], c == 0, c == 3,
                     reads=[R_w5, R_in5[b]], writes=[PS[pb]])
            ti_ = d % 2
            B.tt("dve", t1[ti_][:, 0:N], psum[pa][:, 0:N], sg3[:, 4 + d, 0:N], ALU.mult, reads=[PS[pa], R_in5[b]], writes=[R_t1[ti_]])
            B.tt("dve", t1[1 - ti_][:, 0:N], psum[pb][:, 0:N], sg3[:, 12 + d, 0:N], ALU.mult, reads=[PS[pb], R_in5[b]], writes=[R_t1[1 - ti_]])
            B.tt("dve", m53[:, d, 0:N], t1[0][:, 0:N], t1[1][:, 0:N], ALU.add, reads=[R_t1[0], R_t1[1]], writes=[R_m5])
        for d in range(8):
            pd = pi % 8
            pi += 1
            for c in range(8):
                B.mm(psum[pd][:, 0:N], wob3[:, c, d * 128:(d + 1) * 128], m53[:, c, 0:N], c == 0, c == 7,
                     reads=[R_w5, R_m5], writes=[PS[pd]])
            for (c0, c1, ci) in groups:
                B.stt("dve", x3[:, d, c0:c1], psum[pd][:, c0:c1], modv3[:, 40 + d, ci:ci + 1], x3[:, d, c0:c1],
                      ALU.mult, ALU.add, reads=[PS[pd], R_mod], writes=[R_x5[b]])
            B.act(q3[:, d, 0:N], x3[:, d, 0:N], AF.Square, reads=[R_x5[b]], writes=[R_sq5])
            B.act(rb3[:, d, 0:N], x3[:, d, 0:N], AF.Copy, reads=[R_x5[b]], writes=[R_sq5])
        p1 = pi % 8
        p2 = (pi + 1) % 8
        pi += 2
        for d in range(8):
            B.mm(psum[p1][:, 0:N], ones_b, rb3[:, d, 0:N], d == 0, d == 7, reads=[R_sq5, R_const], writes=[PS[p1]])
        for d in range(8):
            B.mm(psum[p2][:, 0:N], ones_b, q3[:, d, 0:N], d == 0, d == 7, reads=[R_sq5, R_const], writes=[PS[p2]])
        B.ts("dve", st1[:, 0:N], psum[p1][:, 0:N], -1.0 / D, None, ALU.mult, reads=[PS[p1]], writes=[R_st5])
        B.tt("dve", st2[:, 0:N], st1[:, 0:N], st1[:, 0:N], ALU.mult, reads=[R_st5], writes=[R_st5])
        B.stt("dve", st3[:, 0:N], psum[p2][:, 0:N], 1.0 / D, st2[:, 0:N], ALU.mult, ALU.subtract, reads=[PS[p2], R_st5], writes=[R_st5])
        B.ts("dve", st3[:, 0:N], st3[:, 0:N], EPS, None, ALU.add, reads=[R_st5], writes=[R_st5])
        B.act(st3[:, 0:N], st3[:, 0:N], AF.Sqrt, reads=[R_st5], writes=[R_st5])
        s.op("dve", lambda e, a=st3[:, 0:N]: e.reciprocal(a, a), [R_st5], [R_st5])
        for d in range(8):
            B.tt("dve", x3[:, d, 0:N], x3[:, d, 0:N], st1[:, 0:N], ALU.add, reads=[R_x5[b], R_st5], writes=[R_x5[b]])
            B.tt("dve", x3[:, d, 0:N], x3[:, d, 0:N], st3[:, 0:N], ALU.mult, reads=[R_x5[b], R_st5], writes=[R_x5[b]])
            B.ts("dve", x3[:, d, 0:N], x3[:, d, 0:N], lng3[:, 1, d:d + 1], lnb3[:, 1, d:d + 1], ALU.mult, ALU.add,
                 reads=[R_x5[b], R_const], writes=[R_x5[b]])
        B.dma(X2[to][:, :, 0:N], x3[:, :, 0:N], reads=[R_x5[b]], writes=[R_X2], q="pool")

    def load2(tid, dst, res):
        Nn = TN if tid < 16 else 32
        B.dma(dst, X2[tid][:, :, 0:Nn], reads=[R_X2], writes=[res])

    def store2(tid, src, res):
        if tid < 16:
            B.dma(yT[:, :, tid * TN:(tid + 1) * TN], src, reads=[res], final=True, q="pool")
        else:
            B.dma(ysT[:, :, :], src, reads=[res], final=True, q="pool")

    tiles2 = [(t, TN, 0) for t in range(16)] + [(16, 32, [(0, 16, 1), (16, 32, 2)])]
    ffn_stage(1, w_gu[1], w_dn[1], tiles2, 48, 56, 64, 2, load2, store2, "f2")

    s.emit(B.out_dmas)
    es.close()
    return nc


def _fm(a):
    F, N = a.shape
    return np.ascontiguousarray(a.reshape(F // 128, 128, N).transpose(1, 0, 2))


def kernel(**inp):
    f32 = np.float32
    g = {k: np.asarray(v) for k, v in inp.items()}
    nc = build()
    in_maps = []
    w_ada = _fm(g["w_ada"][0])
    b_ada = np.ascontiguousarray(g["b_ada"][0].reshape(72, 128).T)
    wgu = [_fm(g["ffn1_w_gu"][0]), _fm(g["ffn2_w_gu"][0])]
    wdn = [_fm(g["ffn1_w_down"][0]), _fm(g["ffn2_w_down"][0])]
    w_in = _fm(g["w_in"][0])
    b_in = g["b_in"][0]
    fmc = ([512 + 128 * c for c in range(4)] + [2056 + 128 * c for c in range(4)] + [128 * c for c in range(4)] +
           [1544 + 128 * c for c in range(4)] + [3088 + 128 * c for c in range(4)] + [3600 + 128 * c for c in range(8)] +
           [4624 + 128 * c for c in range(8)])
    b_in_fm = np.zeros((128, 40), f32)
    for j, c0 in enumerate(fmc):
        b_in_fm[:, j] = b_in[c0:c0 + 128]
    b_in_fm[0:4, 36] = b_in[3080:3084]
    b_in_fm[0:4, 37] = b_in[3084:3088]
    tm = np.concatenate([b_in[1024:1544], b_in[2568:3080]])
    b_in_bc = np.ascontiguousarray(np.broadcast_to(tm[None, :], (128, 1032)))
    ident = np.eye(128, dtype=f32)
    w_ba = _fm(g["w_branch_a"][0])
    w_bb = _fm(g["w_branch_b"][0])
    w_o = _fm(g["w_out"][0])
    tri = np.triu(np.ones((128, 128), f32))
    pp = np.arange(128)
    segm = ((pp[:, None] // 32 == pp[None, :] // 32) & (pp[:, None] < pp[None, :])).astype(f32)
    ng = np.ascontiguousarray(g["mlstm_norm_g"][0].reshape(4, 128).T)
    cmask = np.where(np.arange(128)[:, None] <= np.arange(128)[None, :], 0.0, -BIG).astype(f32)
    ln_g = np.ascontiguousarray(g["ln_g"][0].reshape(3, 8, 128).transpose(2, 0, 1))
    ln_b = np.ascontiguousarray(g["ln_b"][0].reshape(3, 8, 128).transpose(2, 0, 1))
    conv_w = np.ascontiguousarray(g["conv_w"][0].reshape(4, 8, 128).transpose(2, 1, 0))
    conv_b = np.ascontiguousarray(g["conv_b"][0].reshape(8, 128).T)
    ck = g["cache_fox_k"][0]; cvv = g["cache_fox_v"][0]; clf = g["cache_fox_logf"][0]
    sC = g["state_mlstm_C"][0]; sn = g["state_mlstm_n"][0]; sm = g["state_mlstm_m"][0]; scv = g["state_conv"][0]
    for core in range(8):
        b, q = core // 4, core % 4
        es_ = [2 * core, 2 * core + 1]
        convs = np.ascontiguousarray(np.stack([scv[e].reshape(3, 8, 128) for e in es_], 0).transpose(3, 2, 0, 1))
        kcT = np.ascontiguousarray(np.stack([ck[e].transpose(1, 2, 0) for e in es_], 0))
        vc = np.ascontiguousarray(np.stack([cvv[e].reshape(16, 128, 8, 64).transpose(2, 1, 0, 3) for e in es_], 0))
        lfc = np.ascontiguousarray(np.stack([clf[e].reshape(16, 128, 8).transpose(1, 0, 2).reshape(128, 128) for e in es_], 0))
        Cs0 = np.zeros((2, 128, 4, 129), f32)
        for i_, e in enumerate(es_):
            Cs0[i_, :, :, 0:128] = sC[e].transpose(2, 0, 1)
            Cs0[i_, :, :, 128] = sn[e].T
        Cs0 = Cs0.reshape(2, 128, 516)
        m0c = np.ascontiguousarray(np.stack([sm[e].reshape(4, 1) for e in es_], 0))
        nreal = 4096 * (q + 1)
        xt = np.zeros((D, S), f32)
        xt[:, S - nreal:] = g["x_prompt"][b, :nreal, :].T
        keep = np.zeros((128, NT), f32)
        keep[:, NT - 16 * (q + 1):] = 1.0
        xs = g["x_sample"][2 * core:2 * core + 2].reshape(32, D).T
        c3 = np.stack([g["c_prompt"][b], g["c_sample"][2 * core], g["c_sample"][2 * core + 1]], axis=1)
        in_maps.append({
            "xT": _fm(xt), "xsT": _fm(np.ascontiguousarray(xs)), "keep": keep, "cT": _fm(np.ascontiguousarray(c3)),
            "w_ada": w_ada, "b_ada": b_ada, "w_gu1": wgu[0], "w_gu2": wgu[1], "w_dn1": wdn[0], "w_dn2": wdn[1],
            "w_in": w_in, "b_in_fm": b_in_fm, "b_in_bc": b_in_bc, "ln_g": ln_g, "ln_b": ln_b,
            "conv_w": conv_w, "conv_b": conv_b, "ident": ident, "tri": tri, "cmask": cmask, "segm": segm, "ng": ng, "w_ba": w_ba, "w_bb": w_bb, "w_o": w_o, "convs": convs, "kcT": kcT, "vc": vc, "lfc": lfc, "Cs0": Cs0, "m0c": m0c,
        })
    res = run_bass_kernel_spmd(nc, in_maps, core_ids=list(range(8)))
    R = res.results
    Bn, DB, L = 2, 16, 16
    y_p = np.zeros((Bn, S, D), f32)
    for core in range(8):
        b, q = core // 4, core % 4
        yt = R[core]["yT"]
        y_p[b, q * 4096:(q + 1) * 4096, :] = yt.transpose(2, 1, 0).reshape(4096, D)
    fk = np.zeros((1, Bn, S, 8, 64), f32)
    fv = np.zeros((1, Bn, S, 8, 64), f32)
    fl = np.zeros((1, Bn, S, 8), f32)
    cv = np.zeros((1, Bn, 3, 1024), f32)
    for core in range(8):
        b, q = core // 4, core % 4
        sl = slice(q * 4096, (q + 1) * 4096)
        fk[0, b, sl] = R[core]["okT"].transpose(2, 1, 0).reshape(4096, 8, 64)
        fv[0, b, sl] = R[core]["ov"].reshape(4096, 8, 64)
        fl[0, b, sl] = R[core]["olf"]
        if q == 3:
            cv[0, b] = R[core]["oconv"].transpose(2, 1, 0).reshape(3, 1024)
    mC = np.zeros((1, Bn, 4, 128, 128), f32)
    mn = np.zeros((1, Bn, 4, 128), f32)
    mm_ = np.zeros((1, Bn, 4), f32)
    for b in range(Bn):
        c = R[4 * b + 3]["oC"].reshape(128, 4, 129)
        mC[0, b] = c[:, :, 0:128].transpose(1, 2, 0)
        mn[0, b] = c[:, :, 128].T
        mm_[0, b] = R[4 * b + 3]["om"][31::32, 0]
    y_s = np.zeros((DB, L, D), f32)
    sk = np.zeros((1, DB, L, 8, 64), f32); sv = np.zeros((1, DB, L, 8, 64), f32); sl_ = np.zeros((1, DB, L, 8), f32)
    sCo = np.zeros((1, DB, 4, 128, 128), f32); sno = np.zeros((1, DB, 4, 128), f32); smo = np.zeros((1, DB, 4), f32)
    sco = np.zeros((1, DB, 3, 1024), f32)
    for core in range(8):
        r = R[core]
        ys = r["ysT"].transpose(2, 1, 0).reshape(32, D)
        ks = r["oksT"].transpose(2, 1, 0).reshape(32, 8, 64)
        for i_ in range(2):
            e = 2 * core + i_
            y_s[e] = ys[i_ * 16:(i_ + 1) * 16]
            sk[0, e] = ks[i_ * 16:(i_ + 1) * 16]
            sv[0, e] = r["ovs"][i_].reshape(16, 8, 64)
            sl_[0, e] = r["olfs"][i_]
            c = r["oCs"][i_].reshape(128, 4, 129)
            sCo[0, e] = c[:, :, 0:128].transpose(1, 2, 0)
            sno[0, e] = c[:, :, 128].T
            smo[0, e] = r["oms"][i_][:, 0]
            sco[0, e] = r["ocs"][:, :, i_, :].transpose(2, 1, 0).reshape(3, 1024)
    outs = [y_p, y_s,
            fk, fv, fl,
            mC, mn, mm_,
            cv,
            sk, sv, sl_, sCo, sno, smo, sco]
    return tuple(outs)
```

```python
import numpy as np
import concourse.bass as bass
import concourse.mybir as mybir
from concourse.bass_utils import run_bass_kernel_spmd

F32 = mybir.dt.float32
BF16 = mybir.dt.bfloat16
ALU = mybir.AluOpType
AF = mybir.ActivationFunctionType
AX = mybir.AxisListType

D = 1024
S = 16384
NT = 64
TN = 256
OWN0 = 48
DFF = 2816
NFF = 22
DIN = 5648
ALPHA = 2.0 ** 0.25
EPS = 1e-5
BIG = 30000.0


class Res:
    __slots__ = ("name", "w", "r")

    def __init__(self, name):
        self.name = name
        self.w = None
        self.r = []


class Op:
    __slots__ = ("id", "eng", "fn", "deps", "dma", "sig", "need")

    def __init__(self, i, eng, fn, dma):
        self.id = i
        self.eng = eng
        self.fn = fn
        self.deps = set()
        self.dma = dma
        self.sig = None
        self.need = False


class Sched:
    ENGS = ("pe", "act", "dve", "pool", "sp")

    def __init__(self, nc):
        self.nc = nc
        self.ops = []
        self.last_barrier = None

    def op(self, eng, fn, reads=(), writes=(), dma=False):
        o = Op(len(self.ops), eng, fn, dma)
        for r in reads:
            if r.w is not None:
                o.deps.add(r.w)
        for w in writes:
            if w.w is not None:
                o.deps.add(w.w)
            for x in w.r:
                o.deps.add(x)
        for r in reads:
            r.r.append(o.id)
        for w in writes:
            w.w = o.id
            w.r = []
        o.deps.discard(o.id)
        self.ops.append(o)
        return o

    def emit(self, out_dma_ops):
        nc = self.nc
        ops = self.ops
        for o in ops:
            best = {}
            keep = set()
            for d in o.deps:
                p = ops[d]
                if p.dma:
                    keep.add(d)
                    continue
                if p.eng == "pe" and o.eng == "pe" and not o.dma:
                    continue
                if d > best.get(p.eng, -1):
                    best[p.eng] = d
            keep.update(best.values())
            o.deps = keep
            for d in keep:
                ops[d].need = True
        for d in out_dma_ops:
            ops[d].need = True
        import contextlib
        es = contextlib.ExitStack()
        NDS = 12
        csem = {e: es.enter_context(nc.semaphore("c_" + e)) for e in ("pe", "act", "dve", "pool")}
        dsem = {e: [es.enter_context(nc.semaphore("d_%s%d" % (e, i))) for i in range(NDS)] for e in ("sp", "pool", "act")}
        ccount = {e: 0 for e in csem}
        dcount = {e: [0] * NDS for e in dsem}
        drr = {e: 0 for e in dsem}
        streams = {e: [] for e in self.ENGS}
        known = {e: {} for e in self.ENGS}
        for o in ops:
            want = {}
            for d in o.deps:
                p = ops[d]
                if p.sig is None:
                    continue
                sem, val = p.sig
                if val > want.get(id(sem), (None, 0))[1]:
                    want[id(sem)] = (sem, val)
            waits = []
            for k_, (sem, val) in want.items():
                if known[o.eng].get(k_, 0) >= val:
                    continue
                known[o.eng][k_] = val
                waits.append((sem, val))
            if o.dma:
                i = drr[o.eng]
                drr[o.eng] = (i + 1) % NDS
                sem = dsem[o.eng][i]
                prev = dcount[o.eng][i]
                if prev > 0 and known[o.eng].get(id(sem), 0) < prev:
                    known[o.eng][id(sem)] = prev
                    waits.append((sem, prev))
                dcount[o.eng][i] = prev + 16
                o.sig = (sem, prev + 16)
                inc = 16
            elif o.need:
                ccount[o.eng] += 1
                o.sig = (csem[o.eng], ccount[o.eng])
                inc = 1
            else:
                inc = 0
            streams[o.eng].append((o, waits, inc))
        finals = [ops[d].sig for d in out_dma_ops]
        with es, nc.Block() as block:
            def run(engname):
                def body(eng):
                    for (o, waits, inc) in streams[engname]:
                        for (sem, val) in waits:
                            eng.wait_ge(sem, val)
                        ins = o.fn(eng)
                        if inc:
                            ins.then_inc(o.sig[0], inc)
                    if engname == "sp":
                        for (sem, val) in finals:
                            eng.wait_ge(sem, val)
                return body
            block.tensor(run("pe"))
            block.scalar(run("act"))
            block.vector(run("dve"))
            block.gpsimd(run("pool"))
            block.sync(run("sp"))


class Builder:
    def __init__(self):
        self.nc = bass.Bass("TRN2", target_bir_lowering=False)
        self.s = Sched(self.nc)
        self.out_dmas = []
        self.rescache = {}
        self.dq = 0
        self.bar = None
        self.bar_tile = None

    def R(self, name):
        if name not in self.rescache:
            r = Res(name)
            r.w = self.bar
            self.rescache[name] = r
        return self.rescache[name]

    def barrier(self):
        allr = list(self.rescache.values())
        scr = self.bar_tile
        o = self.s.op("pool", lambda e: e.memset(scr, 0.0), (), allr)
        self.bar = o.id

    def dma(self, out, in_, reads=(), writes=(), final=False, q=None):
        if q is None:
            q = "sp"
        o = self.s.op(q, lambda e, out=out, in_=in_: e.dma_start(out=out, in_=in_), reads, writes, dma=True)
        if final:
            self.out_dmas.append(o.id)
        return o

    def mm(self, out, lhsT, rhs, start, stop, reads=(), writes=()):
        return self.s.op("pe", lambda e: e.matmul(out, lhsT, rhs, start=start, stop=stop), reads, writes)

    def act(self, out, in_, func, bias=None, scale=None, reads=(), writes=(), accum_out=None):
        def f(e):
            kw = {}
            if bias is not None:
                kw["bias"] = bias
            if scale is not None:
                kw["scale"] = scale
            if accum_out is not None:
                kw["accum_out"] = accum_out
            return e.activation(out, in_, func, **kw)
        return self.s.op("act", f, reads, writes)

    def ts(self, eng, out, in0, s1, s2, op0, op1=None, reads=(), writes=()):
        def f(e):
            if op1 is None:
                return e.tensor_scalar(out, in0, s1, None, op0)
            return e.tensor_scalar(out, in0, s1, s2, op0, op1)
        return self.s.op(eng, f, reads, writes)

    def tt(self, eng, out, in0, in1, op, reads=(), writes=()):
        return self.s.op(eng, lambda e: e.tensor_tensor(out, in0, in1, op), reads, writes)

    def stt(self, eng, out, in0, scalar, in1, op0, op1, reads=(), writes=()):
        return self.s.op(eng, lambda e: e.scalar_tensor_tensor(out, in0, scalar, in1, op0, op1), reads, writes)

    def cp(self, eng, out, in_, reads=(), writes=()):
        if eng == "act":
            return self.s.op("act", lambda e: e.copy(out, in_), reads, writes)
        return self.s.op(eng, lambda e: e.tensor_copy(out, in_), reads, writes)

    def memset(self, eng, ap, val, writes=()):
        return self.s.op(eng, lambda e: e.memset(ap, val), (), writes)


def build():
    B = Builder()
    nc = B.nc
    s = B.s

    def din(name, shape, dt=F32):
        return nc.dram_tensor(name, list(shape), dt, kind="ExternalInput").ap()

    def dout(name, shape, dt=F32):
        return nc.dram_tensor(name, list(shape), dt, kind="ExternalOutput").ap()

    def dscr(name, shape, dt=F32):
        return nc.dram_tensor(name, list(shape), dt).ap()

    xT = din("xT", [128, 8, S])
    xsT = din("xsT", [128, 8, 32])
    keep = din("keep", [128, NT])
    cT = din("cT", [128, 8, 3])
    w_ada = din("w_ada", [128, 8, 9 * D])
    b_ada = din("b_ada", [128, 72])
    w_gu = [din("w_gu%d" % i, [128, 8, 2 * DFF]) for i in (1, 2)]
    w_dn = [din("w_dn%d" % i, [128, NFF, D]) for i in (1, 2)]
    w_in = din("w_in", [128, 8, DIN])
    b_in_fm = din("b_in_fm", [128, 40])
    b_in_bc = din("b_in_bc", [128, 1032])
    ident_d = din("ident", [128, 128])
    tri_d = din("tri", [128, 128])
    cmask_d = din("cmask", [128, 128])
    segm_d = din("segm", [128, 128])
    ng_d = din("ng", [128, 4])
    convs_d = din("convs", [128, 8, 2, 3])
    kcT_d = din("kcT", [2, 8, 64, 2048])
    vc_d = din("vc", [2, 8, 128, 16, 64])
    lfc_d = din("lfc", [2, 128, 16 * 8])
    Cs0_d = din("Cs0", [2, 128, 4 * 129])
    m0_d = din("m0c", [2, 4, 1])
    w_ba = din("w_ba", [128, 4, D])
    w_bb = din("w_bb", [128, 4, D])
    w_o = din("w_o", [128, 8, D])
    ln_g = din("ln_g", [128, 3, 8])
    ln_b = din("ln_b", [128, 3, 8])
    conv_w = din("conv_w", [128, 8, 4])
    conv_b = din("conv_b", [128, 8])

    yT = dout("yT", [128, 8, 4096])
    ysT = dout("ysT", [128, 8, 32])
    okT = dout("okT", [128, 4, 4096])
    ov = dout("ov", [4096, 512])
    olf = dout("olf", [4096, 8])
    oconv = dout("oconv", [128, 8, 3])
    oC = dout("oC", [128, 4 * 129])
    oksT = dout("oksT", [128, 4, 32])
    ovs = dout("ovs", [2, 16, 512])
    olfs = dout("olfs", [2, 16, 8])
    ocs = dout("ocs", [128, 8, 2, 3])
    oCs = dout("oCs", [2, 128, 4 * 129])
    oms = dout("oms", [2, 4, 1])
    om = dout("om", [128, 1])
    KT = dscr("KT", [8, 64, S], BF16)
    VA = dscr("VA", [8, 128, 128, 65], BF16)
    MK = dscr("MK", [128, 128, 512], BF16)
    MV = dscr("MV", [128, 128, 4 * 130], BF16)
    GI = dscr("GI", [4, S])
    GF = dscr("GF", [4, S])
    QT = dscr("QT", [8, 64, 4096], BF16)
    MQT = dscr("MQT", [4, 128, 4096], BF16)
    MKT = dscr("MKT", [4, 128, 4096], BF16)
    SG = dscr("SG", [16, 128, 20 * TN], BF16)
    QF = dscr("QF", [8, 4096], BF16)
    OA = dscr("OA", [8, 64, 4096], BF16)
    OB = dscr("OB", [4, 128, 4096], BF16)
    X2 = dscr("X2", [17, 128, 8, TN])
    KTs = dscr("KTs", [8, 64, 32], BF16)
    QTs = dscr("QTs", [8, 64, 32], BF16)
    VAs = dscr("VAs", [2, 16, 8 * 65], BF16)
    MQTs = dscr("MQTs", [4, 128, 32], BF16)
    MKTs = dscr("MKTs", [4, 128, 32], BF16)
    MKs = dscr("MKs", [2, 16, 512], BF16)
    MVs = dscr("MVs", [2, 16, 520], BF16)
    SGs = dscr("SGs", [128, 20 * 32], BF16)
    OAs = dscr("OAs", [8, 64, 32], BF16)
    OBs = dscr("OBs", [4, 128, 32], BF16)
    QFs = dscr("QFs", [2, 8, 16], BF16)

    X1 = dscr("X1", [NT + 1, 128, 8, TN])

    import contextlib
    es = contextlib.ExitStack()
    ARENA_F = 50432
    arena = es.enter_context(nc.sbuf_tensor("arena", [128, 2 * ARENA_F], BF16))
    psum = [es.enter_context(nc.psum_tensor("ps%d" % i, [128, 512], F32)) for i in range(8)]
    PS = [B.R("ps%d" % i) for i in range(8)]

    class Arena:
        def __init__(self):
            self.off = 0

        def f32(self, n):
            a = arena[:, 2 * self.off:2 * (self.off + n)].bitcast(F32)
            self.off += n
            assert self.off <= ARENA_F, self.off
            return a

        def bf16(self, n):
            m = (n + 1) // 2
            a = arena[:, 2 * self.off:2 * (self.off + m)]
            self.off += m
            assert self.off <= ARENA_F, self.off
            return a[:, 0:n]

    A = Arena()
    modv = A.f32(72 * 3)
    modv3 = modv.rearrange("p (j c) -> p j c", c=3)
    opsc = A.f32(72 * 3)
    opsc3 = opsc.rearrange("p (j c) -> p j c", c=3)
    keep_sb = A.f32(NT)
    lng = A.f32(24)
    lnb = A.f32(24)
    lng3 = lng.rearrange("p (a c) -> p a c", c=8)
    lnb3 = lnb.rearrange("p (a c) -> p a c", c=8)
    ones_f = A.f32(128)
    ones_b = A.bf16(128)
    R_const = B.R("const")
    B.memset("pool", ones_f, 1.0, writes=[R_const])
    B.memset("pool", ones_b, 1.0, writes=[R_const])
    B.dma(keep_sb, keep[:, :], writes=[R_const])
    B.dma(lng, ln_g.rearrange("p a c -> p (a c)"), writes=[R_const])
    B.dma(lnb, ln_b.rearrange("p a c -> p (a c)"), writes=[R_const])
    hg = A.f32(72 * 3)
    hg3 = hg.rearrange("p (j c) -> p j c", c=3)
    B.bar_tile = A.f32(2)
    LF = A.f32(128 * 8)
    LF3 = LF.rearrange("p (b h) -> p b h", h=8)
    R_LF = B.R("LF")
    identf = A.f32(128)
    identb = A.bf16(128)
    negkeep = A.f32(NT)
    keepbig = A.f32(NT)
    B.dma(identf, ident_d[:, :], writes=[R_const])
    trif = A.f32(128)
    cmask = A.f32(128)
    B.dma(trif, tri_d[:, :], writes=[R_const])
    B.dma(cmask, cmask_d[:, :], writes=[R_const])
    B.cp("dve", identb, identf, reads=[R_const], writes=[R_const])
    B.ts("dve", negkeep, keep_sb, -1.0, None, ALU.mult, reads=[R_const], writes=[R_const])
    B.ts("dve", keepbig, keep_sb, -1.0, BIG, ALU.add, ALU.mult, reads=[R_const], writes=[R_const])
    gis = A.f32(32)
    gfs = A.f32(32)
    lfn = A.f32(16)
    R_smp = B.R("smp")
    stage_off = A.off

    ct = A.f32(24)
    ct3 = ct.rearrange("p (k c) -> p k c", c=3)
    sct = A.f32(24)
    sct3 = sct.rearrange("p (k c) -> p k c", c=3)
    bada = A.f32(72)
    R_ct = B.R("ct")
    B.dma(ct, cT.rearrange("p k c -> p (k c)"), writes=[R_ct])
    B.dma(bada, b_ada[:, :], writes=[R_ct])
    B.act(sct, ct, AF.Silu, reads=[R_ct], writes=[R_ct])
    WA = 1152
    wbuf = [A.f32(8 * WA) for _ in range(2)]
    Rw = [B.R("wada0"), B.R("wada1")]
    R_mod = B.R("mod")
    for g in range(8):
        wb = wbuf[g % 2]
        wb3 = wb.rearrange("p (k n) -> p k n", n=WA)
        B.dma(wb3, w_ada[:, :, g * WA:(g + 1) * WA], writes=[Rw[g % 2]], q=("sp" if g % 2 == 0 else "pool"))
        pt = psum[g % 2]
        for j in range(9):
            for k in range(8):
                B.mm(pt[:, j * 3:j * 3 + 3], wb3[:, k, j * 128:(j + 1) * 128], sct3[:, k, :], k == 0, k == 7,
                     reads=[Rw[g % 2], R_ct], writes=[PS[g % 2]])
        for j in range(9):
            jj = g * 9 + j
            B.ts("dve", modv3[:, jj, :], pt[:, j * 3:j * 3 + 3], bada[:, jj:jj + 1], None, ALU.add,
                 reads=[PS[g % 2], R_ct], writes=[R_mod])
    B.ts("dve", opsc, modv, 1.0, None, ALU.add, reads=[R_mod], writes=[R_mod])
    B.ts("dve", hg, modv, 0.5, None, ALU.mult, reads=[R_mod], writes=[R_mod])

    def barrier_all(tag):
        r = B.R("bar_" + tag)
        allres = list(B.rescache.values())
        for e in ("pe", "act", "dve", "pool"):
            pass
        return r

    def ffn_stage(idx, wgu_d, wdn_d, tiles, sh_j, sc_j, g_j, ln_i, load_fn, store_fn, tagp):
        B.barrier()
        A.off = stage_off
        wgu = A.bf16(8 * 2 * DFF)
        wgu3 = wgu.rearrange("p (k n) -> p k n", n=2 * DFF)
        wdn = A.bf16(NFF * D)
        wdn3 = wdn.rearrange("p (k n) -> p k n", n=D)
        R_wgu = B.R(tagp + "wgu")
        R_wdn = B.R(tagp + "wdn")
        for k in range(8):
            B.dma(wgu3[:, k, :], wgu_d[:, k, :], writes=[R_wgu], q="pool")
        for k0 in range(0, NFF, 2):
            B.dma(wdn3[:, k0:k0 + 2, :], wdn_d[:, k0:k0 + 2, :], writes=[R_wdn], q="pool")
        xin = [A.f32(8 * TN) for _ in range(2)]
        Rx = [B.R(tagp + "x0"), B.R(tagp + "x1")]
        hb = A.bf16(8 * TN)
        R_h = B.R(tagp + "h")
        actb = A.bf16(NFF * TN)
        R_act = B.R(tagp + "actb")
        sg = [A.f32(TN) for _ in range(2)]
        Rsg = [B.R(tagp + "sg0"), B.R(tagp + "sg1")]
        sq = A.bf16(8 * TN)
        R_sq = B.R(tagp + "sq")
        rb = A.bf16(8 * TN)
        R_rb = B.R(tagp + "rb")
        st1 = A.f32(TN)
        st2 = A.f32(TN)
        st3 = A.f32(TN)
        R_st = B.R(tagp + "st")
        pi = 0
        for ti, (tid, N, cond) in enumerate(tiles):
            b = ti % 2
            x3 = xin[b].rearrange("p (k n) -> p k n", n=TN)
            if ti == 0:
                load_fn(tid, x3[:, :, 0:N], Rx[b])
            if ti + 1 < len(tiles):
                nt_, nN, _ = tiles[ti + 1]
                load_fn(nt_, xin[1 - b].rearrange("p (k n) -> p k n", n=TN)[:, :, 0:nN], Rx[1 - b])
            h3 = hb.rearrange("p (k n) -> p k n", n=TN)
            a3 = actb.rearrange("p (k n) -> p k n", n=TN)
            r3 = x3
            R_r = Rx[b]
            q3 = sq.rearrange("p (k n) -> p k n", n=TN)
            rb3 = rb.rearrange("p (k n) -> p k n", n=TN)
            def grp(cond_, N_):
                return [(0, N_, cond_)] if isinstance(cond_, int) else cond_
            groups = grp(cond, N)

            def emit_h(tj):
                _, Nj, condj = tiles[tj]
                bj = tj % 2
                xj = xin[bj].rearrange("p (k n) -> p k n", n=TN)
                for k in range(8):
                    for (c0, c1, ci) in grp(condj, Nj):
                        B.act(h3[:, k, c0:c1], xj[:, k, c0:c1], AF.Identity,
                              bias=modv3[:, sh_j + k, ci:ci + 1], scale=opsc3[:, sc_j + k, ci:ci + 1],
                              reads=[Rx[bj], R_mod], writes=[R_h])
                B.act(xj[:, :, 0:Nj], xj[:, :, 0:Nj], AF.Copy, scale=ALPHA, reads=[Rx[bj]], writes=[Rx[bj]])
            if ti == 0:
                emit_h(0)
            for m in range(NFF):
                pg = pi % 8
                pu = (pi + 1) % 8
                pi += 2
                for k in range(8):
                    B.mm(psum[pg][:, 0:N], wgu3[:, k, m * 128:(m + 1) * 128], h3[:, k, 0:N], k == 0, k == 7,
                         reads=[R_wgu, R_h], writes=[PS[pg]])
                for k in range(8):
                    B.mm(psum[pu][:, 0:N], wgu3[:, k, DFF + m * 128:DFF + (m + 1) * 128], h3[:, k, 0:N], k == 0, k == 7,
                         reads=[R_wgu, R_h], writes=[PS[pu]])
                si = m % 2
                B.act(sg[si][:, 0:N], psum[pg][:, 0:N], AF.Silu, reads=[PS[pg]], writes=[Rsg[si]])
                B.tt("dve", a3[:, m, 0:N], sg[si][:, 0:N], psum[pu][:, 0:N], ALU.mult,
                     reads=[Rsg[si], PS[pu]], writes=[R_act])
            if ti + 1 < len(tiles):
                emit_h(ti + 1)
            for d in range(8):
                pd = pi % 8
                pi += 1
                for m in range(NFF):
                    B.mm(psum[pd][:, 0:N], wdn3[:, m, d * 128:(d + 1) * 128], a3[:, m, 0:N], m == 0, m == NFF - 1,
                         reads=[R_wdn, R_act], writes=[PS[pd]])
                for (c0, c1, ci) in groups:
                    B.stt("dve", r3[:, d, c0:c1], psum[pd][:, c0:c1], hg3[:, g_j + d, ci:ci + 1], x3[:, d, c0:c1],
                          ALU.mult, ALU.add, reads=[PS[pd], R_mod, Rx[b]], writes=[R_r])
                B.act(q3[:, d, 0:N], r3[:, d, 0:N], AF.Square, reads=[R_r], writes=[R_sq])
                B.act(rb3[:, d, 0:N], r3[:, d, 0:N], AF.Copy, reads=[R_r], writes=[R_rb])
            p1 = pi % 8
            p2 = (pi + 1) % 8
            pi += 2
            for d in range(8):
                B.mm(psum[p1][:, 0:N], ones_b.rearrange("p (a n) -> p a n", a=1)[:, 0, :], rb3[:, d, 0:N], d == 0, d == 7,
                     reads=[R_rb, R_const], writes=[PS[p1]])
            for d in range(8):
                B.mm(psum[p2][:, 0:N], ones_b.rearrange("p (a n) -> p a n", a=1)[:, 0, :], q3[:, d, 0:N], d == 0, d == 7,
                     reads=[R_sq, R_const], writes=[PS[p2]])
            B.ts("dve", st1[:, 0:N], psum[p1][:, 0:N], -1.0 / D, None, ALU.mult, reads=[PS[p1]], writes=[R_st])
            B.tt("dve", st2[:, 0:N], st1[:, 0:N], st1[:, 0:N], ALU.mult, reads=[R_st], writes=[R_st])
            B.stt("dve", st3[:, 0:N], psum[p2][:, 0:N], 1.0 / D, st2[:, 0:N], ALU.mult, ALU.subtract,
                  reads=[PS[p2], R_st], writes=[R_st])
            B.ts("dve", st3[:, 0:N], st3[:, 0:N], EPS, None, ALU.add, reads=[R_st], writes=[R_st])
            B.act(st3[:, 0:N], st3[:, 0:N], AF.Sqrt, reads=[R_st], writes=[R_st])
            B.s.op("dve", lambda e, a=st3[:, 0:N]: e.reciprocal(a, a), [R_st], [R_st])
            R_rd = [B.R(tagp + "rd%d_%d" % (b, d)) for d in range(8)]
            for d in range(8):
                B.tt("dve", r3[:, d, 0:N], r3[:, d, 0:N], st1[:, 0:N], ALU.add, reads=[R_r, R_st], writes=[R_rd[d]])
            for d in range(8):
                B.tt("dve", r3[:, d, 0:N], r3[:, d, 0:N], st3[:, 0:N], ALU.mult, reads=[R_rd[d], R_st], writes=[R_rd[d]])
            for d in range(8):
                B.ts("dve", r3[:, d, 0:N], r3[:, d, 0:N], lng3[:, ln_i, d:d + 1], lnb3[:, ln_i, d:d + 1], ALU.mult, ALU.add,
                     reads=[R_rd[d], R_const], writes=[R_rd[d], R_r])
            store_fn(tid, r3[:, :, 0:N], R_r)

    R_X1 = B.R("X1")

    def load1(tid, dst, res):
        if tid < NT:
            B.dma(dst, xT[:, :, tid * TN:(tid + 1) * TN], writes=[res])
        else:
            B.dma(dst, xsT[:, :, :], writes=[res])

    def store1(tid, src, res):
        Nn = TN if tid < NT else 32
        B.dma(X1[tid][:, :, 0:Nn], src, reads=[res], writes=[R_X1], q="pool")

    tiles1 = [(t, TN, 0) for t in range(NT)] + [(NT, 32, [(0, 16, 1), (16, 32, 2)])]
    ffn_stage(0, w_gu[0], w_dn[0], tiles1, 0, 8, 16, 0, load1, store1, "f1")


    FMC = ([("ak", 512 + 128 * c) for c in range(4)] + [("mk", 2056 + 128 * c) for c in range(4)] +
           [("aq", 128 * c) for c in range(4)] + [("mq", 1544 + 128 * c) for c in range(4)] +
           [("mo", 3088 + 128 * c) for c in range(4)] + [("ga", 3600 + 128 * c) for c in range(8)] +
           [("gb", 4624 + 128 * c) for c in range(8)])
    B.barrier()
    A.off = stage_off
    win = A.bf16(8 * DIN)
    win3 = win.rearrange("p (k n) -> p k n", n=DIN)
    R_win = B.R("win")
    for k in range(8):
        B.dma(win3[:, k, :], w_in[:, k, :], writes=[R_win], q="pool")
    bbc = A.f32(1032)
    bfm = A.f32(40)
    nbfm = A.f32(40)
    cw = A.f32(32)
    cw3 = cw.rearrange("p (c j) -> p c j", j=4)
    cb = A.f32(8)
    R_c2 = B.R("c2")
    B.dma(bbc, b_in_bc[:, :], writes=[R_c2])
    B.dma(bfm, b_in_fm[:, :], writes=[R_c2])
    B.dma(cw, conv_w.rearrange("p c j -> p (c j)"), writes=[R_c2])
    B.dma(cb, conv_b[:, :], writes=[R_c2])
    B.ts("dve", nbfm, bfm, -1.0, None, ALU.mult, reads=[R_c2], writes=[R_c2])
    x1b = [A.f32(8 * TN) for _ in range(3)]
    Rx1 = [B.R("s2x0"), B.R("s2x1"), B.R("s2x2")]
    h2s = [A.bf16(8 * TN) for _ in range(2)]
    h2s3 = [h.rearrange("p (k n) -> p k n", n=TN) for h in h2s]
    R_h2s = [B.R("h2_0"), B.R("h2_1")]
    UW = TN + 3
    ub = A.f32(8 * UW)
    u3 = ub.rearrange("p (c n) -> p c n", n=UW)
    R_u = B.R("u")
    B.memset("pool", ub, 0.0, writes=[R_u])
    co = A.f32(8 * TN)
    co3 = co.rearrange("p (c n) -> p c n", n=TN)
    R_co = B.R("co")
    qkbs = [A.bf16(8 * TN) for _ in range(2)]
    qkbs3 = [q_.rearrange("p (c n) -> p c n", n=TN) for q_ in qkbs]
    R_qkbs = [B.R("qkb0"), B.R("qkb1")]
    kTf = A.f32(4 * TN)
    kTf3 = kTf.rearrange("p (c n) -> p c n", n=TN)
    R_kTf = B.R("kTf")
    kTb = A.bf16(4 * TN)
    kTb3 = kTb.rearrange("p (c n) -> p c n", n=TN)
    R_kTb = B.R("kTb")
    qTb = A.bf16(4 * TN)
    qTb3 = qTb.rearrange("p (c n) -> p c n", n=TN)
    R_qTb = B.R("qTb")
    sgb = A.bf16(20 * TN)
    sgb3 = sgb.rearrange("p (c n) -> p c n", n=TN)
    R_sgb = B.R("sgb")
    vf = [A.f32(520) for _ in range(2)]
    R_vf = [B.R("vf0"), B.R("vf1")]
    va = [A.bf16(8 * 65) for _ in range(2)]
    R_va = [B.R("va0"), B.R("va1")]
    mvb = [A.bf16(4 * 130) for _ in range(2)]
    R_mvb = [B.R("mvb0"), B.R("mvb1")]
    ktok = [A.bf16(512) for _ in range(2)]
    R_ktok = [B.R("ktok0"), B.R("ktok1")]
    lft = A.f32(16)
    R_lft = B.R("lft")
    gt = A.f32(2 * TN)
    gt3 = gt.rearrange("p (a n) -> p a n", n=TN)
    R_gt = B.R("gt")
    for i in range(2):
        B.memset("pool", va[i], 1.0, writes=[R_va[i]])
        B.memset("pool", mvb[i], 1.0, writes=[R_mvb[i]])
    R_KT = B.R("KT"); R_VA = B.R("VA"); R_MK = B.R("MK"); R_MV = B.R("MV"); R_G = B.R("GIF")
    R_QT = B.R("QT"); R_MQT = B.R("MQT"); R_MKT = B.R("MKT"); R_SG = B.R("SG")
    pi = 0
    sq_ = ["sp", "sp"]
    def emit_h2(tj):
        bj = tj % 2
        xj = x1b[tj % 3].rearrange("p (k n) -> p k n", n=TN)
        for k in range(8):
            B.ts("dve", h2s3[bj][:, k, :], xj[:, k, :], opsc3[:, 32 + k, 0:1], modv3[:, 24 + k, 0:1], ALU.mult, ALU.add,
                 reads=[Rx1[tj % 3], R_mod], writes=[R_h2s[bj]])

    def emit_ktr(tj):
        bj = tj % 2
        nonlocal_pi = [0]
        for sb in range(2):
            blk = tj * 2 + sb
            p = 6 + sb
            pb = psum[p][:, :].bitcast(BF16)
            for hh in range(4):
                s.op("pe", lambda e, o_=pb[:, hh * 128:(hh + 1) * 128], i_=qkbs3[bj][:, 4 + hh, sb * 128:(sb + 1) * 128]:
                     e.transpose(o_, i_, identb), [R_qkbs[bj], R_const], [PS[p]])
            B.cp("act", ktok[sb], pb[:, 0:512], reads=[PS[p]], writes=[R_ktok[sb]])
            B.dma(MK[blk], ktok[sb], reads=[R_ktok[sb]], writes=[R_MK], q="sp")

    B.dma(x1b[0].rearrange("p (k n) -> p k n", n=TN), X1[0], reads=[R_X1], writes=[Rx1[0]])
    B.dma(x1b[1].rearrange("p (k n) -> p k n", n=TN), X1[1], reads=[R_X1], writes=[Rx1[1]])
    emit_h2(0)
    pending = None
    for t in range(NT):
        own = t >= OWN0
        b = t % 2
        if t + 2 < NT:
            B.dma(x1b[(t + 2) % 3].rearrange("p (k n) -> p k n", n=TN), X1[t + 2], reads=[R_X1], writes=[Rx1[(t + 2) % 3]])
        if t + 1 < NT:
            emit_h2(t + 1)
        N = TN
        h23 = h2s3[b]
        R_h2 = R_h2s[b]
        qkb3 = qkbs3[b]
        R_qkb = R_qkbs[b]
        kcol = keep_sb[:, t:t + 1]
        nch = len(FMC) if own else 8
        for j in range(nch):
            name, c0 = FMC[j]
            p = pi % 6
            pi += 1
            for k in range(8):
                B.mm(psum[p][:, 0:N], win3[:, k, c0:c0 + 128], h23[:, k, :], k == 0, k == 7,
                     reads=[R_win, R_h2], writes=[PS[p]])
            c = j % 4 if name not in ("ga", "gb") else (j - 20) % 8
            if name == "ak":
                B.act(kTf3[:, c, :], psum[p][:, 0:N], AF.Identity, bias=bfm[:, j:j + 1], reads=[PS[p], R_c2], writes=[R_kTf])
                B.cp("dve", kTb3[:, c, :], kTf3[:, c, :], reads=[R_kTf], writes=[R_kTb])
            elif name == "mk":
                B.ts("dve", u3[:, 4 + c, 3:3 + N], psum[p][:, 0:N], bfm[:, j:j + 1], kcol, ALU.add, ALU.mult,
                     reads=[PS[p], R_c2, R_const], writes=[R_u])
            elif name == "mq":
                B.ts("dve", u3[:, c, 3:3 + N], psum[p][:, 0:N], bfm[:, j:j + 1], kcol, ALU.add, ALU.mult,
                     reads=[PS[p], R_c2, R_const], writes=[R_u])
            elif name == "aq":
                B.ts("dve", qTb3[:, c, :], psum[p][:, 0:N], bfm[:, j:j + 1], 0.125, ALU.add, ALU.mult,
                     reads=[PS[p], R_c2], writes=[R_qTb])
            else:
                jj = {"mo": 0, "ga": 4, "gb": 12}[name] + c
                B.act(sgb3[:, jj, :], psum[p][:, 0:N], AF.Sigmoid, bias=bfm[:, j:j + 1], reads=[PS[p], R_c2], writes=[R_sgb])
        p = pi % 6
        pi += 1
        for k in range(8):
            B.mm(psum[p][0:4, 0:N], win3[:, k, 3080:3084], h23[:, k, :], k == 0, k == 7, reads=[R_win, R_h2], writes=[PS[p]])
        B.ts("dve", gt3[0:4, 0, :], psum[p][0:4, 0:N], bfm[0:4, 36:37], kcol[0:4, :], ALU.add, ALU.mult,
             reads=[PS[p], R_c2, R_const], writes=[R_gt])
        B.ts("dve", gt3[0:4, 0, :], gt3[0:4, 0, :], keepbig[0:4, t:t + 1], None, ALU.add, reads=[R_gt, R_const], writes=[R_gt])
        p = pi % 6
        pi += 1
        for k in range(8):
            B.mm(psum[p][0:4, 0:N], win3[:, k, 3084:3088], h23[:, k, :], k == 0, k == 7, reads=[R_win, R_h2], writes=[PS[p]])
        B.act(gt3[0:4, 1, :], psum[p][0:4, 0:N], AF.Exp, bias=nbfm[0:4, 37:38], scale=-1.0, reads=[PS[p], R_c2], writes=[R_gt])
        B.act(gt3[0:4, 1, :], gt3[0:4, 1, :], AF.Ln, bias=1.0, reads=[R_gt], writes=[R_gt])
        B.ts("dve", gt3[0:4, 1, :], gt3[0:4, 1, :], negkeep[0:4, t:t + 1], None, ALU.mult, reads=[R_gt, R_const], writes=[R_gt])
        B.dma(GI[:, t * TN:(t + 1) * TN], gt3[0:4, 0, :], reads=[R_gt], writes=[R_G], q="sp")
        B.dma(GF[:, t * TN:(t + 1) * TN], gt3[0:4, 1, :], reads=[R_gt], writes=[R_G], q="sp")
        for sb in range(2):
            blk = t * 2 + sb
            tok = slice(sb * 128, (sb + 1) * 128)
            p = pi % 6
            pi += 1
            for k in range(8):
                B.mm(psum[p][:, 0:512], h23[:, k, tok], win3[:, k, 1024:1536], k == 0, k == 7, reads=[R_win, R_h2], writes=[PS[p]])
            B.tt("dve", vf[sb][:, 0:512], psum[p][:, 0:512], bbc[:, 0:512], ALU.add, reads=[PS[p], R_c2], writes=[R_vf[sb]])
            B.cp("act", va[sb].rearrange("p (h d) -> p h d", d=65)[:, :, 0:64],
                 vf[sb][:, 0:512].rearrange("p (h d) -> p h d", d=64), reads=[R_vf[sb]], writes=[R_va[sb]])
            B.dma(VA[:, :, blk, :].rearrange("h p d -> p h d"), va[sb].rearrange("p (h d) -> p h d", d=65), reads=[R_va[sb]], writes=[R_VA], q="sp")
            if own:
                B.dma(ov[(blk - 2 * OWN0) * 128:(blk - 2 * OWN0 + 1) * 128, :], vf[sb][:, 0:512], reads=[R_vf[sb]], final=True, q="sp")
            p = pi % 6
            pi += 1
            for k in range(8):
                B.mm(psum[p][:, 0:8], h23[:, k, tok], win3[:, k, 1536:1544], k == 0, k == 7, reads=[R_win, R_h2], writes=[PS[p]])
            B.tt("dve", lft[:, 0:8], psum[p][:, 0:8], bbc[:, 512:520], ALU.add, reads=[PS[p], R_c2], writes=[R_lft])
            B.act(lft[:, 0:8], lft[:, 0:8], AF.Exp, scale=-1.0, reads=[R_lft], writes=[R_lft])
            B.act(lft[:, 0:8], lft[:, 0:8], AF.Ln, bias=1.0, reads=[R_lft], writes=[R_lft])
            B.ts("dve", LF3[:, blk, :], lft[:, 0:8], negkeep[:, t:t + 1], None, ALU.mult, reads=[R_lft, R_const], writes=[R_LF])
            if own:
                B.dma(olf[(blk - 2 * OWN0) * 128:(blk - 2 * OWN0 + 1) * 128, :], LF3[:, blk, :], reads=[R_LF], final=True, q="sp")
            p = pi % 6
            pi += 1
            for k in range(8):
                B.mm(psum[p][:, 0:512], h23[:, k, tok], win3[:, k, 2568:3080], k == 0, k == 7, reads=[R_win, R_h2], writes=[PS[p]])
            B.tt("dve", mvb[sb].rearrange("p (h d) -> p h d", d=130)[:, :, 0:128],
                 psum[p][:, 0:512].rearrange("p (h d) -> p h d", d=128),
                 bbc[:, 520:1032].rearrange("p (h d) -> p h d", d=128), ALU.add, reads=[PS[p], R_c2], writes=[R_mvb[sb]])
            B.dma(MV[blk], mvb[sb], reads=[R_mvb[sb]], writes=[R_MV], q="sp")
        chs = list(range(8)) if own else [4, 5, 6, 7]
        R_cc = [B.R("co_ch%d" % ch) for ch in range(8)]
        for ch in chs:
            B.ts("dve", co3[:, ch, :], u3[:, ch, 0:N], cw3[:, ch, 0:1], cb[:, ch:ch + 1], ALU.mult, ALU.add,
                 reads=[R_u, R_c2], writes=[R_cc[ch]])
        for j in range(1, 4):
            for ch in chs:
                B.stt("dve", co3[:, ch, :], u3[:, ch, j:j + N], cw3[:, ch, j:j + 1], co3[:, ch, :], ALU.mult, ALU.add,
                      reads=[R_u, R_c2, R_cc[ch]], writes=[R_cc[ch]])
        for ch in chs:
            if ch < 4:
                B.act(qkb3[:, ch, :], co3[:, ch, :], AF.Silu, reads=[R_cc[ch]], writes=[R_qkb])
            else:
                B.act(co3[:, ch, :], co3[:, ch, :], AF.Silu, reads=[R_cc[ch]], writes=[R_cc[ch]])
        for ch in chs:
            if ch >= 4:
                B.ts("dve", qkb3[:, ch, :], co3[:, ch, :], 128.0 ** -0.5, None, ALU.mult, reads=[R_cc[ch]], writes=[R_qkb])
        if t == NT - 1:
            B.dma(oconv[:, :, :], u3[:, :, TN:TN + 3], reads=[R_u], final=True)
        B.cp("dve", u3[:, :, 0:3], u3[:, :, TN:TN + 3], reads=[R_u], writes=[R_u])
        if pending is not None:
            emit_ktr(pending)
        pending = t
        for h in range(8):
            B.dma(KT[h][:, t * TN:(t + 1) * TN], kTb3[(h % 2) * 64:(h % 2 + 1) * 64, h // 2, :], reads=[R_kTb], writes=[R_KT],
                  q=sq_[h % 2])
        if own:
            to = t - OWN0
            B.dma(okT[:, :, to * TN:(to + 1) * TN], kTf3, reads=[R_kTf], final=True, q="sp")
            for h in range(8):
                B.dma(QT[h][:, to * TN:(to + 1) * TN], qTb3[(h % 2) * 64:(h % 2 + 1) * 64, h // 2, :], reads=[R_qTb], writes=[R_QT],
                      q=sq_[h % 2])
            for hh in range(4):
                B.dma(MQT[hh][:, to * TN:(to + 1) * TN], qkb3[:, hh, :], reads=[R_qkb], writes=[R_MQT], q="sp")
                B.dma(MKT[hh][:, to * TN:(to + 1) * TN], qkb3[:, 4 + hh, :], reads=[R_qkb], writes=[R_MKT], q="sp")
            B.dma(SG[to], sgb, reads=[R_sgb], writes=[R_SG], q="sp")


    emit_ktr(pending)
    h23 = h2s3[0]
    R_h2 = R_h2s[0]
    qkb3 = qkbs3[0]
    R_qkb = R_qkbs[0]
    NS = 32
    xs3 = x1b[0].rearrange("p (k n) -> p k n", n=TN)
    B.dma(xs3[:, :, 0:NS], X1[NT][:, :, 0:NS], reads=[R_X1], writes=[Rx1[0]])
    for k in range(8):
        for e in range(2):
            B.act(h23[:, k, e * 16:(e + 1) * 16], xs3[:, k, e * 16:(e + 1) * 16], AF.Identity,
                  bias=modv3[:, 24 + k, 1 + e:2 + e], scale=opsc3[:, 32 + k, 1 + e:2 + e], reads=[Rx1[0], R_mod], writes=[R_h2])
    usb = A.f32(8 * 2 * 19)
    us4 = usb.rearrange("p (c e n) -> p c e n", e=2, n=19)
    R_us = B.R("us")
    B.dma(us4[:, :, :, 0:3], convs_d[:, :, :, :], writes=[R_us])
    cos = co[:, 0:8 * 32]
    cos4 = cos.rearrange("p (c e n) -> p c e n", e=2, n=16)
    R_cos = B.R("co_ch0")
    for j in range(len(FMC)):
        name, c0 = FMC[j]
        p = pi % 8
        pi += 1
        for k in range(8):
            B.mm(psum[p][:, 0:NS], win3[:, k, c0:c0 + 128], h23[:, k, 0:NS], k == 0, k == 7, reads=[R_win, R_h2], writes=[PS[p]])
        c = j % 4 if name not in ("ga", "gb") else (j - 20) % 8
        if name == "ak":
            B.act(kTf3[:, c, 0:NS], psum[p][:, 0:NS], AF.Identity, bias=bfm[:, j:j + 1], reads=[PS[p], R_c2], writes=[R_kTf])
            B.cp("dve", kTb3[:, c, 0:NS], kTf3[:, c, 0:NS], reads=[R_kTf], writes=[R_kTb])
        elif name in ("mk", "mq"):
            ch = c + (4 if name == "mk" else 0)
            B.ts("dve", us4[:, ch, :, 3:19], psum[p][:, 0:NS].rearrange("p (e n) -> p e n", n=16), bfm[:, j:j + 1], None, ALU.add,
                 reads=[PS[p], R_c2], writes=[R_us])
        elif name == "aq":
            B.ts("dve", qTb3[:, c, 0:NS], psum[p][:, 0:NS], bfm[:, j:j + 1], 0.125, ALU.add, ALU.mult, reads=[PS[p], R_c2], writes=[R_qTb])
        else:
            jj = {"mo": 0, "ga": 4, "gb": 12}[name] + c
            B.act(sgb3[:, jj, 0:NS], psum[p][:, 0:NS], AF.Sigmoid, bias=bfm[:, j:j + 1], reads=[PS[p], R_c2], writes=[R_sgb])
    p = pi % 8
    pi += 1
    for k in range(8):
        B.mm(psum[p][0:4, 0:NS], win3[:, k, 3080:3084], h23[:, k, 0:NS], k == 0, k == 7, reads=[R_win, R_h2], writes=[PS[p]])
    B.ts("dve", gis[0:4, :], psum[p][0:4, 0:NS], bfm[0:4, 36:37], None, ALU.add, reads=[PS[p], R_c2], writes=[R_smp])
    p = pi % 8
    pi += 1
    for k in range(8):
        B.mm(psum[p][0:4, 0:NS], win3[:, k, 3084:3088], h23[:, k, 0:NS], k == 0, k == 7, reads=[R_win, R_h2], writes=[PS[p]])
    B.act(gfs[0:4, :], psum[p][0:4, 0:NS], AF.Exp, bias=nbfm[0:4, 37:38], scale=-1.0, reads=[PS[p], R_c2], writes=[R_smp])
    B.act(gfs[0:4, :], gfs[0:4, :], AF.Ln, bias=1.0, reads=[R_smp], writes=[R_smp])
    B.ts("dve", gfs[0:4, :], gfs[0:4, :], -1.0, None, ALU.mult, reads=[R_smp], writes=[R_smp])
    R_VAs = B.R("VAs"); R_MVs = B.R("MVs"); R_MKs = B.R("MKs")
    for e in range(2):
        tok = slice(e * 16, (e + 1) * 16)
        p = pi % 8
        pi += 1
        for k in range(8):
            B.mm(psum[p][0:16, 0:512], h23[:, k, tok], win3[:, k, 1024:1536], k == 0, k == 7, reads=[R_win, R_h2], writes=[PS[p]])
        B.tt("dve", vf[e][0:16, 0:512], psum[p][0:16, 0:512], bbc[0:16, 0:512], ALU.add, reads=[PS[p], R_c2], writes=[R_vf[e]])
        B.cp("act", va[e][0:16, :].rearrange("p (h d) -> p h d", d=65)[:, :, 0:64],
             vf[e][0:16, 0:512].rearrange("p (h d) -> p h d", d=64), reads=[R_vf[e]], writes=[R_va[e]])
        B.dma(VAs[e], va[e][0:16, :], reads=[R_va[e]], writes=[R_VAs], q="sp")
        B.dma(ovs[e], vf[e][0:16, 0:512], reads=[R_vf[e]], final=True, q="sp")
        p = pi % 8
        pi += 1
        for k in range(8):
            B.mm(psum[p][0:16, 0:8], h23[:, k, tok], win3[:, k, 1536:1544], k == 0, k == 7, reads=[R_win, R_h2], writes=[PS[p]])
        B.tt("dve", lft[0:16, 0:8], psum[p][0:16, 0:8], bbc[0:16, 512:520], ALU.add, reads=[PS[p], R_c2], writes=[R_lft])
        B.act(lft[0:16, 0:8], lft[0:16, 0:8], AF.Exp, scale=-1.0, reads=[R_lft], writes=[R_lft])
        B.act(lft[0:16, 0:8], lft[0:16, 0:8], AF.Ln, bias=1.0, reads=[R_lft], writes=[R_lft])
        B.ts("dve", lfn[0:16, e * 8:(e + 1) * 8], lft[0:16, 0:8], -1.0, None, ALU.mult, reads=[R_lft], writes=[R_smp])
        B.dma(olfs[e], lfn[0:16, e * 8:(e + 1) * 8], reads=[R_smp], final=True, q="sp")
        p = pi % 8
        pi += 1
        for k in range(8):
            B.mm(psum[p][0:16, 0:512], h23[:, k, tok], win3[:, k, 2568:3080], k == 0, k == 7, reads=[R_win, R_h2], writes=[PS[p]])
        B.tt("dve", mvb[e][0:16, :].rearrange("p (h d) -> p h d", d=130)[:, :, 0:128],
             psum[p][0:16, 0:512].rearrange("p (h d) -> p h d", d=128),
             bbc[0:16, 520:1032].rearrange("p (h d) -> p h d", d=128), ALU.add, reads=[PS[p], R_c2], writes=[R_mvb[e]])
        B.dma(MVs[e], mvb[e][0:16, :], reads=[R_mvb[e]], writes=[R_MVs], q="sp")
    for ch in range(8):
        B.ts("dve", cos4[:, ch, :, :], us4[:, ch, :, 0:16], cw3[:, ch, 0:1], cb[:, ch:ch + 1], ALU.mult, ALU.add,
             reads=[R_us, R_c2], writes=[R_cos])
        for j in range(1, 4):
            B.stt("dve", cos4[:, ch, :, :], us4[:, ch, :, j:j + 16], cw3[:, ch, j:j + 1], cos4[:, ch, :, :], ALU.mult, ALU.add,
                  reads=[R_us, R_c2, R_cos], writes=[R_cos])
        B.act(cos4[:, ch, :, :], cos4[:, ch, :, :], AF.Silu, reads=[R_cos], writes=[R_cos])
        B.ts("dve", qkb3[:, ch, 0:NS].rearrange("p (e n) -> p e n", n=16), cos4[:, ch, :, :], (1.0 if ch < 4 else 128.0 ** -0.5), None, ALU.mult,
             reads=[R_cos], writes=[R_qkb])
    B.dma(ocs[:, :, :, :], us4[:, :, :, 16:19], reads=[R_us], final=True)
    for e in range(2):
        p = pi % 8
        pi += 1
        pb = psum[p][:, :].bitcast(BF16)
        for hh in range(4):
            s.op("pe", lambda e_, o_=pb[0:16, hh * 128:(hh + 1) * 128], i_=qkb3[:, 4 + hh, e * 16:(e + 1) * 16]:
                 e_.transpose(o_, i_, identb), [R_qkb, R_const], [PS[p]])
        B.cp("act", ktok[e][0:16, :], pb[0:16, 0:512], reads=[PS[p]], writes=[R_ktok[e]])
        B.dma(MKs[e], ktok[e][0:16, :], reads=[R_ktok[e]], writes=[R_MKs], q="sp")
    R_KTs = B.R("KTs"); R_QTs = B.R("QTs"); R_MQTs = B.R("MQTs"); R_SGs = B.R("SGs")
    B.dma(oksT[:, :, :], kTf3[:, :, 0:NS], reads=[R_kTf], final=True, q="sp")
    for h in range(8):
        B.dma(KTs[h], kTb3[(h % 2) * 64:(h % 2 + 1) * 64, h // 2, 0:NS], reads=[R_kTb], writes=[R_KTs], q=sq_[h % 2])
        B.dma(QTs[h], qTb3[(h % 2) * 64:(h % 2 + 1) * 64, h // 2, 0:NS], reads=[R_qTb], writes=[R_QTs], q=sq_[h % 2])
    for hh in range(4):
        B.dma(MQTs[hh], qkb3[:, hh, 0:NS], reads=[R_qkb], writes=[R_MQTs], q="sp")
        B.dma(MKTs[hh], qkb3[:, 4 + hh, 0:NS], reads=[R_qkb], writes=[R_MQTs], q="sp")
    B.dma(SGs.rearrange("p (c n) -> p c n", n=NS), sgb3[:, :, 0:NS], reads=[R_sgb], writes=[R_SGs], q="sp")

    B.barrier()
    A.off = stage_off
    R_F = B.R("F")
    Fk = A.f32(1024)
    Fk3 = Fk.rearrange("p (b h) -> p b h", h=8)
    Tb = [A.f32(1024) for _ in range(2)]
    R_Tb = [B.R("Tb0"), B.R("Tb1")]
    for half in range(2):
        cs = slice(half * 512, (half + 1) * 512)
        B.mm(psum[half][:, :], trif, LF[:, cs], True, True, reads=[R_const, R_LF], writes=[PS[half]])
        B.mm(psum[2 + half][:, :], ones_f, LF[:, cs], True, True, reads=[R_const, R_LF], writes=[PS[2 + half]])
        B.cp("dve", Fk[:, cs], psum[half][:, :], reads=[PS[half]], writes=[R_F])
        B.cp("dve", Tb[0][:, cs], psum[2 + half][:, :], reads=[PS[2 + half]], writes=[R_Tb[0]])
    B.tt("dve", Fk, Fk, Tb[0], ALU.subtract, reads=[R_F, R_Tb[0]], writes=[R_F])
    cur = 0
    sh = 1
    while sh < 128:
        a3 = Tb[cur].rearrange("p (b h) -> p b h", h=8)
        n3 = Tb[1 - cur].rearrange("p (b h) -> p b h", h=8)
        B.cp("dve", n3[:, 0:sh, :], a3[:, 0:sh, :], reads=[R_Tb[cur]], writes=[R_Tb[1 - cur]])
        B.tt("dve", n3[:, sh:128, :], a3[:, sh:128, :], a3[:, 0:128 - sh, :], ALU.add, reads=[R_Tb[cur]], writes=[R_Tb[1 - cur]])
        cur = 1 - cur
        sh *= 2
    Tinc3 = Tb[cur].rearrange("p (b h) -> p b h", h=8)
    B.tt("dve", Fk, Fk, Tb[cur], ALU.add, reads=[R_F, R_Tb[cur]], writes=[R_F])
    for h in range(8):
        B.ts("dve", Fk3[:, :, h], Fk3[:, :, h], Tinc3[:, 2 * OWN0 - 1, h:h + 1], None, ALU.subtract,
             reads=[R_F, R_Tb[cur]], writes=[R_F])
    bK = A.f32(1024)
    bK3 = bK.rearrange("p (h b) -> p h b", b=128)
    R_bK = B.R("bK")
    for h in range(8):
        B.ts("dve", bK3[:, h, :], Fk3[:, :, h], -1.0, None, ALU.mult, reads=[R_F], writes=[R_bK])
    for t in range(OWN0):
        B.ts("dve", bK3[:, :, 2 * t:2 * t + 2], bK3[:, :, 2 * t:2 * t + 2], keepbig[:, t:t + 1], None, ALU.add,
             reads=[R_bK, R_const], writes=[R_bK])
    fq = A.bf16(4096)
    R_fq = B.R("fq")
    R_QF = B.R("QF")
    for g in range(8):
        p = 4 + g % 2
        for i in range(4):
            blk = 2 * OWN0 + g * 4 + i
            s.op("pe", lambda e, o_=psum[p][0:8, i * 128:(i + 1) * 128], i_=Fk3[:, blk, :]: e.transpose(o_, i_, identf),
                 [R_F, R_const], [PS[p]])
        B.cp("act", fq[0:8, g * 512:(g + 1) * 512], psum[p][0:8, :], reads=[PS[p]], writes=[R_fq])
    B.dma(QF[:, :], fq[0:8, :], reads=[R_fq], writes=[R_QF])

    Kb = [A.bf16(S) for _ in range(2)]
    Qb = [A.bf16(4096) for _ in range(2)]
    Vb = [A.bf16(128 * 65) for _ in range(2)]
    R_Kb = [B.R("Kb0"), B.R("Kb1")]
    R_Qb = [B.R("Qb0"), B.R("Qb1")]
    R_Vb = [B.R("Vb0"), B.R("Vb1")]
    for i in range(2):
        B.memset("pool", Kb[i][64:65, :], 1.0, writes=[R_Kb[i]])
    NPB = 4
    Pb = [A.bf16(512) for _ in range(NPB)]
    R_Pb = [B.R("Pb%d" % i) for i in range(NPB)]
    dtmp = [A.f32(128) for _ in range(2)]
    R_dt = [B.R("dt0"), B.R("dt1")]
    onf = A.f32(512)
    R_onf = B.R("onf")
    rden = A.f32(512)
    R_rden = B.R("rden")
    oTb = [A.bf16(512) for _ in range(2)]
    R_oTb = [B.R("oTb0"), B.R("oTb1")]
    R_OA = B.R("OA")

    def load_head(h):
        hb = h % 2
        B.dma(Kb[hb][0:64, :], KT[h], reads=[R_KT], writes=[R_Kb[hb]], q="sp")
        B.dma(Qb[hb][0:64, :], QT[h], reads=[R_QT], writes=[R_Qb[hb]], q="sp")
        B.dma(Qb[hb][64:65, :], QF[h:h + 1, :], reads=[R_QF], writes=[R_Qb[hb]], q="sp")
        B.dma(Vb[hb].rearrange("p (b d) -> p b d", d=65), VA[h], reads=[R_VA], writes=[R_Vb[hb]], q="pool")

    load_head(0)
    LA = 3
    ntile = 0
    for h in range(8):
        hb = h % 2
        if h + 1 < 8:
            load_head(h + 1)
        V3 = Vb[hb].rearrange("p (b d) -> p b d", d=65)
        items = []
        for lt in range(8):
            nkb = 2 * OWN0 + 4 * lt + 4
            for kb in range(nkb):
                d = kb - (2 * OWN0 + 4 * lt)
                c0 = 0 if d < 0 else d * 128
                items.append((lt, kb, c0, d, kb == 0, kb == nkb - 1))
        def emit_mm1(i):
            lt, kb, c0, d, first, last = items[i]
            sbk = i % 4
            B.mm(psum[sbk][:, c0:512], Kb[hb][0:65, kb * 128:(kb + 1) * 128], Qb[hb][0:65, lt * 512 + c0:(lt + 1) * 512],
                 True, True, reads=[R_Kb[hb], R_Qb[hb]], writes=[PS[sbk]])
        for i in range(min(LA, len(items))):
            emit_mm1(i)
        for i, (lt, kb, c0, d, first, last) in enumerate(items):
            if i + LA < len(items):
                emit_mm1(i + LA)
            sbk = i % 4
            pb_ = i % NPB
            ob = 4 + (ntile % 2)
            bias = bK3[:, h, kb:kb + 1]
            if d >= 0:
                di = i % 2
                B.tt("dve", dtmp[di], psum[sbk][:, c0:c0 + 128], cmask, ALU.add, reads=[PS[sbk], R_const], writes=[R_dt[di]])
                B.act(Pb[pb_][:, c0:c0 + 128], dtmp[di], AF.Exp, bias=bias, reads=[R_dt[di], R_bK], writes=[R_Pb[pb_]])
                if c0 + 128 < 512:
                    B.act(Pb[pb_][:, c0 + 128:512], psum[sbk][:, c0 + 128:512], AF.Exp, bias=bias,
                          reads=[PS[sbk], R_bK], writes=[R_Pb[pb_]])
            else:
                B.act(Pb[pb_][:, :], psum[sbk][:, :], AF.Exp, bias=bias, reads=[PS[sbk], R_bK], writes=[R_Pb[pb_]])
            B.mm(psum[ob][0:65, c0:512], V3[:, kb, :], Pb[pb_][:, c0:512], first, last,
                 reads=[R_Vb[hb], R_Pb[pb_]], writes=[PS[ob]])
            if last:
                B.cp("act", onf[0:64, :], psum[ob][0:64, :], reads=[PS[ob]], writes=[R_onf])
                s.op("dve", lambda e, o_=rden[64:65, :], i_=psum[ob][64:65, :]: e.reciprocal(o_, i_), [PS[ob]], [R_rden])
                B.mm(psum[6][0:64, :], ones_f[64:65, 0:64], rden[64:65, :], True, True, reads=[R_const, R_rden], writes=[PS[6]])
                ot = ntile % 2
                B.tt("dve", oTb[ot][0:64, :], onf[0:64, :], psum[6][0:64, :], ALU.mult, reads=[R_onf, PS[6]], writes=[R_oTb[ot]])
                B.dma(OA[h][:, lt * 512:(lt + 1) * 512], oTb[ot][0:64, :], reads=[R_oTb[ot]], writes=[R_OA], q="pool")
                ntile += 1


    B.barrier()
    A.off = stage_off
    R_p4 = B.R("p4")
    segm = A.f32(128)
    ngs = A.f32(4)
    B.dma(segm, segm_d[:, :], writes=[R_p4])
    B.dma(ngs, ng_d[:, :], writes=[R_p4])
    gi = A.f32(512)
    gf = A.f32(512)
    B.dma(gi, GI.rearrange("h (s j) -> (h s) j", j=512), reads=[R_G], writes=[R_p4])
    B.dma(gf, GF.rearrange("h (s j) -> (h s) j", j=512), reads=[R_G], writes=[R_p4])
    cbuf = [gf, A.f32(512)]
    cur = 0
    sh = 1
    while sh < 512:
        B.cp("dve", cbuf[1 - cur][:, 0:sh], cbuf[cur][:, 0:sh], reads=[R_p4], writes=[R_p4])
        B.tt("dve", cbuf[1 - cur][:, sh:512], cbuf[cur][:, sh:512], cbuf[cur][:, 0:512 - sh], ALU.add, reads=[R_p4], writes=[R_p4])
        cur = 1 - cur
        sh *= 2
    Bc = cbuf[cur]
    other = cbuf[1 - cur]
    offs = A.f32(1)
    B.mm(psum[0][:, 0:1], segm, Bc[:, 511:512], True, True, reads=[R_p4], writes=[PS[0]])
    B.cp("dve", offs, psum[0][:, 0:1], reads=[PS[0]], writes=[R_p4])
    B.ts("dve", Bc, Bc, offs[:, 0:1], None, ALU.add, reads=[R_p4], writes=[R_p4])
    gg = other
    B.tt("dve", gg, gi, Bc, ALU.subtract, reads=[R_p4], writes=[R_p4])
    gmax = A.f32(8)
    s.op("dve", lambda e: e.tensor_reduce(gmax, gg.rearrange("p (j t) -> p j t", t=64), AX.X, ALU.max), [R_p4], [R_p4])
    for j in range(1, 8):
        B.tt("dve", gmax[:, j:j + 1], gmax[:, j:j + 1], gmax[:, j - 1:j], ALU.max, reads=[R_p4], writes=[R_p4])
    s.op("pe", lambda e: e.transpose(psum[1][0:1, 0:128], gmax[:, 7:8], identf), [R_p4, R_const], [PS[1]])
    rowa = A.f32(132)
    rowb = A.f32(132)
    B.memset("dve", rowa[0:1, :], 0.0, writes=[R_p4])
    B.memset("dve", rowb[0:1, :], 0.0, writes=[R_p4])
    ra3 = rowa[0:1, :].rearrange("p (h s) -> p h s", s=33)
    rb3 = rowb[0:1, :].rearrange("p (h s) -> p h s", s=33)
    B.cp("dve", ra3[:, :, 1:33], psum[1][0:1, 0:128].rearrange("p (h s) -> p h s", s=32), reads=[PS[1]], writes=[R_p4])
    rc, rn = ra3, rb3
    sh = 1
    while sh < 33:
        B.cp("dve", rn[:, :, 0:sh], rc[:, :, 0:sh], reads=[R_p4], writes=[R_p4])
        B.tt("dve", rn[:, :, sh:33], rc[:, :, sh:33], rc[:, :, 0:33 - sh], ALU.max, reads=[R_p4], writes=[R_p4])
        rc, rn = rn, rc
        sh *= 2
    prow = A.f32(128)
    B.cp("dve", prow[0:1, :].rearrange("p (h s) -> p h s", s=32), rc[:, :, 0:32], reads=[R_p4], writes=[R_p4])
    B.mm(psum[2][:, 0:1], prow[0:1, :], ones_f[0:1, 0:1], True, True, reads=[R_p4, R_const], writes=[PS[2]])
    pm = A.f32(1)
    B.cp("dve", pm, psum[2][:, 0:1], reads=[PS[2]], writes=[R_p4])
    mu = A.f32(8)
    mup = A.f32(8)
    B.ts("dve", mu, gmax, pm[:, 0:1], None, ALU.max, reads=[R_p4], writes=[R_p4])
    B.cp("dve", mup[:, 1:8], mu[:, 0:7], reads=[R_p4], writes=[R_p4])
    B.cp("dve", mup[:, 0:1], pm, reads=[R_p4], writes=[R_p4])
    nmu = A.f32(8)
    B.ts("dve", nmu, mu, -1.0, None, ALU.mult, reads=[R_p4], writes=[R_p4])
    alph = A.f32(8)
    B.tt("dve", alph, mup, mu, ALU.subtract, reads=[R_p4], writes=[R_p4])
    B.act(alph, alph, AF.Exp, reads=[R_p4], writes=[R_p4])
    mo_ = A.f32(1)
    B.tt("dve", mo_, mu[:, 7:8], Bc[:, 511:512], ALU.add, reads=[R_p4], writes=[R_p4])
    B.dma(om[:, :], mo_, reads=[R_p4], final=True)
    for j in range(8):
        B.act(gg[:, j * 64:(j + 1) * 64], gg[:, j * 64:(j + 1) * 64], AF.Exp, bias=nmu[:, j:j + 1], reads=[R_p4], writes=[R_p4])
        B.act(gi[:, j * 64:(j + 1) * 64], Bc[:, j * 64:(j + 1) * 64], AF.Exp, bias=nmu[:, j:j + 1], scale=-1.0, reads=[R_p4], writes=[R_p4])
    aT = A.f32(1024)
    eT = A.f32(1024)
    abc = A.f32(1024)
    aT3 = aT.rearrange("p (j q) -> p j q", q=128)
    eT3 = eT.rearrange("p (j q) -> p j q", q=128)
    abc3 = abc.rearrange("p (j q) -> p j q", q=128)
    R_aT = B.R("aT")
    dg = A.f32(128)
    for (src, dst) in ((gg, aT), (gi, eT)):
        for half in range(2):
            p = 3 + half
            for jj in range(4):
                j = half * 4 + jj
                s.op("pe", lambda e, o_=psum[p][0:64, jj * 128:(jj + 1) * 128], i_=src[:, j * 64:(j + 1) * 64]: e.transpose(o_, i_, identf),
                     [R_p4, R_const], [PS[p]])
            B.cp("dve", dst[0:64, half * 512:(half + 1) * 512], psum[p][0:64, :], reads=[PS[p]], writes=[R_aT])
    for half in range(2):
        p = 5 + half
        for jj in range(4):
            j = half * 4 + jj
            B.ts("dve", dg, identf, alph[:, j:j + 1], None, ALU.mult, reads=[R_p4, R_const], writes=[R_p4])
            B.mm(psum[p][:, jj * 128:(jj + 1) * 128], ones_f, dg, True, True, reads=[R_p4, R_const], writes=[PS[p]])
        B.cp("dve", abc[:, half * 512:(half + 1) * 512], psum[p][:, :], reads=[PS[p]], writes=[R_aT])

    Cst = A.f32(4 * 129)
    Cst3 = Cst.rearrange("p (h d) -> p h d", d=129)
    R_C = [B.R("C%d" % h) for h in range(4)]
    B.memset("dve", Cst, 0.0, writes=R_C)
    kg = [A.bf16(8 * 512) for _ in range(2)]
    vg = [A.bf16(8 * 520) for _ in range(2)]
    R_kg = [B.R("kg0"), B.R("kg1")]
    R_vg = [B.R("vg0"), B.R("vg1")]
    qTg = [A.bf16(4 * 512) for _ in range(2)]
    kTg = [A.bf16(4 * 512) for _ in range(2)]
    mog = [A.bf16(4 * 512) for _ in range(2)]
    R_qTg = [B.R("qTg0"), B.R("qTg1")]
    ka = [A.bf16(128) for _ in range(4)]
    R_ka = [B.R("ka%d" % i) for i in range(4)]
    AT = [A.bf16(64) for _ in range(2)]
    R_AT = [B.R("AT0"), B.R("AT1")]
    Cab = [A.bf16(130) for _ in range(2)]
    R_Cab = [B.R("Cab0"), B.R("Cab1")]
    for i in range(2):
        B.memset("pool", Cab[i], 0.0, writes=[R_Cab[i]])
    cl = A.f32(8)
    R_cl = B.R("cl")
    hx = [A.f32(128) for _ in range(2)]
    R_hx = [B.R("hx0"), B.R("hx1")]
    bst = A.f32(4 * 6)
    bag = A.f32(4 * 2)
    R_bst = B.R("bst")
    rstd = A.f32(4)
    hnb = A.bf16(4 * 128)
    hnb3 = hnb.rearrange("p (h d) -> p h d", d=128)
    R_hnb = B.R("hnb")
    obT = [A.bf16(4 * 512) for _ in range(2)]
    R_obT = [B.R("obT0"), B.R("obT1")]
    R_OB = B.R("OB")
    MKc = MK.rearrange("b (u p) f -> (b u) p f", u=2)
    MVc = MV.rearrange("b (u p) f -> (b u) p f", u=2)
    pi = 0

    def load_group(g):
        gb = g % 2
        B.dma(kg[gb][0:64, :].rearrange("p (c f) -> p c f", f=512), MKc[g * 8:(g + 1) * 8].rearrange("c p f -> p c f"),
              reads=[R_MK], writes=[R_kg[gb]], q="sp")
        B.dma(vg[gb][0:64, :].rearrange("p (c f) -> p c f", f=520), MVc[g * 8:(g + 1) * 8].rearrange("c p f -> p c f"),
              reads=[R_MV], writes=[R_vg[gb]], q="pool")
        if g >= 24:
            go = g - 24
            B.dma(qTg[gb].rearrange("p (h n) -> p h n", n=512), MQT[:, :, go * 512:(go + 1) * 512].rearrange("h p n -> p h n"),
                  reads=[R_MQT], writes=[R_qTg[gb]], q="sp")
            B.dma(kTg[gb].rearrange("p (h n) -> p h n", n=512), MKT[:, :, go * 512:(go + 1) * 512].rearrange("h p n -> p h n"),
                  reads=[R_MKT], writes=[R_qTg[gb]], q="sp")
            for u in range(2):
                B.dma(mog[gb].rearrange("p (h n) -> p h n", n=512)[:, :, u * 256:(u + 1) * 256],
                      SG[2 * go + u].rearrange("p (c n) -> p c n", n=TN)[:, 0:4, :], reads=[R_SG], writes=[R_qTg[gb]], q="pool")

    load_group(0)
    for g in range(32):
        gb = g % 2
        if g + 1 < 32:
            load_group(g + 1)
        own = g >= 24
        k4 = kg[gb].rearrange("p (c h d) -> p c h d", h=4, d=128)
        v4 = vg[gb].rearrange("p (c h d) -> p c h d", h=4, d=130)
        q3 = qTg[gb].rearrange("p (h n) -> p h n", n=512)
        kT3 = kTg[gb].rearrange("p (h n) -> p h n", n=512)
        mo3 = mog[gb].rearrange("p (h n) -> p h n", n=512)
        ob3 = obT[gb].rearrange("p (h n) -> p h n", n=512)
        seg = g
        for j in range(8):
            tk = slice(j * 64, (j + 1) * 64)
            for h in range(4):
                col = h * 32 + seg
                a_s = aT3[0:64, j, col:col + 1]
                al = abc3[:, j, col:col + 1]
                ki = (j * 4 + h) % 4
                B.act(ka[ki][0:64, :], k4[0:64, j, h, :], AF.Copy, scale=a_s, reads=[R_kg[gb], R_aT], writes=[R_ka[ki]])
                pu = pi % 3
                pi += 1
                B.mm(psum[pu][:, 0:130], ka[ki][0:64, :], v4[0:64, j, h, :], True, True, reads=[R_ka[ki], R_vg[gb]], writes=[PS[pu]])
                if own:
                    ai = (j * 4 + h) % 2
                    B.mm(psum[3][0:64, 0:64], kT3[:, h, tk], q3[:, h, tk], True, True, reads=[R_qTg[gb]], writes=[PS[3]])
                    B.stt("dve", AT[ai][0:64, :], psum[3][0:64, 0:64], a_s, trif[0:64, 0:64], ALU.mult, ALU.mult,
                          reads=[PS[3], R_aT, R_const], writes=[R_AT[ai]])
                    B.ts("dve", Cab[ai][:, 0:129], Cst3[:, h, :], al, None, ALU.mult, reads=[R_C[h], R_aT], writes=[R_Cab[ai]])
                    ph = 4 + ai
                    B.mm(psum[ph][0:64, 0:130], AT[ai][0:64, :], v4[0:64, j, h, :], True, False,
                         reads=[R_AT[ai], R_vg[gb]], writes=[PS[ph]])
                    B.mm(psum[ph][0:64, 0:130], q3[:, h, tk], Cab[ai], False, True, reads=[R_qTg[gb], R_Cab[ai]], writes=[PS[ph]])
                    e_t = eT3[0:64, j, col:col + 1]
                    B.act(cl[0:64, h:h + 1], psum[ph][0:64, 128:129], AF.Abs, reads=[PS[ph]], writes=[R_cl])
                    B.ts("dve", cl[0:64, h:h + 1], cl[0:64, h:h + 1], e_t, None, ALU.max, reads=[R_cl, R_aT], writes=[R_cl])
                    s.op("dve", lambda e, a_=cl[0:64, h:h + 1]: e.reciprocal(a_, a_), [R_cl], [R_cl])
                    B.ts("dve", hx[ai][0:64, :], psum[ph][0:64, 0:128], cl[0:64, h:h + 1], None, ALU.mult,
                         reads=[PS[ph], R_cl], writes=[R_hx[ai]])
                    s.op("dve", lambda e, o_=bst[0:64, h * 6:(h + 1) * 6], i_=hx[ai][0:64, :]: e.bn_stats(o_, i_), [R_hx[ai]], [R_bst])
                    s.op("dve", lambda e, o_=bag[0:64, h * 2:(h + 1) * 2], i_=bst[0:64, h * 6:(h + 1) * 6]: e.bn_aggr(o_, i_), [R_bst], [R_bst])
                    B.ts("dve", rstd[0:64, h:h + 1], bag[0:64, h * 2 + 1:h * 2 + 2], EPS, None, ALU.add, reads=[R_bst], writes=[R_bst])
                    B.act(rstd[0:64, h:h + 1], rstd[0:64, h:h + 1], AF.Sqrt, reads=[R_bst], writes=[R_bst])
                    s.op("dve", lambda e, a_=rstd[0:64, h:h + 1]: e.reciprocal(a_, a_), [R_bst], [R_bst])
                    B.ts("dve", hnb3[0:64, h, :], hx[ai][0:64, :], bag[0:64, h * 2:h * 2 + 1], rstd[0:64, h:h + 1], ALU.subtract, ALU.mult,
                         reads=[R_hx[ai], R_bst], writes=[R_hnb])
                B.stt("dve", Cst3[:, h, :], Cst3[:, h, :], al, psum[pu][:, 0:129], ALU.mult, ALU.add,
                      reads=[R_C[h], R_aT, PS[pu]], writes=[R_C[h]])
            if own:
                pt = 6 + j % 2
                ptb = psum[pt][:, :].bitcast(BF16)
                for h in range(4):
                    s.op("pe", lambda e, o_=ptb[:, h * 64:(h + 1) * 64], i_=hnb3[0:64, h, :]: e.transpose(o_, i_, identb[0:64, 0:64]),
                         [R_hnb, R_const], [PS[pt]])
                for h in range(4):
                    B.stt("dve", ob3[:, h, tk], ptb[:, h * 64:(h + 1) * 64], ngs[:, h:h + 1], mo3[:, h, tk], ALU.mult, ALU.mult,
                          reads=[PS[pt], R_p4, R_qTg[gb]], writes=[R_obT[gb]])
        if own:
            go = g - 24
            B.dma(OB[:, :, go * 512:(go + 1) * 512].rearrange("h p n -> p h n"), ob3, reads=[R_obT[gb]], writes=[R_OB], q="sp")
    B.dma(oC[:, :], Cst, reads=R_C, final=True)


    B.barrier()
    R_s4 = B.R("s4")
    Kc = A.bf16(2064)
    Kst = A.f32(2048)
    Vst = A.f32(16 * 64)
    Vc = A.bf16(17 * 65)
    Vc3 = Vc.rearrange("p (b d) -> p b d", d=65)
    Qa = A.bf16(16)
    R_Kc = B.R("Kc"); R_Kst = B.R("Kst"); R_Vst = B.R("Vst"); R_Vc = B.R("Vc"); R_Qa = B.R("Qa")
    B.memset("pool", Kc[64:65, :], 1.0, writes=[R_Kc])
    B.memset("pool", Vc, 1.0, writes=[R_Vc])
    LFs = A.f32(17 * 8)
    LFs3 = LFs.rearrange("p (b h) -> p b h", h=8)
    Fs = A.f32(17 * 8)
    Fs3 = Fs.rearrange("p (b h) -> p b h", h=8)
    Ts = [A.f32(17 * 8) for _ in range(2)]
    bKs = A.f32(8 * 17)
    bKs3 = bKs.rearrange("p (h b) -> p h b", b=17)
    fqs = A.bf16(16)
    Pbs = [A.bf16(16) for _ in range(2)]
    R_Pbs = [B.R("Pbs0"), B.R("Pbs1")]
    dts = A.f32(16)
    onfs = A.f32(16)
    rdens = A.f32(16)
    oTs = A.bf16(16)
    R_OAs = B.R("OAs"); R_OBs = B.R("OBs"); R_QFs = B.R("QFs")
    for e in range(2):
        B.memset("dve", LFs, 0.0, writes=[R_s4])
        B.dma(LFs[:, 0:128], lfc_d[e], writes=[R_s4])
        B.cp("dve", LFs3[0:16, 16, :], lfn[0:16, e * 8:(e + 1) * 8], reads=[R_smp, R_s4], writes=[R_s4])
        B.mm(psum[0][:, 0:136], trif, LFs, True, True, reads=[R_const, R_s4], writes=[PS[0]])
        B.mm(psum[1][:, 0:136], ones_f, LFs, True, True, reads=[R_const, R_s4], writes=[PS[1]])
        B.cp("dve", Fs, psum[0][:, 0:136], reads=[PS[0]], writes=[R_s4])
        B.cp("dve", Ts[0], psum[1][:, 0:136], reads=[PS[1]], writes=[R_s4])
        B.tt("dve", Fs, Fs, Ts[0], ALU.subtract, reads=[R_s4], writes=[R_s4])
        cur = 0
        sh = 1
        while sh < 17:
            a3 = Ts[cur].rearrange("p (b h) -> p b h", h=8)
            n3 = Ts[1 - cur].rearrange("p (b h) -> p b h", h=8)
            B.cp("dve", n3[:, 0:sh, :], a3[:, 0:sh, :], reads=[R_s4], writes=[R_s4])
            B.tt("dve", n3[:, sh:17, :], a3[:, sh:17, :], a3[:, 0:17 - sh, :], ALU.add, reads=[R_s4], writes=[R_s4])
            cur = 1 - cur
            sh *= 2
        Ti3 = Ts[cur].rearrange("p (b h) -> p b h", h=8)
        B.tt("dve", Fs, Fs, Ts[cur], ALU.add, reads=[R_s4], writes=[R_s4])
        for h in range(8):
            B.ts("dve", Fs3[:, :, h], Fs3[:, :, h], Ti3[:, 15, h:h + 1], None, ALU.subtract, reads=[R_s4], writes=[R_s4])
            B.ts("dve", bKs3[:, h, :], Fs3[:, :, h], -1.0, None, ALU.mult, reads=[R_s4], writes=[R_s4])
        s.op("pe", lambda e_, o_=psum[2][0:8, 0:16], i_=Fs3[0:16, 16, :]: e_.transpose(o_, i_, identf[0:16, 0:16]), [R_s4, R_const], [PS[2]])
        B.cp("act", fqs[0:8, :], psum[2][0:8, 0:16], reads=[PS[2]], writes=[R_s4])
        B.dma(QFs[e], fqs[0:8, :], reads=[R_s4], writes=[R_QFs])
        for h in range(8):
            B.dma(Kst[0:64, :], kcT_d[e, h], writes=[R_Kst], q="sp")
            B.dma(Vst.rearrange("p (b d) -> p b d", d=64), vc_d[e, h], writes=[R_Vst], q="pool")
            B.cp("dve", Kc[0:64, 0:2048], Kst[0:64, :], reads=[R_Kst], writes=[R_Kc])
            B.dma(Kc[0:64, 2048:2064], KTs[h][:, e * 16:(e + 1) * 16], reads=[R_KTs], writes=[R_Kc], q="sp")
            B.cp("act", Vc3[:, 0:16, 0:64], Vst.rearrange("p (b d) -> p b d", d=64), reads=[R_Vst], writes=[R_Vc])
            B.dma(Vc3[0:16, 16, :], VAs[e][:, h * 65:(h + 1) * 65], reads=[R_VAs], writes=[R_Vc], q="pool")
            B.dma(Qa[0:64, :], QTs[h][:, e * 16:(e + 1) * 16], reads=[R_QTs], writes=[R_Qa], q="sp")
            B.dma(Qa[64:65, :], QFs[e][h:h + 1, :], reads=[R_QFs], writes=[R_Qa], q="sp")
            for kb in range(17):
                sbk = kb % 4
                pb_ = kb % 2
                if kb < 16:
                    B.mm(psum[sbk][:, 0:16], Kc[0:65, kb * 128:(kb + 1) * 128], Qa[0:65, :], True, True, reads=[R_Kc, R_Qa], writes=[PS[sbk]])
                    B.act(Pbs[pb_][:, :], psum[sbk][:, 0:16], AF.Exp, bias=bKs3[:, h, kb:kb + 1], reads=[PS[sbk], R_s4], writes=[R_Pbs[pb_]])
                    B.mm(psum[4][0:65, 0:16], Vc3[:, kb, :], Pbs[pb_][:, :], kb == 0, False, reads=[R_Vc, R_Pbs[pb_]], writes=[PS[4]])
                else:
                    B.mm(psum[sbk][0:16, 0:16], Kc[0:65, 2048:2064], Qa[0:65, :], True, True, reads=[R_Kc, R_Qa], writes=[PS[sbk]])
                    B.tt("dve", dts[0:16, :], psum[sbk][0:16, 0:16], cmask[0:16, 0:16], ALU.add, reads=[PS[sbk], R_const], writes=[R_s4])
                    B.act(Pbs[pb_][0:16, :], dts[0:16, :], AF.Exp, bias=bKs3[0:16, h, 16:17], reads=[R_s4], writes=[R_Pbs[pb_]])
                    B.mm(psum[4][0:65, 0:16], Vc3[0:16, 16, :], Pbs[pb_][0:16, :], False, True, reads=[R_Vc, R_Pbs[pb_]], writes=[PS[4]])
            B.cp("act", onfs[0:64, :], psum[4][0:64, 0:16], reads=[PS[4]], writes=[R_s4])
            s.op("dve", lambda e_, o_=rdens[64:65, :], i_=psum[4][64:65, 0:16]: e_.reciprocal(o_, i_), [PS[4]], [R_s4])
            B.mm(psum[5][0:64, 0:16], ones_f[64:65, 0:64], rdens[64:65, :], True, True, reads=[R_const, R_s4], writes=[PS[5]])
            B.tt("dve", oTs[0:64, :], onfs[0:64, :], psum[5][0:64, 0:16], ALU.mult, reads=[R_s4, PS[5]], writes=[R_s4])
            B.dma(OAs[h][:, e * 16:(e + 1) * 16], oTs[0:64, :], reads=[R_s4], writes=[R_OAs], q="pool")
    Cs = A.f32(4 * 129)
    Cs3 = Cs.rearrange("p (h d) -> p h d", d=129)
    m0t = A.f32(1)
    bcs = [A.f32(16), A.f32(16)]
    ggs = A.f32(16)
    ees = A.f32(16)
    gmx = A.f32(1)
    mus = A.f32(1)
    nmus = A.f32(1)
    als = A.f32(1)
    mos = A.f32(1)
    aTs = A.f32(4)
    eTs = A.f32(4)
    abcs = A.f32(4)
    dgs = A.f32(4)
    kts = A.bf16(512)
    kts3 = kts.rearrange("p (h d) -> p h d", d=128)
    vts = A.bf16(520)
    vts3 = vts.rearrange("p (h d) -> p h d", d=130)
    qTs_ = A.bf16(4 * 16)
    kTs_ = A.bf16(4 * 16)
    mos_ = A.bf16(4 * 16)
    qTs3 = qTs_.rearrange("p (h n) -> p h n", n=16)
    kTs3 = kTs_.rearrange("p (h n) -> p h n", n=16)
    mos3 = mos_.rearrange("p (h n) -> p h n", n=16)
    obs = A.bf16(4 * 16)
    obs3 = obs.rearrange("p (h n) -> p h n", n=16)
    R_m4 = B.R("m4s")
    for e in range(2):
        ec = slice(e * 16, (e + 1) * 16)
        B.dma(Cs, Cs0_d[e], writes=[R_m4])
        B.dma(m0t[0:4, :], m0_d[e], writes=[R_m4])
        B.dma(kts[0:16, :], MKs[e], reads=[R_MKs], writes=[R_m4])
        B.dma(vts[0:16, :], MVs[e], reads=[R_MVs], writes=[R_m4])
        B.dma(qTs3, MQTs[:, :, ec].rearrange("h p n -> p h n"), reads=[R_MQTs], writes=[R_m4])
        B.dma(kTs3, MKTs[:, :, ec].rearrange("h p n -> p h n"), reads=[R_MQTs], writes=[R_m4])
        B.dma(mos3, SGs.rearrange("p (c n) -> p c n", n=32)[:, 0:4, ec], reads=[R_SGs], writes=[R_m4])
        B.cp("dve", bcs[0][0:4, :], gfs[0:4, ec], reads=[R_smp], writes=[R_m4])
        cur = 0
        sh = 1
        while sh < 16:
            B.cp("dve", bcs[1 - cur][0:4, 0:sh], bcs[cur][0:4, 0:sh], reads=[R_m4], writes=[R_m4])
            B.tt("dve", bcs[1 - cur][0:4, sh:16], bcs[cur][0:4, sh:16], bcs[cur][0:4, 0:16 - sh], ALU.add, reads=[R_m4], writes=[R_m4])
            cur = 1 - cur
            sh *= 2
        bb = bcs[cur]
        B.tt("dve", ggs[0:4, :], gis[0:4, ec], bb[0:4, :], ALU.subtract, reads=[R_smp, R_m4], writes=[R_m4])
        s.op("dve", lambda e_: e_.tensor_reduce(gmx[0:4, :], ggs[0:4, :], AX.X, ALU.max), [R_m4], [R_m4])
        B.tt("dve", mus[0:4, :], gmx[0:4, :], m0t[0:4, :], ALU.max, reads=[R_m4], writes=[R_m4])
        B.ts("dve", nmus[0:4, :], mus[0:4, :], -1.0, None, ALU.mult, reads=[R_m4], writes=[R_m4])
        B.tt("dve", als[0:4, :], m0t[0:4, :], mus[0:4, :], ALU.subtract, reads=[R_m4], writes=[R_m4])
        B.act(als[0:4, :], als[0:4, :], AF.Exp, reads=[R_m4], writes=[R_m4])
        B.tt("dve", mos[0:4, :], mus[0:4, :], bb[0:4, 15:16], ALU.add, reads=[R_m4], writes=[R_m4])
        B.dma(oms[e], mos[0:4, :], reads=[R_m4], final=True)
        B.act(ggs[0:4, :], ggs[0:4, :], AF.Exp, bias=nmus[0:4, 0:1], reads=[R_m4], writes=[R_m4])
        B.act(ees[0:4, :], bb[0:4, :], AF.Exp, bias=nmus[0:4, 0:1], scale=-1.0, reads=[R_m4], writes=[R_m4])
        s.op("pe", lambda e_: e_.transpose(psum[0][0:16, 0:4], ggs[0:4, :], identf[0:4, 0:4]), [R_m4, R_const], [PS[0]])
        s.op("pe", lambda e_: e_.transpose(psum[1][0:16, 0:4], ees[0:4, :], identf[0:4, 0:4]), [R_m4, R_const], [PS[1]])
        B.cp("dve", aTs[0:16, :], psum[0][0:16, 0:4], reads=[PS[0]], writes=[R_m4])
        B.cp("dve", eTs[0:16, :], psum[1][0:16, 0:4], reads=[PS[1]], writes=[R_m4])
        B.ts("dve", dgs[0:4, :], identf[0:4, 0:4], als[0:4, 0:1], None, ALU.mult, reads=[R_m4, R_const], writes=[R_m4])
        B.mm(psum[2][:, 0:4], ones_f[0:4, :], dgs[0:4, :], True, True, reads=[R_m4, R_const], writes=[PS[2]])
        B.cp("dve", abcs, psum[2][:, 0:4], reads=[PS[2]], writes=[R_m4])
        for h in range(4):
            a_s = aTs[0:16, h:h + 1]
            al = abcs[:, h:h + 1]
            B.act(ka[0][0:16, :], kts3[0:16, h, :], AF.Copy, scale=a_s, reads=[R_m4], writes=[R_ka[0]])
            B.mm(psum[3][:, 0:130], ka[0][0:16, :], vts3[0:16, h, :], True, True, reads=[R_ka[0], R_m4], writes=[PS[3]])
            B.mm(psum[4][0:16, 0:16], kTs3[:, h, :], qTs3[:, h, :], True, True, reads=[R_m4], writes=[PS[4]])
            B.stt("dve", AT[0][0:16, 0:16], psum[4][0:16, 0:16], a_s, trif[0:16, 0:16], ALU.mult, ALU.mult,
                  reads=[PS[4], R_m4, R_const], writes=[R_AT[0]])
            B.ts("dve", Cab[0][:, 0:129], Cs3[:, h, :], al, None, ALU.mult, reads=[R_m4], writes=[R_Cab[0]])
            B.mm(psum[5][0:16, 0:130], AT[0][0:16, 0:16], vts3[0:16, h, :], True, False, reads=[R_AT[0], R_m4], writes=[PS[5]])
            B.mm(psum[5][0:16, 0:130], qTs3[:, h, :], Cab[0], False, True, reads=[R_m4, R_Cab[0]], writes=[PS[5]])
            B.act(cl[0:16, h:h + 1], psum[5][0:16, 128:129], AF.Abs, reads=[PS[5]], writes=[R_cl])
            B.ts("dve", cl[0:16, h:h + 1], cl[0:16, h:h + 1], eTs[0:16, h:h + 1], None, ALU.max, reads=[R_cl, R_m4], writes=[R_cl])
            s.op("dve", lambda e_, a_=cl[0:16, h:h + 1]: e_.reciprocal(a_, a_), [R_cl], [R_cl])
            B.ts("dve", hx[0][0:16, :], psum[5][0:16, 0:128], cl[0:16, h:h + 1], None, ALU.mult, reads=[PS[5], R_cl], writes=[R_hx[0]])
            s.op("dve", lambda e_, o_=bst[0:16, h * 6:(h + 1) * 6], i_=hx[0][0:16, :]: e_.bn_stats(o_, i_), [R_hx[0]], [R_bst])
            s.op("dve", lambda e_, o_=bag[0:16, h * 2:(h + 1) * 2], i_=bst[0:16, h * 6:(h + 1) * 6]: e_.bn_aggr(o_, i_), [R_bst], [R_bst])
            B.ts("dve", rstd[0:16, h:h + 1], bag[0:16, h * 2 + 1:h * 2 + 2], EPS, None, ALU.add, reads=[R_bst], writes=[R_bst])
            B.act(rstd[0:16, h:h + 1], rstd[0:16, h:h + 1], AF.Sqrt, reads=[R_bst], writes=[R_bst])
            s.op("dve", lambda e_, a_=rstd[0:16, h:h + 1]: e_.reciprocal(a_, a_), [R_bst], [R_bst])
            B.ts("dve", hnb3[0:16, h, :], hx[0][0:16, :], bag[0:16, h * 2:h * 2 + 1], rstd[0:16, h:h + 1], ALU.subtract, ALU.mult,
                 reads=[R_hx[0], R_bst], writes=[R_hnb])
            B.stt("dve", Cs3[:, h, :], Cs3[:, h, :], al, psum[3][:, 0:129], ALU.mult, ALU.add, reads=[R_m4, PS[3]], writes=[R_m4])
        ptb = psum[6][:, :].bitcast(BF16)
        for h in range(4):
            s.op("pe", lambda e_, o_=ptb[:, h * 16:(h + 1) * 16], i_=hnb3[0:16, h, :]: e_.transpose(o_, i_, identb[0:16, 0:16]),
                 [R_hnb, R_const], [PS[6]])
        for h in range(4):
            B.stt("dve", obs3[:, h, :], ptb[:, h * 16:(h + 1) * 16], ngs[:, h:h + 1], mos3[:, h, :], ALU.mult, ALU.mult,
                  reads=[PS[6], R_p4, R_m4], writes=[R_m4])
        B.dma(OBs[:, :, ec].rearrange("h p n -> p h n"), obs3, reads=[R_m4], writes=[R_OBs], q="sp")
        B.dma(oCs[e], Cs, reads=[R_m4], final=True)

    B.barrier()
    A.off = stage_off
    wab = A.bf16(4 * D)
    wbb = A.bf16(4 * D)
    wob = A.bf16(8 * D)
    wab3 = wab.rearrange("p (k n) -> p k n", n=D)
    wbb3 = wbb.rearrange("p (k n) -> p k n", n=D)
    wob3 = wob.rearrange("p (k n) -> p k n", n=D)
    R_w5 = B.R("w5")
    B.dma(wab3, w_ba, writes=[R_w5], q="pool")
    B.dma(wbb3, w_bb, writes=[R_w5], q="pool")
    for k0 in range(0, 8, 2):
        B.dma(wob3[:, k0:k0 + 2, :], w_o[:, k0:k0 + 2, :], writes=[R_w5], q="pool")
    x5 = [A.f32(8 * TN) for _ in range(2)]
    R_x5 = [B.R("x5_0"), B.R("x5_1")]
    oa5 = [A.bf16(4 * TN) for _ in range(2)]
    ob5 = [A.bf16(4 * TN) for _ in range(2)]
    sg5 = [A.bf16(20 * TN) for _ in range(2)]
    R_in5 = [B.R("in5_0"), B.R("in5_1")]
    m5 = A.bf16(8 * TN)
    m53 = m5.rearrange("p (k n) -> p k n", n=TN)
    R_m5 = B.R("m5")
    t1 = [A.f32(TN) for _ in range(2)]
    R_t1 = [B.R("t1_0"), B.R("t1_1")]
    sq5 = A.bf16(8 * TN)
    rb5 = A.bf16(8 * TN)
    R_sq5 = B.R("sq5")
    st1 = A.f32(TN)
    st2 = A.f32(TN)
    st3 = A.f32(TN)
    R_st5 = B.R("st5")
    R_X2 = B.R("X2")
    tiles5 = [(to, TN, 0) for to in range(16)] + [(16, 32, [(0, 16, 1), (16, 32, 2)])]

    def load5(ti):
        to, N, cond = tiles5[ti]
        b = ti % 2
        oa3 = oa5[b].rearrange("p (c n) -> p c n", n=TN)
        if to == 16:
            B.dma(x5[b].rearrange("p (k n) -> p k n", n=TN)[:, :, 0:32], X1[NT][:, :, 0:32], reads=[R_X1], writes=[R_x5[b]], q="sp")
            for hh in range(2):
                B.dma(oa3[hh * 64:(hh + 1) * 64, :, 0:32], OAs[hh::2].rearrange("c d n -> d c n"), reads=[R_OAs], writes=[R_in5[b]], q="pool")
            B.dma(ob5[b].rearrange("p (c n) -> p c n", n=TN)[:, :, 0:32], OBs.rearrange("h p n -> p h n"), reads=[R_OBs], writes=[R_in5[b]], q="pool")
            B.dma(sg5[b].rearrange("p (c n) -> p c n", n=TN)[:, :, 0:32], SGs.rearrange("p (c n) -> p c n", n=32), reads=[R_SGs], writes=[R_in5[b]], q="sp")
            return
        B.dma(x5[b].rearrange("p (k n) -> p k n", n=TN), X1[OWN0 + to], reads=[R_X1], writes=[R_x5[b]], q="sp")
        for hh in range(2):
            B.dma(oa3[hh * 64:(hh + 1) * 64, :, :], OA[hh::2][:, :, to * TN:(to + 1) * TN].rearrange("c d n -> d c n"),
                  reads=[R_OA], writes=[R_in5[b]], q="pool")
        B.dma(ob5[b].rearrange("p (c n) -> p c n", n=TN), OB[:, :, to * TN:(to + 1) * TN].rearrange("h p n -> p h n"),
              reads=[R_OB], writes=[R_in5[b]], q="pool")
        B.dma(sg5[b], SG[to], reads=[R_SG], writes=[R_in5[b]], q="sp")

    load5(0)
    pi = 0
    for ti, (to, N, cond) in enumerate(tiles5):
        b = ti % 2
        if ti + 1 < len(tiles5):
            load5(ti + 1)
        x3 = x5[b].rearrange("p (k n) -> p k n", n=TN)
        oa3 = oa5[b].rearrange("p (c n) -> p c n", n=TN)
        ob3 = ob5[b].rearrange("p (c n) -> p c n", n=TN)
        sg3 = sg5[b].rearrange("p (c n) -> p c n", n=TN)
        q3 = sq5.rearrange("p (k n) -> p k n", n=TN)
        rb3 = rb5.rearrange("p (k n) -> p k n", n=TN)
        groups = [(0, N, cond)] if isinstance(cond, int) else cond
        B.act(x3[:, :, 0:N], x3[:, :, 0:N], AF.Copy, scale=ALPHA, reads=[R_x5[b]], writes=[R_x5[b]])
        for d in range(8):
            pa = pi % 8
            pb = (pi + 1) % 8
            pi += 2
            for c in range(4):
                B.mm(psum[pa][:, 0:N], wab3[:, c, d * 128:(d + 1) * 128], oa3[:, c, 0:N], c == 0, c == 3,
                     reads=[R_w5, R_in5[b]], writes=[PS[pa]])
            for c in range(4):
                B.mm(psum[pb][:, 0:N], wbb3[:, c, d * 128:(d + 1) * 128], ob3[:, c, 0:N], c == 0, c == 3,
                     reads=[R_w5, R_in5[b]], writes=[PS[pb]])
            ti_ = d % 2
            B.tt("dve", t1[ti_][:, 0:N], psum[pa][:, 0:N], sg3[:, 4 + d, 0:N], ALU.mult, reads=[PS[pa], R_in5[b]], writes=[R_t1[ti_]])
            B.tt("dve", t1[1 - ti_][:, 0:N], psum[pb][:, 0:N], sg3[:, 12 + d, 0:N], ALU.mult, reads=[PS[pb], R_in5[b]], writes=[R_t1[1 - ti_]])
            B.tt("dve", m53[:, d, 0:N], t1[0][:, 0:N], t1[1][:, 0:N], ALU.add, reads=[R_t1[0], R_t1[1]], writes=[R_m5])
        for d in range(8):
            pd = pi % 8
            pi += 1
            for c in range(8):
                B.mm(psum[pd][:, 0:N], wob3[:, c, d * 128:(d + 1) * 128], m53[:, c, 0:N], c == 0, c == 7,
                     reads=[R_w5, R_m5], writes=[PS[pd]])
            for (c0, c1, ci) in groups:
                B.stt("dve", x3[:, d, c0:c1], psum[pd][:, c0:c1], modv3[:, 40 + d, ci:ci + 1], x3[:, d, c0:c1],
                      ALU.mult, ALU.add, reads=[PS[pd], R_mod], writes=[R_x5[b]])
            B.act(q3[:, d, 0:N], x3[:, d, 0:N], AF.Square, reads=[R_x5[b]], writes=[R_sq5])
            B.act(rb3[:, d, 0:N], x3[:, d, 0:N], AF.Copy, reads=[R_x5[b]], writes=[R_sq5])
        p1 = pi % 8
        p2 = (pi + 1) % 8
        pi += 2
        for d in range(8):
            B.mm(psum[p1][:, 0:N], ones_b, rb3[:, d, 0:N], d == 0, d == 7, reads=[R_sq5, R_const], writes=[PS[p1]])
        for d in range(8):
            B.mm(psum[p2][:, 0:N], ones_b, q3[:, d, 0:N], d == 0, d == 7, reads=[R_sq5, R_const], writes=[PS[p2]])
        B.ts("dve", st1[:, 0:N], psum[p1][:, 0:N], -1.0 / D, None, ALU.mult, reads=[PS[p1]], writes=[R_st5])
        B.tt("dve", st2[:, 0:N], st1[:, 0:N], st1[:, 0:N], ALU.mult, reads=[R_st5], writes=[R_st5])
        B.stt("dve", st3[:, 0:N], psum[p2][:, 0:N], 1.0 / D, st2[:, 0:N], ALU.mult, ALU.subtract, reads=[PS[p2], R_st5], writes=[R_st5])
        B.ts("dve", st3[:, 0:N], st3[:, 0:N], EPS, None, ALU.add, reads=[R_st5], writes=[R_st5])
        B.act(st3[:, 0:N], st3[:, 0:N], AF.Sqrt, reads=[R_st5], writes=[R_st5])
        s.op("dve", lambda e, a=st3[:, 0:N]: e.reciprocal(a, a), [R_st5], [R_st5])
        R_xd = [B.R("x5d%d_%d" % (b, d)) for d in range(8)]
        for d in range(8):
            B.tt("dve", x3[:, d, 0:N], x3[:, d, 0:N], st1[:, 0:N], ALU.add, reads=[R_x5[b], R_st5], writes=[R_xd[d]])
        for d in range(8):
            B.tt("dve", x3[:, d, 0:N], x3[:, d, 0:N], st3[:, 0:N], ALU.mult, reads=[R_xd[d], R_st5], writes=[R_xd[d]])
        for d in range(8):
            B.ts("dve", x3[:, d, 0:N], x3[:, d, 0:N], lng3[:, 1, d:d + 1], lnb3[:, 1, d:d + 1], ALU.mult, ALU.add,
                 reads=[R_xd[d], R_const], writes=[R_xd[d], R_x5[b]])
        B.dma(X2[to][:, :, 0:N], x3[:, :, 0:N], reads=[R_x5[b]], writes=[R_X2], q="pool")

    def load2(tid, dst, res):
        Nn = TN if tid < 16 else 32
        B.dma(dst, X2[tid][:, :, 0:Nn], reads=[R_X2], writes=[res])

    def store2(tid, src, res):
        if tid < 16:
            B.dma(yT[:, :, tid * TN:(tid + 1) * TN], src, reads=[res], final=True, q="pool")
        else:
            B.dma(ysT[:, :, :], src, reads=[res], final=True, q="pool")

    tiles2 = [(t, TN, 0) for t in range(16)] + [(16, 32, [(0, 16, 1), (16, 32, 2)])]
    ffn_stage(1, w_gu[1], w_dn[1], tiles2, 48, 56, 64, 2, load2, store2, "f2")

    s.emit(B.out_dmas)
    es.close()
    return nc


def _fm(a):
    F, N = a.shape
    return np.ascontiguousarray(a.reshape(F // 128, 128, N).transpose(1, 0, 2))


def kernel(**inp):
    f32 = np.float32
    g = {k: np.asarray(v) for k, v in inp.items()}
    nc = build()
    in_maps = []
    w_ada = _fm(g["w_ada"][0])
    b_ada = np.ascontiguousarray(g["b_ada"][0].reshape(72, 128).T)
    wgu = [_fm(g["ffn1_w_gu"][0]), _fm(g["ffn2_w_gu"][0])]
    wdn = [_fm(g["ffn1_w_down"][0]), _fm(g["ffn2_w_down"][0])]
    w_in = _fm(g["w_in"][0])
    b_in = g["b_in"][0]
    fmc = ([512 + 128 * c for c in range(4)] + [2056 + 128 * c for c in range(4)] + [128 * c for c in range(4)] +
           [1544 + 128 * c for c in range(4)] + [3088 + 128 * c for c in range(4)] + [3600 + 128 * c for c in range(8)] +
           [4624 + 128 * c for c in range(8)])
    b_in_fm = np.zeros((128, 40), f32)
    for j, c0 in enumerate(fmc):
        b_in_fm[:, j] = b_in[c0:c0 + 128]
    b_in_fm[0:4, 36] = b_in[3080:3084]
    b_in_fm[0:4, 37] = b_in[3084:3088]
    tm = np.concatenate([b_in[1024:1544], b_in[2568:3080]])
    b_in_bc = np.ascontiguousarray(np.broadcast_to(tm[None, :], (128, 1032)))
    ident = np.eye(128, dtype=f32)
    w_ba = _fm(g["w_branch_a"][0])
    w_bb = _fm(g["w_branch_b"][0])
    w_o = _fm(g["w_out"][0])
    tri = np.triu(np.ones((128, 128), f32))
    pp = np.arange(128)
    segm = ((pp[:, None] // 32 == pp[None, :] // 32) & (pp[:, None] < pp[None, :])).astype(f32)
    ng = np.ascontiguousarray(g["mlstm_norm_g"][0].reshape(4, 128).T)
    cmask = np.where(np.arange(128)[:, None] <= np.arange(128)[None, :], 0.0, -BIG).astype(f32)
    ln_g = np.ascontiguousarray(g["ln_g"][0].reshape(3, 8, 128).transpose(2, 0, 1))
    ln_b = np.ascontiguousarray(g["ln_b"][0].reshape(3, 8, 128).transpose(2, 0, 1))
    conv_w = np.ascontiguousarray(g["conv_w"][0].reshape(4, 8, 128).transpose(2, 1, 0))
    conv_b = np.ascontiguousarray(g["conv_b"][0].reshape(8, 128).T)
    ck = g["cache_fox_k"][0]; cvv = g["cache_fox_v"][0]; clf = g["cache_fox_logf"][0]
    sC = g["state_mlstm_C"][0]; sn = g["state_mlstm_n"][0]; sm = g["state_mlstm_m"][0]; scv = g["state_conv"][0]
    for core in range(8):
        b, q = core // 4, core % 4
        es_ = [2 * core, 2 * core + 1]
        convs = np.ascontiguousarray(np.stack([scv[e].reshape(3, 8, 128) for e in es_], 0).transpose(3, 2, 0, 1))
        kcT = np.ascontiguousarray(np.stack([ck[e].transpose(1, 2, 0) for e in es_], 0))
        vc = np.ascontiguousarray(np.stack([cvv[e].reshape(16, 128, 8, 64).transpose(2, 1, 0, 3) for e in es_], 0))
        lfc = np.ascontiguousarray(np.stack([clf[e].reshape(16, 128, 8).transpose(1, 0, 2).reshape(128, 128) for e in es_], 0))
        Cs0 = np.zeros((2, 128, 4, 129), f32)
        for i_, e in enumerate(es_):
            Cs0[i_, :, :, 0:128] = sC[e].transpose(2, 0, 1)
            Cs0[i_, :, :, 128] = sn[e].T
        Cs0 = Cs0.reshape(2, 128, 516)
        m0c = np.ascontiguousarray(np.stack([sm[e].reshape(4, 1) for e in es_], 0))
        nreal = 4096 * (q + 1)
        xt = np.zeros((D, S), f32)
        xt[:, S - nreal:] = g["x_prompt"][b, :nreal, :].T
        keep = np.zeros((128, NT), f32)
        keep[:, NT - 16 * (q + 1):] = 1.0
        xs = g["x_sample"][2 * core:2 * core + 2].reshape(32, D).T
        c3 = np.stack([g["c_prompt"][b], g["c_sample"][2 * core], g["c_sample"][2 * core + 1]], axis=1)
        in_maps.append({
            "xT": _fm(xt), "xsT": _fm(np.ascontiguousarray(xs)), "keep": keep, "cT": _fm(np.ascontiguousarray(c3)),
            "w_ada": w_ada, "b_ada": b_ada, "w_gu1": wgu[0], "w_gu2": wgu[1], "w_dn1": wdn[0], "w_dn2": wdn[1],
            "w_in": w_in, "b_in_fm": b_in_fm, "b_in_bc": b_in_bc, "ln_g": ln_g, "ln_b": ln_b,
            "conv_w": conv_w, "conv_b": conv_b, "ident": ident, "tri": tri, "cmask": cmask, "segm": segm, "ng": ng, "w_ba": w_ba, "w_bb": w_bb, "w_o": w_o, "convs": convs, "kcT": kcT, "vc": vc, "lfc": lfc, "Cs0": Cs0, "m0c": m0c,
        })
    res = run_bass_kernel_spmd(nc, in_maps, core_ids=list(range(8)))
    R = res.results
    Bn, DB, L = 2, 16, 16
    y_p = np.zeros((Bn, S, D), f32)
    for core in range(8):
        b, q = core // 4, core % 4
        yt = R[core]["yT"]
        y_p[b, q * 4096:(q + 1) * 4096, :] = yt.transpose(2, 1, 0).reshape(4096, D)
    fk = np.zeros((1, Bn, S, 8, 64), f32)
    fv = np.zeros((1, Bn, S, 8, 64), f32)
    fl = np.zeros((1, Bn, S, 8), f32)
    cv = np.zeros((1, Bn, 3, 1024), f32)
    for core in range(8):
        b, q = core // 4, core % 4
        sl = slice(q * 4096, (q + 1) * 4096)
        fk[0, b, sl] = R[core]["okT"].transpose(2, 1, 0).reshape(4096, 8, 64)
        fv[0, b, sl] = R[core]["ov"].reshape(4096, 8, 64)
        fl[0, b, sl] = R[core]["olf"]
        if q == 3:
            cv[0, b] = R[core]["oconv"].transpose(2, 1, 0).reshape(3, 1024)
    mC = np.zeros((1, Bn, 4, 128, 128), f32)
    mn = np.zeros((1, Bn, 4, 128), f32)
    mm_ = np.zeros((1, Bn, 4), f32)
    for b in range(Bn):
        c = R[4 * b + 3]["oC"].reshape(128, 4, 129)
        mC[0, b] = c[:, :, 0:128].transpose(1, 2, 0)
        mn[0, b] = c[:, :, 128].T
        mm_[0, b] = R[4 * b + 3]["om"][31::32, 0]
    y_s = np.zeros((DB, L, D), f32)
    sk = np.zeros((1, DB, L, 8, 64), f32); sv = np.zeros((1, DB, L, 8, 64), f32); sl_ = np.zeros((1, DB, L, 8), f32)
    sCo = np.zeros((1, DB, 4, 128, 128), f32); sno = np.zeros((1, DB, 4, 128), f32); smo = np.zeros((1, DB, 4), f32)
    sco = np.zeros((1, DB, 3, 1024), f32)
    for core in range(8):
        r = R[core]
        ys = r["ysT"].transpose(2, 1, 0).reshape(32, D)
        ks = r["oksT"].transpose(2, 1, 0).reshape(32, 8, 64)
        for i_ in range(2):
            e = 2 * core + i_
            y_s[e] = ys[i_ * 16:(i_ + 1) * 16]
            sk[0, e] = ks[i_ * 16:(i_ + 1) * 16]
            sv[0, e] = r["ovs"][i_].reshape(16, 8, 64)
            sl_[0, e] = r["olfs"][i_]
            c = r["oCs"][i_].reshape(128, 4, 129)
            sCo[0, e] = c[:, :, 0:128].transpose(1, 2, 0)
            sno[0, e] = c[:, :, 128].T
            smo[0, e] = r["oms"][i_][:, 0]
            sco[0, e] = r["ocs"][:, :, i_, :].transpose(2, 1, 0).reshape(3, 1024)
    outs = [y_p, y_s,
            fk, fv, fl,
            mC, mn, mm_,
            cv,
            sk, sv, sl_, sCo, sno, smo, sco]
    return tuple(outs)
```
